# Optimizing a Trainium2 kernel written in Bass

```python
import math
import jax, jax.numpy as jnp
from jax import lax
import numpy as np

D_MODEL = 1024
BATCH = 8
SEQ = 4096
DEPTH = 1
DEC_BATCH = 128
DEC_SEQ = 1
PAST_LEN = 16384
PAGE_SIZE = 128

ATT_HEADS = 8
ATT_KV_HEADS = 2
ATT_GROUP = ATT_HEADS // ATT_KV_HEADS
HEAD_DIM = 64
WINDOW = 128
ATT_BLOCK = WINDOW
ROPE_DIMS = HEAD_DIM // 4
ROPE_THETA = 500000.0
RET_HEADS = 4
RET_DK = 128
RET_DV = 128
RET_CHUNK = 128
RET_THETA = 10000.0
ATT_WIDTH = ATT_HEADS * HEAD_DIM
KV_WIDTH = ATT_KV_HEADS * HEAD_DIM
RET_QK_WIDTH = RET_HEADS * RET_DK
RET_WIDTH = RET_HEADS * RET_DV
MIX_WIDTH = ATT_WIDTH + RET_WIDTH
IN_SPLITS = (ATT_WIDTH, KV_WIDTH, KV_WIDTH, RET_QK_WIDTH, RET_QK_WIDTH, RET_WIDTH, RET_WIDTH)
IN_WIDTH = sum(IN_SPLITS)
D_FF = -(-8 * D_MODEL // (3 * 256)) * 256
DEEPNORM_ALPHA = (2 * DEPTH) ** 0.25
DEEPNORM_BETA = (8 * DEPTH) ** -0.25
LN_EPS = 1e-5
GN_EPS = 1e-6

kernel_name = 'hymba_swa_sink_retention_deepnorm_adaln_step'


def layer_norm(x, w, b):
    xf = x.astype(jnp.float32)
    mu = jnp.mean(xf, -1, keepdims=True)
    var = jnp.mean(jnp.square(xf - mu), -1, keepdims=True)
    return ((xf - mu) * lax.rsqrt(var + LN_EPS) * w.astype(jnp.float32) + b.astype(jnp.float32)).astype(x.dtype)


def rope(x, pos, n_rot, theta):
    half = n_rot // 2
    inv = theta ** (-jnp.arange(half, dtype=jnp.float32) / half)
    ang = pos.astype(jnp.float32)[:, None] * inv[None, :]
    cos = jnp.cos(ang)[:, None, :]
    sin = jnp.sin(ang)[:, None, :]
    xf = x.astype(jnp.float32)
    x1 = xf[..., :half]
    x2 = xf[..., half:n_rot]
    out = jnp.concatenate([x1 * cos - x2 * sin, x2 * cos + x1 * sin, xf[..., n_rot:]], -1)
    return out.astype(x.dtype)


def ada_mod(c, w_ada, b_ada):
    m = jax.nn.silu(c) @ w_ada + b_ada
    shift, scale, gate = jnp.split(m, 3, axis=-1)
    return shift[:, None, :], scale[:, None, :], gate[:, None, :]


def project_in(h, w_in):
    z = h @ w_in
    return jnp.split(z, np.cumsum(IN_SPLITS)[:-1].tolist(), axis=-1)


def attend(q, k, v, qpos, kpos, sinks):
    s = jnp.einsum('bnqhgd,bnkhd->bnhgqk', q, k).astype(jnp.float32) * (HEAD_DIM ** -0.5)
    dist = qpos[:, :, None] - kpos[:, None, :]
    mask = (dist >= 0) & (dist <= WINDOW) & (kpos[:, None, :] >= 0)
    s = jnp.where(mask[None, :, None, None], s, -jnp.inf)
    sink = sinks.astype(jnp.float32).reshape(ATT_KV_HEADS, ATT_GROUP)[None, None, :, :, None, None]
    m = jnp.maximum(jnp.max(s, -1, keepdims=True), sink)
    p = jnp.exp(s - m)
    w = p / (jnp.sum(p, -1, keepdims=True) + jnp.exp(sink - m))
    o = jnp.einsum('bnhgqk,bnkhd->bnqhgd', w.astype(v.dtype), v)
    B, N, Lq = o.shape[:3]
    return o.reshape(B, N * Lq, ATT_WIDTH)


def retention_log_decay():
    return jnp.log1p(-jnp.exp(jnp.linspace(math.log(1.0 / 32), math.log(1.0 / 512), RET_HEADS))).astype(jnp.float32)


def retention_chunk(S, q, k, v, log_gamma):
    q = q.astype(jnp.float32)
    k = k.astype(jnp.float32)
    v = v.astype(jnp.float32)
    Sf = S.astype(jnp.float32)
    L = q.shape[1]
    idx = jnp.arange(L, dtype=jnp.float32)
    diff = idx[:, None] - idx[None, :]
    decay = jnp.where(diff[None] >= 0, jnp.exp(log_gamma[:, None, None] * jnp.maximum(diff, 0.0)[None]), 0.0)
    scores = jnp.einsum('bihd,bjhd->bhij', q, k) * decay[None]
    o = jnp.einsum('bhij,bjhe->bihe', scores, v)
    q_dec = q * jnp.exp(log_gamma[None, :] * (idx[:, None] + 1.0))[None, :, :, None]
    o = o + jnp.einsum('bihd,bhde->bihe', q_dec, Sf)
    k_dec = k * jnp.exp(log_gamma[None, :] * (L - 1.0 - idx[:, None]))[None, :, :, None]
    S_new = jnp.exp(log_gamma * L)[None, :, None, None] * Sf + jnp.einsum('bjhd,bjhe->bhde', k_dec, v)
    return S_new.astype(S.dtype), o


def retention_output(o, g, gn_w):
    mu = jnp.mean(o, -1, keepdims=True)
    var = jnp.mean(jnp.square(o - mu), -1, keepdims=True)
    n = ((o - mu) * lax.rsqrt(var + GN_EPS)).reshape(o.shape[0], o.shape[1], RET_WIDTH)
    n = n * gn_w.astype(jnp.float32)
    return (jax.nn.silu(g.astype(jnp.float32)) * n).astype(g.dtype)


def split_heads(q, k, v, rq, rk, rv, pos):
    B, L, _ = q.shape
    q = rope(q.reshape(B, L, ATT_HEADS, HEAD_DIM), pos, ROPE_DIMS, ROPE_THETA)
    k = rope(k.reshape(B, L, ATT_KV_HEADS, HEAD_DIM), pos, ROPE_DIMS, ROPE_THETA)
    v = v.reshape(B, L, ATT_KV_HEADS, HEAD_DIM)
    rq = rope(rq.reshape(B, L, RET_HEADS, RET_DK), pos, RET_DK, RET_THETA)
    rk = rope(rk.reshape(B, L, RET_HEADS, RET_DK), pos, RET_DK, RET_THETA) * (RET_DK ** -0.5)
    rv = rv.reshape(B, L, RET_HEADS, RET_DV)
    return q, k, v, rq, rk, rv


def mixer_prompt(h, w_in, sinks, gn_w, w_out):
    B, S, _ = h.shape
    q, k, v, rq, rk, rv, rg = project_in(h, w_in)
    pos = jnp.arange(S, dtype=jnp.int32)
    q, k, v, rq, rk, rv = split_heads(q, k, v, rq, rk, rv, pos)
    nb = S // ATT_BLOCK
    qb = q.reshape(B, nb, ATT_BLOCK, ATT_KV_HEADS, ATT_GROUP, HEAD_DIM)
    kb = k.reshape(B, nb, ATT_BLOCK, ATT_KV_HEADS, HEAD_DIM)
    vb = v.reshape(B, nb, ATT_BLOCK, ATT_KV_HEADS, HEAD_DIM)
    pad = ((0, 0), (1, 0), (0, 0), (0, 0), (0, 0))
    k_band = jnp.concatenate([jnp.pad(kb, pad)[:, :-1], kb], axis=2)
    v_band = jnp.concatenate([jnp.pad(vb, pad)[:, :-1], vb], axis=2)
    qpos = pos.reshape(nb, ATT_BLOCK)
    kpos = (jnp.arange(nb, dtype=jnp.int32)[:, None] - 1) * ATT_BLOCK + jnp.arange(2 * ATT_BLOCK, dtype=jnp.int32)[None, :]
    att = attend(qb, k_band, v_band, qpos, kpos, sinks)
    nc = S // RET_CHUNK
    to_chunks = lambda t: jnp.moveaxis(t.reshape(B, nc, RET_CHUNK, RET_HEADS, t.shape[-1]), 1, 0)
    log_gamma = retention_log_decay()
    S0 = jnp.zeros((B, RET_HEADS, RET_DK, RET_DV), jnp.float32)
    S_fin, o = lax.scan(lambda st, xs: retention_chunk(st, xs[0], xs[1], xs[2], log_gamma), S0,
                        (to_chunks(rq), to_chunks(rk), to_chunks(rv)))
    o = jnp.moveaxis(o, 0, 1).reshape(B, S, RET_HEADS, RET_DV)
    ret = retention_output(o, rg, gn_w)
    y = jnp.concatenate([att, ret], -1) @ w_out
    return y, k[:, S - WINDOW:], v[:, S - WINDOW:], S_fin


def mixer_sample(h, cache_k, cache_v, state, w_in, sinks, gn_w, w_out):
    B, L, _ = h.shape
    q, k, v, rq, rk, rv, rg = project_in(h, w_in)
    pos = PAST_LEN + jnp.arange(L, dtype=jnp.int32)
    q, k, v, rq, rk, rv = split_heads(q, k, v, rq, rk, rv, pos)
    k_all = jnp.concatenate([cache_k.astype(k.dtype), k], axis=1)
    v_all = jnp.concatenate([cache_v.astype(v.dtype), v], axis=1)
    kpos = PAST_LEN - WINDOW + jnp.arange(WINDOW + L, dtype=jnp.int32)
    att = attend(q.reshape(B, 1, L, ATT_KV_HEADS, ATT_GROUP, HEAD_DIM), k_all[:, None], v_all[:, None],
                 pos[None], kpos[None], sinks)
    S_new, o = retention_chunk(state, rq, rk, rv, retention_log_decay())
    ret = retention_output(o, rg, gn_w)
    y = jnp.concatenate([att, ret], -1) @ w_out
    return y, k_all[:, L:], v_all[:, L:], S_new


def swiglu(h, w_up, w_down):
    g, u = jnp.split(h @ w_up, 2, axis=-1)
    return (jax.nn.silu(g) * u) @ w_down


def setup_inputs(seed: int = 0) -> dict:
    key = jax.random.key(seed)
    ks = jax.random.split(key, 24)
    nrm = lambda k, shape, s: jax.random.normal(k, shape, jnp.float32) * s
    D = D_MODEL
    return {
        'x_prompt': nrm(ks[0], (BATCH, SEQ, D), 1.0),
        'x_sample': nrm(ks[1], (DEC_BATCH, DEC_SEQ, D), 1.0),
        'c_prompt': nrm(ks[2], (BATCH, D), 1.0),
        'c_sample': nrm(ks[3], (DEC_BATCH, D), 1.0),
        'cache_k': nrm(ks[4], (DEPTH, DEC_BATCH, WINDOW, ATT_KV_HEADS, HEAD_DIM), 1.0),
        'cache_v': nrm(ks[5], (DEPTH, DEC_BATCH, WINDOW, ATT_KV_HEADS, HEAD_DIM), 1.0),
        'state_ret': nrm(ks[6], (DEPTH, DEC_BATCH, RET_HEADS, RET_DK, RET_DV), 0.5),
        'w_ada_mix': nrm(ks[7], (DEPTH, D, 3 * D), 0.1 * D ** -0.5),
        'b_ada_mix': nrm(ks[8], (DEPTH, 3 * D), 0.01),
        'w_in': nrm(ks[9], (DEPTH, D, IN_WIDTH), D ** -0.5),
        'att_sinks': nrm(ks[10], (DEPTH, ATT_HEADS), 1.0),
        'ret_gn_w': 1.0 + nrm(ks[11], (DEPTH, RET_WIDTH), 0.01),
        'w_out': nrm(ks[12], (DEPTH, MIX_WIDTH, D), DEEPNORM_BETA * MIX_WIDTH ** -0.5),
        'ln1_w': 1.0 + nrm(ks[13], (DEPTH, D), 0.01),
        'ln1_b': nrm(ks[14], (DEPTH, D), 0.01),
        'w_ada_ffn': nrm(ks[15], (DEPTH, D, 3 * D), 0.1 * D ** -0.5),
        'b_ada_ffn': nrm(ks[16], (DEPTH, 3 * D), 0.01),
        'w_up': nrm(ks[17], (DEPTH, D, 2 * D_FF), D ** -0.5),
        'w_down': nrm(ks[18], (DEPTH, D_FF, D), DEEPNORM_BETA * D_FF ** -0.5),
        'ln2_w': 1.0 + nrm(ks[19], (DEPTH, D), 0.01),
        'ln2_b': nrm(ks[20], (DEPTH, D), 0.01),
    }


def reference(x_prompt, x_sample, c_prompt, c_sample, cache_k, cache_v, state_ret,
              w_ada_mix, b_ada_mix, w_in, att_sinks, ret_gn_w, w_out, ln1_w, ln1_b,
              w_ada_ffn, b_ada_ffn, w_up, w_down, ln2_w, ln2_b):
    yp, ys = x_prompt, x_sample
    kp_l, vp_l, sp_l, ks_l, vs_l, ss_l = [], [], [], [], [], []
    for l in range(DEPTH):
        shp, scp, gtp = ada_mod(c_prompt, w_ada_mix[l], b_ada_mix[l])
        shs, scs, gts = ada_mod(c_sample, w_ada_mix[l], b_ada_mix[l])
        mp, kp, vp, sp = mixer_prompt(yp * (1.0 + scp) + shp, w_in[l], att_sinks[l], ret_gn_w[l], w_out[l])
        ms, ks, vs, ss = mixer_sample(ys * (1.0 + scs) + shs, cache_k[l], cache_v[l], state_ret[l],
                                      w_in[l], att_sinks[l], ret_gn_w[l], w_out[l])
        yp = layer_norm(DEEPNORM_ALPHA * yp + (1.0 + gtp) * mp, ln1_w[l], ln1_b[l])
        ys = layer_norm(DEEPNORM_ALPHA * ys + (1.0 + gts) * ms, ln1_w[l], ln1_b[l])
        shp, scp, gtp = ada_mod(c_prompt, w_ada_ffn[l], b_ada_ffn[l])
        shs, scs, gts = ada_mod(c_sample, w_ada_ffn[l], b_ada_ffn[l])
        fp = swiglu(yp * (1.0 + scp) + shp, w_up[l], w_down[l])
        fs = swiglu(ys * (1.0 + scs) + shs, w_up[l], w_down[l])
        yp = layer_norm(DEEPNORM_ALPHA * yp + (1.0 + gtp) * fp, ln2_w[l], ln2_b[l])
        ys = layer_norm(DEEPNORM_ALPHA * ys + (1.0 + gts) * fs, ln2_w[l], ln2_b[l])
        kp_l.append(kp); vp_l.append(vp); sp_l.append(sp)
        ks_l.append(ks); vs_l.append(vs); ss_l.append(ss)
    new_k_prompt = jnp.stack(kp_l, 0)
    new_v_prompt = jnp.stack(vp_l, 0)
    new_ret_prompt = jnp.stack(sp_l, 0)
    new_k_sample = jnp.stack(ks_l, 0)
    new_v_sample = jnp.stack(vs_l, 0)
    new_ret_sample = jnp.stack(ss_l, 0)
    return (yp, ys, new_k_prompt, new_v_prompt, new_ret_prompt, new_k_sample, new_v_sample, new_ret_sample)
```

```python
import contextlib
import math
import numpy as np
import concourse.bass as bass
import concourse.mybir as mybir
from concourse.bass_utils import run_bass_kernel_spmd

F32 = mybir.dt.float32
BF16 = mybir.dt.bfloat16
ALU = mybir.AluOpType
AF = mybir.ActivationFunctionType
AX = mybir.AxisListType

NT = 32
D = 1024
DFF = 2816
ALPHA = 2.0 ** 0.25
LN_EPS = 1e-5
GN_EPS = 1e-6
NB = 16


class _Probe:
    def __init__(self):
        self.calls = []

    def __getattr__(self, name):
        def f(*a, **k):
            self.calls.append((name, a, k))
            return self
        return f


def _free_size(ap):
    n = 1
    for d in ap.shape[1:]:
        n *= int(d)
    return n


def _is_psum(ap):
    try:
        return type(ap.tensor).__name__.startswith('PSum')
    except Exception:
        return False


def _est_ns(eng, fns):
    tot = 0.0
    for f in fns:
        pr = _Probe()
        try:
            f(pr)
        except Exception:
            tot += 300.0
            continue
        for (name, a, k) in pr.calls:
            aps = [v for v in list(a) + list(k.values()) if hasattr(v, 'shape') and hasattr(v, 'tensor')]
            out = k.get('out', aps[0] if aps else None)
            if eng == 'pe':
                mv = k.get('rhs', k.get('in_', out))
                n = _free_size(mv) if name != 'transpose' else _free_size(out)
                tot += 45.0 if n <= 32 else 228.0
            else:
                fd = _free_size(out) if out is not None else 64
                ps = any(_is_psum(v) for v in aps)
                if eng == 'act':
                    tot += 210.0 + 0.8 * fd
                elif eng == 'dve':
                    if ps:
                        tot += 175.0 + 1.0 * fd
                    elif name == 'scalar_tensor_tensor':
                        tot += 230.0 + 1.0 * fd
                    else:
                        tot += 110.0 + 1.0 * fd
                else:
                    tot += 100.0 + fd * 2.35
    return tot


class Prog:
    PSUM_ROOTS = ('pT', 'pTb', 'pO', 'pK', 'pZ', 'pS', 'pG', 'pU', 'pA')
    DMA_BW = 230.0
    DMA_LAT = 2000.0
    STORE_SLACK = 12000.0
    SYNC_LAT = 400.0
    WINDOW = 0.0

    def __init__(self, nc):
        self.nc = nc
        self.names = ['pe', 'act', 'dve', 'pool', 'sp']
        self.ops = []
        self.epoch = 0
        self.prio = 0.0
        self.sched = True

    def op(self, eng, fn, reads=(), writes=(), dur=None):
        fns = fn if isinstance(fn, (list, tuple)) else [fn]
        writes = list(writes) + [k for k in reads if (k if isinstance(k, str) else k[0]) in self.PSUM_ROOTS and k not in writes]
        self.ops.append(dict(id=len(self.ops), eng=eng, fns=list(fns), reads=list(reads), writes=writes, dma=None,
                             epoch=self.epoch, prio=self.prio, dur=dur, final=False))

    def dma(self, eng, pairs, reads=(), writes=(), semkey=None, final=False, slow=False, transpose=False, slack=None, not_before=0.0):
        if not isinstance(pairs, list):
            pairs = [pairs]
        nbytes = 0
        for (o, a) in pairs:
            n = 1
            for d in o.shape:
                n *= int(d)
            nbytes += n * 4
        issue_ns = (900.0 if eng == 'pool' else 60.0) * len(pairs)
        if transpose:
            fns = [lambda e, o=o, a=a: e.dma_start_transpose(out=o, in_=a) for (o, a) in pairs]
            issue_ns = 1280.0 * len(pairs)
        else:
            fns = [lambda e, o=o, a=a: e.dma_start(out=o, in_=a, allow_slow_non_contiguous=slow) for (o, a) in pairs]
        is_store = not type(pairs[0][0].tensor).__name__.startswith('SB')
        self.ops.append(dict(id=len(self.ops), eng=eng, fns=fns, reads=list(reads), writes=list(writes), dma='D:' + str(semkey),
                             epoch=self.epoch, prio=self.prio, dur=None, final=final, nbytes=nbytes, issue_ns=issue_ns, slack=(slack if slack is not None else (self.STORE_SLACK if is_store else 0.0)), not_before=not_before))

    def barrier(self):
        self.epoch += 1

    def _deps(self, ops):
        lastw, readers = {}, {}
        for o in ops:
            d = set()
            for k in o['reads']:
                if k in lastw:
                    d.add(lastw[k])
            for k in o['writes']:
                if k in lastw:
                    d.add(lastw[k])
                for r in readers.get(k, ()):
                    d.add(r)
            d.discard(o['id'])
            o['deps'] = d
            for k in o['writes']:
                lastw[k] = o['id']
                readers[k] = []
            for k in o['reads']:
                if k not in o['writes']:
                    readers.setdefault(k, []).append(o['id'])

    def _schedule(self, ops):
        byid = {o['id']: o for o in ops}
        if not self.sched:
            return list(ops)
        succ = {o['id']: [] for o in ops}
        nd = {}
        for o in ops:
            nd[o['id']] = len(o['deps'])
            for d in o['deps']:
                succ[d].append(o['id'])
        for o in ops:
            if o['dur'] is None:
                o['dur'] = 60.0 if o['dma'] else _est_ns(o['eng'], o['fns'])
        free = {e: 0.0 for e in self.names}
        cand = {e: [] for e in self.names}
        ready_t = {}
        fin = {}
        dma_free = [0.0]
        for o in ops:
            if nd[o['id']] == 0:
                ready_t[o['id']] = 0.0
                cand[o['eng']].append(o['id'])
        order = []
        nleft = len(ops)
        while nleft:
            best = None
            for e in self.names:
                if not cand[e]:
                    continue
                fe = free[e]
                est = {i: max(fe, ready_t[i] + byid[i].get('slack', 0.0), byid[i].get('not_before', 0.0)) for i in cand[e]}
                m0 = min(est.values())
                bi = min((i for i in cand[e] if est[i] <= m0 + self.WINDOW), key=lambda i: (byid[i]['prio'], i))
                st = est[bi]
                if best is None or (st, byid[bi]['prio'], bi) < (best[0], best[1], best[2]):
                    best = (st, byid[bi]['prio'], bi, e)
            st, _, i, e = best
            o = byid[i]
            cand[e].remove(i)
            if o['dma']:
                free[e] = st + o['issue_ns']
                t0 = max(st, dma_free[0])
                dma_free[0] = t0 + o['nbytes'] / self.DMA_BW
                fin[i] = dma_free[0] + self.DMA_LAT
            else:
                free[e] = st + o['dur']
                fin[i] = free[e]
            o['start'] = st
            order.append(o)
            nleft -= 1
            for j in succ[i]:
                nd[j] -= 1
                ready_t[j] = max(ready_t.get(j, 0.0), fin[i] + (self.SYNC_LAT if byid[j]['eng'] != e else 60.0))
                if nd[j] == 0:
                    cand[byid[j]['eng']].append(j)
        self.model_ns = getattr(self, 'model_ns', 0.0) + max(fin.values())
        return order

    def emit(self):
        nc = self.nc
        engs = {'pe': 'tensor', 'act': 'scalar', 'dve': 'vector', 'pool': 'gpsimd', 'sp': 'sync'}
        nep = self.epoch + 1
        q = {e: [] for e in self.names}
        cnt = {e: 0 for e in self.names}
        dcnt = {}
        known = {e: {} for e in self.names}
        tok = {}
        finals = {}
        sem_names = set('E:' + e for e in self.names)
        for ep in range(nep):
            ops = [o for o in self.ops if o['epoch'] == ep]
            if not ops:
                continue
            self._deps(ops)
            order = self._schedule(ops)
            if ep > 0:
                toks = [('E:' + e, cnt[e]) for e in self.names if cnt[e] > 0] + list(dcnt.items())
                for e in self.names:
                    waits = []
                    for (s, v) in toks:
                        if e == 'pe' and s == 'E:pe':
                            continue
                        if known[e].get(s, 0) < v:
                            known[e][s] = v
                            waits.append((s, v))
                    if waits:
                        q[e].append(([], waits, None))
            for o in order:
                e = o['eng']
                waits = {}
                for d in sorted(o['deps']):
                    s, v, de = tok[d]
                    if de == 'pe' and e == 'pe':
                        continue
                    if known[e].get(s, 0) >= v:
                        continue
                    if waits.get(s, 0) < v:
                        waits[s] = v
                for s, v in waits.items():
                    known[e][s] = v
                if o['dma']:
                    s = o['dma']
                    sem_names.add(s)
                    dcnt[s] = dcnt.get(s, 0) + 16 * len(o['fns'])
                    tok[o['id']] = (s, dcnt[s], 'dma')
                    wl = list(waits.items())
                    for i, f in enumerate(o['fns']):
                        q[e].append(([f], wl if i == 0 else [], (s, 16)))
                    if o['final']:
                        finals[s] = dcnt[s]
                else:
                    cnt[e] += 1
                    tok[o['id']] = ('E:' + e, cnt[e], e)
                    q[e].append((o['fns'], list(waits.items()), ('E:' + e, 1)))
        with contextlib.ExitStack() as st:
            sems = {}
            for i, s in enumerate(sorted(sem_names)):
                sems[s] = st.enter_context(nc.semaphore('s%d' % i))
            fin = dict(finals)
            for e in self.names:
                if cnt[e] > 0:
                    fin['E:' + e] = cnt[e]
            block = st.enter_context(nc.Block())

            def make(ename):
                def body(eng):
                    for fns, waits, inc in q[ename]:
                        for (s, v) in waits:
                            eng.wait_ge(sems[s], v)
                        ins = None
                        for f in fns:
                            ins = f(eng)
                        if inc is not None:
                            ins.then_inc(sems[inc[0]], inc[1])
                    if ename == 'sp':
                        for s, v in fin.items():
                            eng.wait_ge(sems[s], v)
                return body

            for ename in self.names:
                getattr(block, engs[ename])(make(ename))
        return nc


def _log_gamma():
    lg = np.log1p(-np.exp(np.linspace(math.log(1.0 / 32), math.log(1.0 / 512), 4).astype(np.float32))).astype(np.float32)
    return lg.astype(np.float64)


def _consts():
    c = {}
    c['ident'] = np.eye(128, dtype=np.float32)
    lg = _log_gamma()
    inv = (np.float32(10000.0) ** (-np.arange(64, dtype=np.float32) / np.float32(64))).astype(np.float32)
    pos = np.arange(NT * 128, dtype=np.float32)
    ang = (pos[:, None] * inv[None, :]).astype(np.float32).astype(np.float64)
    cos, sin = np.cos(ang).reshape(NT, 128, 1, 64), np.sin(ang).reshape(NT, 128, 1, 64)
    i = np.arange(128, dtype=np.float64)
    gq = np.exp(lg[None, :] * (i[:, None] - 127.0)).reshape(1, 128, 4, 1)
    gk = (np.exp(lg[None, :] * (127.0 - i[:, None])) * (128.0 ** -0.5)).reshape(1, 128, 4, 1)
    c['tabs'] = np.stack([cos * gq, sin * gq, cos * gk, sin * gk], axis=2).astype(np.float32).reshape(NT, 128, 1024)
    inva = (np.float32(500000.0) ** (-np.arange(8, dtype=np.float32) / np.float32(8))).astype(np.float32)
    anga = (pos[:, None] * inva[None, :]).astype(np.float32).astype(np.float64).reshape(NT, 128, 8)
    c['ropeA'] = np.ascontiguousarray(np.concatenate([np.cos(anga), np.sin(anga)], -1).transpose(1, 0, 2)).astype(np.float32).reshape(128, NT * 16)
    ps = np.float32(16384.0)
    a1 = (ps * inv).astype(np.float32).astype(np.float64)
    a2 = (ps * inva).astype(np.float32).astype(np.float64)
    row = np.concatenate([np.cos(a1), np.sin(a1), np.cos(a2), np.sin(a2)]).astype(np.float32)
    c['tabS'] = np.tile(row[None, :], (NB, 1))
    jj, ii = np.meshgrid(np.arange(128), np.arange(128), indexing='ij')
    c['mcur'] = (jj <= ii).astype(np.float32)
    c['mprev'] = (jj >= ii).astype(np.float32)
    G = np.exp(lg * 128.0)
    c['Gt'] = np.tile(np.repeat(G, 128)[None, :], (128, 1)).astype(np.float32)
    c['G1t'] = np.tile(np.repeat(np.exp(lg), 128)[None, :], (128, 1)).astype(np.float32)
    c['gam'] = np.exp(lg).astype(np.float64)
    c['G'] = G
    r = np.arange(128)
    c['selb'] = (r[:, None] % 16 == np.arange(16)[None, :]).astype(np.float32)
    gg = (r // 16) % 2
    c['selg'] = np.stack([(gg == 0), (gg == 1)], 1).astype(np.float32)
    c['hrow'] = gg * 4 + (r // 16) // 2
    return c


_C = None


def build_nc():
    nc = bass.Bass("TRN2", target_bir_lowering=False)

    def din(name, shape):
        return nc.dram_tensor(name, list(shape), F32, kind="ExternalInput").ap()

    def dout(name, shape):
        return nc.dram_tensor(name, list(shape), F32, kind="ExternalOutput").ap()

    xp = din("xp", [NT * 128, D]); xs = din("xs", [NB, D]); cT = din("cT", [128, 8 * 17])
    ck = din("ck", [NB, 128, 128]); cv = din("cv", [NB, 128, 128]); stt = din("st", [NB, 4, 128, 128])
    wam = din("wam", [D, 3 * D]); bam = din("bam", [128, 24]); waf = din("waf", [D, 3 * D]); baf = din("baf", [128, 24])
    w_in = din("w_in", [D, DFF]); sinks = din("sinks", [128, 4]); sinkr = din("sinkr", [128, 1]); gnw = din("gnw", [128, 4]); w_out = din("w_out", [D, D])
    ln1w = din("ln1w", [D]); ln1b = din("ln1b", [D]); ln2w = din("ln2w", [D]); ln2b = din("ln2b", [D])
    w_up = din("w_up", [D, 2 * DFF]); w_down = din("w_down", [DFF, D])
    ident_d = din("ident", [128, 128]); tabs = din("tabs", [NT, 128, 1024]); ropeA_d = din("ropeA", [128, NT * 16])
    tabS_d = din("tabS", [NB, 144]); mcur_d = din("mcur", [128, 128]); mprev_d = din("mprev", [128, 128])
    Gt_d = din("Gt", [128, 512]); G1t_d = din("G1t", [128, 512]); selb_d = din("selb", [128, 16]); selg_d = din("selg", [128, 2])

    yp = dout("yp", [NT * 128, D]); ys = dout("ys", [NB, D])
    nkp = dout("nkp", [128, 128]); nvp = dout("nvp", [128, 128]); nrp = dout("nrp", [4, 128, 128])
    nks = dout("nks", [NB, 128, 128]); nvs = dout("nvs", [NB, 128, 128]); nrs = dout("nrs", [NB, 4, 128, 128])

    y1d = nc.dram_tensor("y1d", [NT * 128, D], F32, kind="Internal").ap()
    wus = nc.dram_tensor("wus", [22, 128, 8, 256], BF16, kind="Internal").ap()
    xbd = nc.dram_tensor("xbd", [NT * 128, D], BF16, kind="Internal").ap()
    y1b = nc.dram_tensor("y1b", [NT * 128, D], BF16, kind="Internal").ap()

    cst = _consts()
    gam = [float(x) for x in cst['gam']]
    Gh = [float(x) for x in cst['G']]

    P = Prog(nc)

    with contextlib.ExitStack() as gst:
        def sbg(name, shape, dt=F32):
            return gst.enter_context(nc.sbuf_tensor("s_" + name, shape, dt))

        pT = gst.enter_context(nc.psum_tensor("pT", [128, 512], F32))
        pZ = [gst.enter_context(nc.psum_tensor("pZ%d" % i, [128, 512], F32)) for i in range(2)]
        pS = [gst.enter_context(nc.psum_tensor("pS%d" % i, [128, 512], F32)) for i in range(2)]
        pO = gst.enter_context(nc.psum_tensor("pO", [128, 512], F32))
        pK = gst.enter_context(nc.psum_tensor("pK", [128, 512], F32))

        s1o = contextlib.ExitStack()

        def sbo(name, shape, dt=F32):
            return s1o.enter_context(nc.sbuf_tensor("s_" + name, shape, dt))
        idf = sbg("idf", [128, 128]); idb = sbg("idb", [128, 128], BF16)
        esink = sbg("esink", [128, 4]); sinkrow = sbg("sinkrow", [128, 1])
        mT1 = sbg("mT1", [128, 24, 17]); mT2 = sbg("mT2", [128, 24, 17])
        sc1 = sbg("sc1", [128, 8, 17]); sc2 = sbg("sc2", [128, 8, 17]); gp1 = sbg("gp1", [128, 8, 17]); gp2 = sbg("gp2", [128, 8, 17])
        epsl = sbg("epsl", [128, 1]); epsg = sbg("epsg", [128, 1])
        y1s = sbg("y1s", [NB, D])
        selb = sbg("selb", [128, 16])
        win = sbg("win", [128, 8, DFF], BF16)
        wdn = win[:].rearrange("p a b -> p (a b)").rearrange("p (c n) -> p c n", c=22)
        mcur = sbo("mcur", [128, 128], BF16); mprev = sbo("mprev", [128, 128], BF16)
        mtmp = sbo("mtmp", [128, 256])
        Gt = sbo("Gt", [128, 512]); ropeA = sbo("ropeA", [128, NT * 16])
        gnwT = sbo("gnwT", [128, 4])
        ln1wb = sbo("ln1wb", [128, D]); ln1bb = sbo("ln1bb", [128, D])
        g1 = sbo("g1", [128, D]); g1s = sbo("g1s", [NB, D])
        onesp = sbo("onesp", [128, 2, 128], BF16)

        P.dma('sp', (idf[:], ident_d), writes=['idf'], semkey='c0')
        P.dma('sp', [(mtmp[:, 0:128], mcur_d), (mtmp[:, 128:256], mprev_d)], writes=['mtmp'], semkey='c1')
        P.dma('sp', [(Gt[:], Gt_d), (ropeA[:], ropeA_d), (esink[:], sinks), (sinkrow[:], sinkr), (selb[:], selb_d)], writes=['Gt', 'ropeA', 'esink', 'sinkrow', 'selb'], semkey='c2')
        P.dma('sp', [(gnwT[:], gnw), (ln1wb[:], ln1w.partition_broadcast(128)), (ln1bb[:], ln1b.partition_broadcast(128))],
              writes=['gnwT', 'ln1wb', 'ln1bb'], semkey='c3')
        P.op('dve', lambda e: e.tensor_copy(out=idb[:], in_=idf[:]), reads=['idf'], writes=['idb'])
        P.op('dve', lambda e: e.tensor_copy(out=mcur[:], in_=mtmp[:, 0:128]), reads=['mtmp'], writes=['mcur'])
        P.op('dve', lambda e: e.tensor_copy(out=mprev[:], in_=mtmp[:, 128:256]), reads=['mtmp'], writes=['mprev'])
        P.op('act', lambda e: e.activation(out=esink[:], in_=esink[:], func=AF.Exp), reads=['esink'], writes=['esink'])
        P.op('dve', lambda e: e.memset(epsl[:], LN_EPS), writes=['epsl'])
        P.op('dve', lambda e: e.memset(epsg[:], GN_EPS), writes=['epsg'])
        P.op('dve', lambda e: e.memset(onesp[:], 0.0), writes=['onesp'])
        P.op('dve', lambda e: e.memset(onesp[:, 0, 0:64], 1.0), writes=['onesp'])
        P.op('dve', lambda e: e.memset(onesp[:, 1, 64:128], 1.0), writes=['onesp'])

        WoA = sbo("WoA", [128, 4, D], BF16); WoR = sbo("WoR", [128, 4, D], BF16)

        def make_gate(gp, gnm, gt, gtn, gs, gsn, dg, onesf, xk=()):
            for kc in range(8):
                P.op('dve', lambda e, kc=kc: e.tensor_scalar(out=dg[:, kc, :], in0=idf[:], scalar1=gp[:, kc, 0:1], scalar2=None, op0=ALU.mult),
                     reads=['idf', gnm], writes=[('dg', kc)] + list(xk))
            for half in range(2):
                P.op('pe', [lambda e, kk=kk, half=half: e.matmul(pT[:, kk * 128:(kk + 1) * 128], lhsT=onesf[:], rhs=dg[:, half * 4 + kk, :], start=True, stop=True)
                            for kk in range(4)], reads=[('dg', half * 4 + kk) for kk in range(4)] + ['onesf'] + list(xk), writes=['pT'])
                P.op('act', lambda e, half=half: e.activation(out=gt[:, half * 512:(half + 1) * 512], in_=pT[:], func=AF.Copy), reads=['pT'], writes=[gtn])
                P.op('pe', [lambda e, kk=kk, half=half: e.transpose(out=pT[0:NB, kk * 128:(kk + 1) * 128], in_=gp[:, half * 4 + kk, 1:17], identity=idf[:])
                            for kk in range(4)], reads=[gnm, 'idf'], writes=['pT'])
                P.op('act', lambda e, half=half: e.activation(out=gs[:, half * 512:(half + 1) * 512], in_=pT[0:NB, :], func=AF.Copy), reads=['pT'], writes=[gsn])

        waring = sbo("waring", [128, 2, 8, 256], BF16)
        cTt = sbo("cTt", [128, 8 * 17]); scT = sbo("scT", [128, 8, 17], BF16); sct1 = sbo("sct1", [128, 8 * 17])
        b1 = sbo("b1", [128, 24]); b2 = sbo("b2", [128, 24])
        onesf = sbo("onesf", [128, 128])

        def ada(wsrc, bb, bnm, mT, mnm, slack0, dslack, pa=None, pkey='pT', slots=None):
            pa = pT if pa is None else pa
            if slots is None:
                slots = [(waring[:, i, :, :], ('waring', i), 'wa%d' % i) for i in range(2)]
            ns = len(slots)
            for j2 in range(12):
                wt, wkey, wsem = slots[j2 % ns]
                P.dma('pool', (wt, wsrc[:, j2 * 256:(j2 + 1) * 256].rearrange("(kc p) n -> p kc n", p=128)), writes=[wkey], semkey=wsem,
                      slack=0.0, not_before=slack0 + j2 * dslack)
                for jj in range(2):
                    j = j2 * 2 + jj
                    P.op('pe', [lambda e, j=j, jj=jj, kc=kc, wt=wt: e.matmul(pa[:, j * 17:(j + 1) * 17], lhsT=wt[:, kc, jj * 128:(jj + 1) * 128], rhs=scT[:, kc, :], start=(kc == 0), stop=(kc == 7))
                                for kc in range(8)], reads=[wkey, 'scT'], writes=[pkey])
            P.op('dve', lambda e: e.tensor_tensor(out=mT[:], in0=pa[:, 0:408].rearrange("p (a b) -> p a b", a=24), in1=bb[:].unsqueeze(2).to_broadcast([128, 24, 17]), op=ALU.add),
                 reads=[pkey, bnm], writes=[mnm])

        def emit_phase0():
            P.prio = -10.0
            P.dma('sp', [(cTt[:], cT), (b1[:], bam), (b2[:], baf)], writes=['cTt', 'b1', 'b2'], semkey='c4')
            P.op('act', lambda e: e.activation(out=sct1[:], in_=cTt[:], func=AF.Exp, scale=-1.0), reads=['cTt'], writes=['sct1'])
            P.op('act', lambda e: e.activation(out=sct1[:], in_=sct1[:], func=AF.Ln, bias=1.0, scale=1.0), reads=['sct1'], writes=['sct1'])
            P.op('act', lambda e: e.activation(out=sct1[:], in_=sct1[:], func=AF.Exp, scale=-1.0), reads=['sct1'], writes=['sct1'])
            P.op('dve', lambda e: e.tensor_tensor(out=scT[:].rearrange("p a b -> p (a b)"), in0=cTt[:], in1=sct1[:], op=ALU.mult), reads=['cTt', 'sct1'], writes=['scT'])
            def as_ring(t32):
                return t32[:].bitcast(BF16).rearrange("p (kc n) -> p kc n", kc=8)
            ring = [(waring[:, i, :, :], ('waring', i), 'wa%d' % i) for i in range(2)] + \
                   [(as_ring(xt[0]), ('xt', 0), 'wa4'), (as_ring(xt[1]), ('xt', 1), 'wa5'), (as_ring(y1[0]), ('y1', 0), 'wa6'), (as_ring(y1[1]), ('y1', 1), 'wa7'),
                    (as_ring(tm), 'tm', 'wa2'), (as_ring(rr), 'rr', 'wa3')]
            ada(wam, b1, 'b1', mT1, 'mT1', 0.0, 0.0, slots=ring)
            P.op('dve', lambda e: e.tensor_scalar_add(out=sc1[:], in0=mT1[:, 8:16, :], scalar1=1.0), reads=['mT1'], writes=['sc1'])
            P.op('dve', lambda e: e.tensor_scalar_add(out=gp1[:], in0=mT1[:, 16:24, :], scalar1=1.0), reads=['mT1'], writes=['gp1'])
            P.dma('pool', (xbd[0:512, :], xp[0:512, :]), writes=[('xbd', 0)], semkey='xbd0', slack=0.0, not_before=20e3)
            for j in range(6):
                P.dma('pool', (win[:, :, j * 512:min(DFF, (j + 1) * 512)], w_in[:, j * 512:min(DFF, (j + 1) * 512)].rearrange("(kc p) n -> p kc n", p=128)),
                      writes=[('win', j)], semkey='win%d' % j, slack=0.0, not_before=30e3 + j * 6e3)
            wo_att = w_out[0:512, :].rearrange("(g hg d) n -> g d hg n", g=2, hg=4, d=64)
            P.dma('pool', [(WoA[0:64, :, :], wo_att[0]), (WoA[64:128, :, :], wo_att[1]),
                           (WoR[:], w_out[512:1024, :].rearrange("(h p) n -> p h n", p=128))], writes=['WoA', 'WoR'], semkey='wo', slack=0.0, not_before=70e3)
            for h in range(4):
                P.op('dve', lambda e, h=h: e.tensor_scalar(out=WoR[:, h, :], in0=WoR[:, h, :], scalar1=gnwT[:, h:h + 1], scalar2=None, op0=ALU.mult), reads=['WoR', 'gnwT'], writes=['WoR'])
            P.op('dve', lambda e: e.memset(onesf[:], 1.0), writes=['onesf'])
            make_gate(gp1, 'gp1', g1, 'g1', g1s, 'g1s', tm[:].rearrange("p (a b) -> p a b", a=8), onesf, xk=['tm'])
            for c4 in range(1, NT // 4):
                P.dma('pool', (xbd[c4 * 512:(c4 + 1) * 512, :], xp[c4 * 512:(c4 + 1) * 512, :]), reads=([('xbd', c4 - 2)] if c4 >= 2 else []), writes=[('xbd', c4)],
                      semkey='xbd%d' % (c4 % 2), slack=0.0, not_before=max(110e3, c4 * 100e3 - 60e3))
            P.prio = 1000.0
            for c in range(22):
                P.dma('pool', [(wus[c, :, :, 0:128], w_up[:, c * 128:(c + 1) * 128].rearrange("(kc p) n -> p kc n", p=128)),
                               (wus[c, :, :, 128:256], w_up[:, DFF + c * 128:DFF + (c + 1) * 128].rearrange("(kc p) n -> p kc n", p=128))],
                      reads=([('wus', c - 8)] if c >= 8 else []), writes=[('wus', c)], semkey='wus%d' % (c % 8), slack=0.0, not_before=350e3 + c * 20e3)
            P.prio = 0.0

        with contextlib.ExitStack() as s1:
            def sb1(name, shape, dt=F32):
                return s1.enter_context(nc.sbuf_tensor("s_" + name, shape, dt))
            with contextlib.ExitStack() as s1p:
                def sbp(name, shape, dt=F32):
                    return s1p.enter_context(nc.sbuf_tensor("s_" + name, shape, dt))
                pTb = s1p.enter_context(nc.psum_tensor("pTb", [128, 1024], BF16))
                xt = [sbp("xt%d" % i, [128, D]) for i in range(2)]
                xTb = [sbp("xTb%d" % i, [128, 8, 128], BF16) for i in range(2)]
                tab = [sbp("tab%d" % i, [128, 1024]) for i in range(2)]
                hT = [sbp("hT%d" % i, [128, 8, 128], BF16) for i in range(2)]
                qr = [sbp("qr%d" % i, [128, 512], BF16) for i in range(2)]
                kr = [sbp("kr%d" % i, [128, 128]) for i in range(2)]
                Vp = [sbp("Vp%d" % i, [128, 2, 128], BF16) for i in range(3)]
                kT = [sbp("kT%d" % i, [128, 128], BF16) for i in range(3)]
                rqh = [sbp("rqh%d" % i, [128, 512], BF16) for i in range(2)]
                rkh = [sbp("rkh%d" % i, [128, 512], BF16) for i in range(3)]
                rv = [sbp("rv%d" % i, [128, 512], BF16) for i in range(3)]
                sg = [sbp("sg%d" % i, [128, 512]) for i in range(3)]
                tA = [sbp("tA%d" % i, [128, 512]) for i in range(2)]
                tB = [sbp("tB%d" % i, [128, 512]) for i in range(2)]
                ta = sbp("ta", [128, 8, 16]); tb = sbp("tb", [128, 8, 16]); rst = sbp("rst", [128, 8, 16])
                qT = [sbp("qT%d" % i, [128, 4, 128], BF16) for i in range(3)]
                rqkT = [sbp("rqkT%d" % i, [128, 8, 128], BF16) for i in range(3)]
                pex = [sbp("pex%d" % i, [128, 512], BF16) for i in range(2)]
                PT = [sbp("PT%d" % i, [128, 512], BF16) for i in range(4)]
                dsum = sbp("dsum", [128, 512])
                attT = [sbp("attT%d" % i, [128, 4, 128], BF16) for i in range(2)]
                PTr = sbp("PTr", [128, 512], BF16)
                osb = sbp("osb", [128, 512]); osq = sbp("osq", [128, 512]); onr = sbp("onr", [128, 512])
                S = sbp("S", [128, 512]); Sb = sbp("Sb", [128, 4, 128], BF16)
                ret = [sbp("ret%d" % i, [128, 512], BF16) for i in range(2)]
                retT = sbp("retT", [128, 4, 128], BF16)
                st4 = sbp("st4", [128, 16])
                tm = sbp("tm", [128, D]); rr = sbp("rr", [128, D]); jk = None
                y1 = [sbp("y1_%d" % i, [128, D]) for i in range(2)]
                st2 = sbp("st2", [128, 8])
                etmp = sbp("etmp", [128, 512])

                P.op('dve', lambda e: e.memset(S[:], 0.0), writes=['S'])
                for i in range(3):
                    P.op('dve', lambda e, i=i: e.memset(Vp[i][:], 0.0), writes=[('Vp', i)])

                def ln_tail(rr_ap, n, stt, wb, bb, out_ap, keys_r, key_out, npart=128, jk_=None):
                    pp = slice(0, npart)
                    jk_ = tm[:].bitcast(BF16) if jk_ is None else jk_
                    P.op('act', lambda e: e.activation(out=jk_[pp, 0:n], in_=rr_ap, func=AF.Identity, accum_out=stt[pp, 0:1]), reads=keys_r, writes=['jk', 'tm', 'stt'])
                    P.op('act', lambda e: e.activation(out=jk_[pp, 0:n], in_=rr_ap, func=AF.Square, accum_out=stt[pp, 1:2]), reads=keys_r, writes=['jk', 'tm', 'stt'])
                    P.op('dve', lambda e: e.tensor_scalar(out=stt[pp, 2:3], in0=stt[pp, 0:1], scalar1=1.0 / n, scalar2=None, op0=ALU.mult), reads=['stt'], writes=['stt'])
                    P.op('dve', lambda e: e.tensor_tensor(out=stt[pp, 3:4], in0=stt[pp, 2:3], in1=stt[pp, 2:3], op=ALU.mult), reads=['stt'], writes=['stt'])
                    P.op('dve', lambda e: e.scalar_tensor_tensor(out=stt[pp, 4:5], in0=stt[pp, 1:2], scalar=1.0 / n, in1=stt[pp, 3:4], op0=ALU.mult, op1=ALU.subtract), reads=['stt'], writes=['stt'])
                    P.op('act', lambda e: e.activation(out=stt[pp, 5:6], in_=stt[pp, 4:5], func=AF.Ln, bias=epsl[pp, 0:1], scale=1.0), reads=['stt', 'epsl'], writes=['stt'])
                    P.op('act', lambda e: e.activation(out=stt[pp, 6:7], in_=stt[pp, 5:6], func=AF.Exp, scale=-0.5), reads=['stt'], writes=['stt'])
                    P.op('dve', lambda e: e.scalar_tensor_tensor(out=stt[pp, 7:8], in0=stt[pp, 2:3], scalar=-1.0, in1=stt[pp, 6:7], op0=ALU.mult, op1=ALU.mult), reads=['stt'], writes=['stt'])
                    P.op('act', lambda e: e.activation(out=rr_ap, in_=rr_ap, func=AF.Identity, scale=stt[pp, 6:7], bias=stt[pp, 7:8]), reads=keys_r + ['stt'], writes=keys_r)
                    P.op('dve', lambda e: e.tensor_tensor(out=rr_ap, in0=rr_ap, in1=wb, op=ALU.mult), reads=keys_r, writes=keys_r)
                    P.op('dve', lambda e: e.tensor_tensor(out=out_ap, in0=rr_ap, in1=bb, op=ALU.add), reads=keys_r, writes=[key_out])

                def rope_att(src, H, cosap, sinap, dst, dkey, zkey, npart=128):
                    pp = slice(0, npart)
                    v = src.rearrange("p (h d) -> p h d", h=H)[:, :, 0:16].rearrange("p h (two j) -> p h two j", two=2)
                    dv = dst.rearrange("p (h d) -> p h d", h=H)
                    tav = ta[pp, 0:H, :].rearrange("p h (two j) -> p h two j", two=2)
                    tbv = tb[pp, 0:H, :].rearrange("p h (two j) -> p h two j", two=2)
                    cb = cosap.unsqueeze(1).unsqueeze(1).to_broadcast([npart, H, 2, 8])
                    sb_ = sinap.unsqueeze(1).unsqueeze(1).to_broadcast([npart, H, 2, 8])
                    rsv = rst[pp, 0:H, :].rearrange("p h (two j) -> p h two j", two=2)
                    P.op('act', lambda e: e.activation(out=dst, in_=src, func=AF.Copy), reads=[zkey], writes=[dkey])
                    P.op('act', lambda e: e.activation(out=rsv, in_=v, func=AF.Copy), reads=[zkey], writes=['rst'])
                    P.op('dve', lambda e: e.tensor_tensor(out=tav, in0=rsv, in1=cb, op=ALU.mult), reads=['rst', 'ropeA'], writes=['ta'])
                    P.op('dve', lambda e: e.tensor_tensor(out=tbv, in0=rsv, in1=sb_, op=ALU.mult), reads=['rst', 'ropeA'], writes=['tb'])
                    P.op('dve', lambda e: e.tensor_tensor(out=dv[:, :, 0:8], in0=tav[:, :, 0, :], in1=tbv[:, :, 1, :], op=ALU.subtract), reads=['ta', 'tb'], writes=[dkey])
                    P.op('dve', lambda e: e.tensor_tensor(out=dv[:, :, 8:16], in0=tav[:, :, 1, :], in1=tbv[:, :, 0, :], op=ALU.add), reads=['ta', 'tb'], writes=[dkey])

                def front(t):
                    s2, s3, s4 = t % 2, t % 3, t % 4
                    def xT_load(tt):
                        P.dma('sp', [(xTb[tt % 2][:, kc, :], xbd[tt * 128:(tt + 1) * 128, kc * 128:(kc + 1) * 128]) for kc in range(8)],
                              reads=[('xbd', tt // 4)], writes=[('xTb', tt % 2)], semkey='xTb%d' % (tt % 2), transpose=True)
                    if t == 0:
                        xT_load(0)
                    if t + 1 < NT:
                        xT_load(t + 1)
                    P.dma('sp', (tab[s2][:], tabs[t]), writes=[('tab', s2)], semkey='tab%d' % s2)
                    for kc in range(8):
                        if kc % 2 == 0:
                            P.op('dve', lambda e, kc=kc: e.tensor_scalar(out=hT[s2][:, kc, :], in0=xTb[s2][:, kc, :], scalar1=sc1[:, kc, 0:1], scalar2=mT1[:, kc, 0:1],
                                                                       op0=ALU.mult, op1=ALU.add), reads=[('xTb', s2), 'sc1', 'mT1'], writes=[('hT', s2)])
                        else:
                            P.op('act', lambda e, kc=kc: e.activation(out=hT[s2][:, kc, :], in_=xTb[s2][:, kc, :], func=AF.Identity, scale=sc1[:, kc, 0:1], bias=mT1[:, kc, 0:1]),
                                 reads=[('xTb', s2), 'sc1', 'mT1'], writes=[('hT', s2)])
                    for ci in range(6):
                        n0 = ci * 512
                        nw = 512 if ci < 5 else 256
                        pz = (pZ[0], pZ[1], pT)[ci % 3]
                        zkey = (('pZ', 0), ('pZ', 1), 'pT')[ci % 3]
                        P.op('pe', [lambda e, kc=kc, pz=pz, n0=n0, nw=nw: e.matmul(pz[:, 0:nw], lhsT=hT[s2][:, kc, :], rhs=win[:, kc, n0:n0 + nw], start=(kc == 0), stop=(kc == 7))
                                    for kc in range(8)], reads=[('hT', s2), ('win', ci)], writes=[zkey])
                        if ci == 0:
                            rope_att(pz[:, 0:512], 8, ropeA[:, t * 16:t * 16 + 8], ropeA[:, t * 16 + 8:t * 16 + 16], qr[s2][:], ('qr', s2), zkey)
                        elif ci in (1, 2):
                            dst = rqh[s2] if ci == 1 else rkh[s3]
                            dkey = ('rqh', s2) if ci == 1 else ('rkh', s3)
                            cofs = 0 if ci == 1 else 512
                            A, B = tA[ci - 1], tB[ci - 1]
                            zv = pz[:, 0:512].rearrange("p (h two j) -> p h two j", h=4, two=2)
                            cb = tab[s2][:, cofs:cofs + 256].rearrange("p (h j) -> p h j", h=4).unsqueeze(2).to_broadcast([128, 4, 2, 64])
                            sb_ = tab[s2][:, cofs + 256:cofs + 512].rearrange("p (h j) -> p h j", h=4).unsqueeze(2).to_broadcast([128, 4, 2, 64])
                            Av = A[:].rearrange("p (h two j) -> p h two j", h=4, two=2)
                            Bv = B[:].rearrange("p (h two j) -> p h two j", h=4, two=2)
                            dv = dst[:].rearrange("p (h two j) -> p h two j", h=4, two=2)
                            P.op('dve', lambda e, Av=Av, zv=zv, cb=cb: e.tensor_tensor(out=Av, in0=zv, in1=cb, op=ALU.mult), reads=[zkey, ('tab', s2)], writes=[('tA', ci)])
                            P.op('dve', lambda e, Bv=Bv, zv=zv, sb_=sb_: e.tensor_tensor(out=Bv, in0=zv, in1=sb_, op=ALU.mult), reads=[zkey, ('tab', s2)], writes=[('tB', ci)])
                            P.op('dve', lambda e, dv=dv, Av=Av, Bv=Bv: e.tensor_tensor(out=dv[:, :, 0, :], in0=Av[:, :, 0, :], in1=Bv[:, :, 1, :], op=ALU.subtract),
                                 reads=[('tA', ci), ('tB', ci)], writes=[dkey])
                            P.op('dve', lambda e, dv=dv, Av=Av, Bv=Bv: e.tensor_tensor(out=dv[:, :, 1, :], in0=Av[:, :, 1, :], in1=Bv[:, :, 0, :], op=ALU.add),
                                 reads=[('tA', ci), ('tB', ci)], writes=[dkey])
                        elif ci == 3:
                            P.op('act', lambda e, pz=pz: e.activation(out=rv[s3][:], in_=pz[:, 0:512], func=AF.Copy), reads=[zkey], writes=[('rv', s3)])
                        elif ci == 4:
                            P.op('act', lambda e, pz=pz: e.activation(out=sg[s3][:], in_=pz[:, 0:512], func=AF.Copy), reads=[zkey], writes=[('sg', s3)])
                            P.op('act', lambda e, pz=pz: e.activation(out=etmp[:], in_=pz[:, 0:512], func=AF.Exp, scale=-1.0), reads=[zkey], writes=['etmp'])
                            P.op('act', lambda e: e.activation(out=etmp[:], in_=etmp[:], func=AF.Ln, bias=1.0, scale=1.0), reads=['etmp'], writes=['etmp'])
                            P.op('act', lambda e: e.activation(out=etmp[:], in_=etmp[:], func=AF.Exp, scale=-1.0), reads=['etmp'], writes=['etmp'])
                            P.op('dve', lambda e: e.tensor_tensor(out=sg[s3][:], in0=sg[s3][:], in1=etmp[:], op=ALU.mult), reads=[('sg', s3), 'etmp'], writes=[('sg', s3)])
                        else:
                            rope_att(pz[:, 0:128], 2, ropeA[:, t * 16:t * 16 + 8], ropeA[:, t * 16 + 8:t * 16 + 16], kr[s2][:], ('kr', s2), zkey)
                            vdst = Vp[s3][:].rearrange("p g c -> p (g c)")
                            P.op('act', lambda e, pz=pz, vdst=vdst: e.activation(out=vdst[:, 0:64], in_=pz[:, 128:192], func=AF.Copy), reads=[zkey], writes=[('Vp', s3)])
                            P.op('act', lambda e, pz=pz, vdst=vdst: e.activation(out=vdst[:, 192:256], in_=pz[:, 192:256], func=AF.Copy), reads=[zkey], writes=[('Vp', s3)])
                            if t == NT - 1:
                                vout = tA[0]
                                P.op('act', lambda e, pz=pz, vout=vout: e.activation(out=vout[:, 0:128], in_=pz[:, 128:256], func=AF.Copy), reads=[zkey], writes=[('tA', 1)])
                                P.dma('sp', (nvp, vout[:, 0:128]), reads=[('tA', 1)], semkey='nvp', final=True)
                    if t == NT - 1:
                        P.dma('sp', (nkp, kr[s2][:]), reads=[('kr', s2)], semkey='nkp', final=True)
                    P.op('pe', [lambda e, hg=hg: e.transpose(out=pTb[:, hg * 128:(hg + 1) * 128], in_=qr[s2][:, hg * 128:(hg + 1) * 128], identity=idb[:]) for hg in range(4)],
                         reads=[('qr', s2), 'idb'], writes=['pTb'])
                    P.op('act', lambda e: e.activation(out=qT[s3][:].rearrange("p a b -> p (a b)"), in_=pTb[:, 0:512], func=AF.Copy), reads=['pTb'], writes=[('qT', s3)])
                    P.op('pe', lambda e: e.transpose(out=pK[:, 0:128], in_=kr[s2][:], identity=idf[:]), reads=[('kr', s2), 'idf'], writes=['pK'])
                    P.op('dve', lambda e: e.tensor_copy(out=kT[s3][:], in_=pK[:, 0:128]), reads=['pK'], writes=[('kT', s3)])
                    P.op('pe', [lambda e, h=h: e.transpose(out=pTb[:, h * 128:(h + 1) * 128], in_=rqh[s2][:, h * 128:(h + 1) * 128], identity=idb[:]) for h in range(4)] +
                               [lambda e, h=h: e.transpose(out=pTb[:, (4 + h) * 128:(5 + h) * 128], in_=rkh[s3][:, h * 128:(h + 1) * 128], identity=idb[:]) for h in range(4)],
                         reads=[('rqh', s2), ('rkh', s3), 'idb'], writes=['pTb'])
                    P.op('dve', lambda e: e.tensor_copy(out=rqkT[s3][:].rearrange("p a b -> p (a b)"), in_=pTb[:, 0:1024]), reads=['pTb'], writes=[('rqkT', s3)])

                def back1(t):
                    s2, s3 = t % 2, t % 3
                    sprev = (t - 1) % 3
                    blks = ([('prev', sprev, mprev)] if t > 0 else []) + [('cur', s3, mcur)]
                    idx = 0
                    ptl = []
                    for g in range(2):
                        for (bn, ks, msk) in blks:
                            ps_ = pS[idx % 2]
                            pe_ = pex[idx % 2]
                            pt_ = PT[idx]
                            P.op('pe', lambda e, ps_=ps_, ks=ks, g=g: e.matmul(ps_[:], lhsT=kT[ks][g * 64:(g + 1) * 64, :], rhs=qT[s3][g * 64:(g + 1) * 64, :, :].rearrange("p a b -> p (a b)"),
                                                                              start=True, stop=True), reads=[('kT', ks), ('qT', s3)], writes=[('pS', idx % 2)])
                            P.op('act', lambda e, ps_=ps_, pe_=pe_: e.activation(out=pe_[:], in_=ps_[:], func=AF.Exp, scale=0.125), reads=[('pS', idx % 2)], writes=[('pex', idx % 2)])
                            P.op('dve', lambda e, pe_=pe_, pt_=pt_, msk=msk: e.tensor_tensor(out=pt_[:].rearrange("p (a b) -> p a b", a=4), in0=pe_[:].rearrange("p (a b) -> p a b", a=4),
                                                                                         in1=msk[:].unsqueeze(1).to_broadcast([128, 4, 128]), op=ALU.mult),
                                 reads=[('pex', idx % 2), 'mcur', 'mprev'], writes=[('PT', idx)])
                            ptl.append((g, ks, idx))
                            idx += 1
                    n = len(ptl)
                    P.op('pe', [lambda e, g=g, ks=ks, ix=ix, j=j: e.matmul(pO[:], lhsT=Vp[ks][:, g, :], rhs=PT[ix][:], start=(j == 0), stop=(j == n - 1)) for j, (g, ks, ix) in enumerate(ptl)],
                         reads=[('Vp', ks) for (_, ks, _) in ptl] + [('PT', ix) for (_, _, ix) in ptl], writes=['pO'])
                    P.op('pe', [lambda e, g=g, ix=ix, j=j: e.matmul(pK[:], lhsT=onesp[:, g, :], rhs=PT[ix][:], start=(j == 0), stop=(j == n - 1)) for j, (g, ks, ix) in enumerate(ptl)],
                         reads=['onesp'] + [('PT', ix) for (_, _, ix) in ptl], writes=['pK'])
                    P.op('dve', lambda e: e.tensor_tensor(out=dsum[:].rearrange("p (a b) -> p a b", a=4), in0=pK[:].rearrange("p (a b) -> p a b", a=4),
                                                          in1=esink[:].unsqueeze(2).to_broadcast([128, 4, 128]), op=ALU.add), reads=['pK', 'esink'], writes=['dsum'])
                    P.op('act', lambda e: e.activation(out=dsum[:], in_=dsum[:], func=AF.Ln), reads=['dsum'], writes=['dsum'])
                    P.op('act', lambda e: e.activation(out=dsum[:], in_=dsum[:], func=AF.Exp, scale=-1.0), reads=['dsum'], writes=['dsum'])
                    P.op('dve', lambda e: e.tensor_tensor(out=attT[s2][:].rearrange("p a b -> p (a b)"), in0=pO[:], in1=dsum[:], op=ALU.mult), reads=['pO', 'dsum'], writes=[('attT', s2)])
                    P.op('pe', [lambda e, h=h: e.matmul(pS[0][:, h * 128:(h + 1) * 128], lhsT=rqkT[s3][:, 4 + h, :], rhs=rqkT[s3][:, h, :], start=True, stop=True) for h in range(4)],
                         reads=[('rqkT', s3)], writes=[('pS', 0)])
                    P.op('dve', lambda e: e.tensor_tensor(out=PTr[:].rearrange("p (a b) -> p a b", a=4), in0=pS[0][:].rearrange("p (a b) -> p a b", a=4),
                                                          in1=mcur[:].unsqueeze(1).to_broadcast([128, 4, 128]), op=ALU.mult), reads=[('pS', 0), 'mcur'], writes=['PTr'])
                    fns = []
                    for h in range(4):
                        fns.append(lambda e, h=h: e.matmul(pO[:, h * 128:(h + 1) * 128], lhsT=PTr[:, h * 128:(h + 1) * 128], rhs=rv[s3][:, h * 128:(h + 1) * 128], start=True, stop=(t == 0)))
                        if t > 0:
                            fns.append(lambda e, h=h: e.matmul(pO[:, h * 128:(h + 1) * 128], lhsT=rqkT[s3][:, h, :], rhs=Sb[:, h, :], start=False, stop=True))
                    P.op('pe', fns, reads=['PTr', ('rv', s3), ('rqkT', s3), 'Sb'], writes=['pO'])
                    P.op('act', lambda e: e.activation(out=osb[:], in_=pO[:], func=AF.Copy), reads=['pO'], writes=['osb'])
                    P.op('pe', [lambda e, h=h: e.matmul(pK[:, h * 128:(h + 1) * 128], lhsT=rkh[s3][:, h * 128:(h + 1) * 128], rhs=rv[s3][:, h * 128:(h + 1) * 128], start=True, stop=True) for h in range(4)],
                         reads=[('rkh', s3), ('rv', s3)], writes=['pK'])
                    for h in range(4):
                        P.op('dve', lambda e, h=h: e.scalar_tensor_tensor(out=S[:, h * 128:(h + 1) * 128], in0=S[:, h * 128:(h + 1) * 128], scalar=Gh[h], in1=pK[:, h * 128:(h + 1) * 128],
                                                                         op0=ALU.mult, op1=ALU.add), reads=['S', 'pK'], writes=['S'])
                    if t < NT - 1:
                        P.op('dve', lambda e: e.tensor_tensor(out=Sb[:].rearrange("p a b -> p (a b)"), in0=S[:], in1=Gt[:], op=ALU.mult), reads=['S', 'Gt'], writes=['Sb'])
                    else:
                        P.dma('sp', (nrp.rearrange("h k v -> k h v"), S[:].rearrange("p (h v) -> p h v", h=4)), reads=['S'], semkey='nrp', final=True)
                    gn_tail(osb, sg[s3], ('sg', s3), ret[s2], ('ret', s2), 128)

                def gn_tail(osb_, sg_, sgkey, ret_, retkey, npart, osq_=None, onr_=None, st4_=None):
                    pp = slice(0, npart)
                    osq_ = osq if osq_ is None else osq_
                    onr_ = onr if onr_ is None else onr_
                    st4_ = st4 if st4_ is None else st4_
                    o3 = osb_[pp, :].rearrange("p (a b) -> p a b", a=4)
                    P.op('dve', lambda e: e.tensor_reduce(out=st4_[pp, 0:4], in_=o3, axis=AX.X, op=ALU.add), reads=['osb'], writes=['st4'])
                    P.op('act', lambda e: e.activation(out=osq_[pp, :], in_=osb_[pp, :], func=AF.Square), reads=['osb'], writes=['osq'])
                    P.op('dve', lambda e: e.tensor_reduce(out=st4_[pp, 4:8], in_=osq_[pp, :].rearrange("p (a b) -> p a b", a=4), axis=AX.X, op=ALU.add), reads=['osq'], writes=['st4'])
                    P.op('dve', lambda e: e.tensor_scalar(out=st4_[pp, 0:4], in0=st4_[pp, 0:4], scalar1=1.0 / 128, scalar2=None, op0=ALU.mult), reads=['st4'], writes=['st4'])
                    P.op('dve', lambda e: e.tensor_tensor(out=st4_[pp, 8:12], in0=st4_[pp, 0:4], in1=st4_[pp, 0:4], op=ALU.mult), reads=['st4'], writes=['st4'])
                    P.op('dve', lambda e: e.scalar_tensor_tensor(out=st4_[pp, 4:8], in0=st4_[pp, 4:8], scalar=1.0 / 128, in1=st4_[pp, 8:12], op0=ALU.mult, op1=ALU.subtract), reads=['st4'], writes=['st4'])
                    P.op('act', lambda e: e.activation(out=st4_[pp, 8:12], in_=st4_[pp, 4:8], func=AF.Ln, bias=epsg[pp, 0:1], scale=1.0), reads=['st4', 'epsg'], writes=['st4'])
                    P.op('act', lambda e: e.activation(out=st4_[pp, 12:16], in_=st4_[pp, 8:12], func=AF.Exp, scale=-0.5), reads=['st4'], writes=['st4'])
                    n3 = onr_[pp, :].rearrange("p (a b) -> p a b", a=4)
                    P.op('dve', lambda e: e.scalar_tensor_tensor(out=st4_[pp, 8:12], in0=st4_[pp, 0:4], scalar=-1.0, in1=st4_[pp, 12:16], op0=ALU.mult, op1=ALU.mult), reads=['st4'], writes=['st4'])
                    for h in range(4):
                        P.op('act', lambda e, h=h: e.activation(out=onr_[pp, h * 128:(h + 1) * 128], in_=osb_[pp, h * 128:(h + 1) * 128], func=AF.Identity,
                                                               scale=st4_[pp, 12 + h:13 + h], bias=st4_[pp, 8 + h:9 + h]), reads=['osb', 'st4'], writes=['onr'])
                    P.op('dve', lambda e: e.tensor_tensor(out=ret_[pp, :], in0=onr_[pp, :], in1=sg_[pp, :], op=ALU.mult), reads=['onr', sgkey], writes=[retkey])

                def back2(t):
                    s2, s4 = t % 2, t % 2
                    P.dma('sp', (xt[s4][:], xp[t * 128:(t + 1) * 128, :]), writes=[('xt', s4)], semkey='xt%d' % s4)
                    P.op('pe', [lambda e, h=h: e.transpose(out=pTb[:, h * 128:(h + 1) * 128], in_=ret[s2][:, h * 128:(h + 1) * 128], identity=idb[:]) for h in range(4)],
                         reads=[('ret', s2), 'idb'], writes=['pTb'])
                    P.op('act', lambda e: e.activation(out=retT[:].rearrange("p a b -> p (a b)"), in_=pTb[:, 0:512], func=AF.Copy), reads=['pTb'], writes=['retT'])
                    for nn in range(2):
                        P.op('pe', [lambda e, hg=hg, nn=nn: e.matmul(pZ[nn][:], lhsT=attT[s2][:, hg, :], rhs=WoA[:, hg, nn * 512:(nn + 1) * 512], start=(hg == 0), stop=False) for hg in range(4)] +
                                   [lambda e, h=h, nn=nn: e.matmul(pZ[nn][:], lhsT=retT[:, h, :], rhs=WoR[:, h, nn * 512:(nn + 1) * 512], start=False, stop=(h == 3)) for h in range(4)],
                             reads=[('attT', s2), 'retT', 'WoA', 'WoR'], writes=[('pZ', nn)])
                        P.op('dve', lambda e, nn=nn: e.tensor_tensor(out=tm[:, nn * 512:(nn + 1) * 512], in0=pZ[nn][:], in1=g1[:, nn * 512:(nn + 1) * 512], op=ALU.mult), reads=[('pZ', nn), 'g1'], writes=['tm'])
                    P.op('dve', lambda e: e.scalar_tensor_tensor(out=rr[:], in0=xt[s4][:], scalar=ALPHA, in1=tm[:], op0=ALU.mult, op1=ALU.add), reads=[('xt', s4), 'tm'], writes=['rr'])
                    ln_tail(rr[:], D, st2, ln1wb[:], ln1bb[:], y1[s2][:], ['rr'], ('y1', s2))
                    P.dma('sp', (y1d[t * 128:(t + 1) * 128, :], y1[s2][:]), reads=[('y1', s2)], writes=[('y1d', t)], semkey='y1o%d' % s2)
                    if t % 4 == 3:
                        g0 = (t - 3) * 128
                        P.dma('pool', (y1b[g0:g0 + 512, :], y1d[g0:g0 + 512, :]), reads=[('y1d', tt) for tt in range(t - 3, t + 1)] + ([('y1b', t - 8)] if t >= 11 else []),
                              writes=[('y1b', tt) for tt in range(t - 3, t + 1)], semkey='y1p%d' % ((t // 4) % 2), slack=0.0)

                emit_phase0()
                NT1 = DBG['nt1']
                for step in range(NT1 + 2):
                    if step < NT1:
                        P.prio = float(step)
                        front(step)
                    if 0 <= step - 1 < NT1:
                        P.prio = float(step - 1) + 0.3
                        back1(step - 1)
                    if 0 <= step - 2 < NT1:
                        P.prio = float(step - 2) + 0.6
                        back2(step - 2)
                P.prio = 0.0
            P.barrier()
            if SAMPLE:
              with contextlib.ExitStack() as s1s:
                def sbs(name, shape, dt=F32):
                    return s1s.enter_context(nc.sbuf_tensor("s_" + name, shape, dt))
                pA = s1s.enter_context(nc.psum_tensor("pA", [128, 512], F32))
                ada(waf, b2, 'b2', mT2, 'mT2', 0.0, 0.0, pa=pA, pkey='pA')
                P.op('dve', lambda e: e.tensor_scalar_add(out=sc2[:], in0=mT2[:, 8:16, :], scalar1=1.0), reads=['mT2'], writes=['sc2'])
                P.op('dve', lambda e: e.tensor_scalar_add(out=gp2[:], in0=mT2[:, 16:24, :], scalar1=1.0), reads=['mT2'], writes=['gp2'])
                xs_t = sbs("xs_t", [NB, D]); tabS = sbs("tabS", [NB, 144]); G1t = sbs("G1t", [128, 512]); selg = sbs("selg", [128, 2])
                ckt = sbs("ckt", [128, NB, 128]); cvt = ckt; Ss = sbs("Ss", [128, 32, 128])
                tmpS = sbs("tmpS", [128, 128]); hsT = sbs("hsT", [128, 8, NB], BF16)
                zs = sbs("zs", [NB, DFF]); rqs = sbs("rqs", [NB, 512]); rks = sbs("rks", [NB, 512])
                tAs = sbs("tAs", [NB, 512]); tBs = sbs("tBs", [NB, 512]); tas = sbs("tas", [NB, 8, 16]); tbs = sbs("tbs", [NB, 8, 16])
                rqT_s = sbs("rqT_s", [128, 4, NB]); ZQ = sbs("ZQ", [128, 4, NB * NB])
                qk_s = sbs("qk_s", [NB, 4]); osb_s = sbs("osb_s", [NB, 512]); osq_s = sbs("osq_s", [NB, 512]); onr_s = sbs("onr_s", [NB, 512])
                st4_s = sbs("st4_s", [NB, 16]); sg_s = sbs("sg_s", [NB, 512]); rets = sbs("rets", [NB, 512])
                KM = sbs("KM", [NB, 4, 512])
                QB = sbs("QB", [NB, 8, 128]); QT = sbs("QT", [128, 128]); ZQa = sbs("ZQa", [128, NB * 129]); ZP = ZQa
                KT = sbs("KT", [128, NB, 129]); Ps = sbs("Ps", [128, 129]); PTs = sbs("PTs", [128, 128])
                sm = sbs("sm", [128, 8]); PNT = sbs("PNT", [128, NB]); PN = sbs("PN", [NB, 128]); Apad = sbs("Apad", [128, 128])
                attT_s = sbs("attT_s", [128, 4, NB], BF16); retT_s = sbs("retT_s", [128, 4, NB], BF16)
                KMf = KM[:].rearrange("p a b -> p (a b)"); tm_s = KMf[:, 0:D]; rr_s = KMf[:, D:2 * D]; jk_s = sbs("jk_s", [NB, D], BF16); st2_s = sbs("st2_s", [NB, 8])

                P.dma('sp', [(xs_t[:], xs), (tabS[:], tabS_d), (G1t[:], G1t_d), (selg[:], selg_d)], writes=['xs_t', 'tabS', 'G1t', 'selg'], semkey='sa')
                P.dma('sp', (ckt[:], ck.rearrange("b j c -> j b c")), writes=['ckt'], semkey='sb')
                P.dma('sp', [(Ss[:, b * 4:(b + 1) * 4, :], stt[b].rearrange("h k v -> k h v")) for b in range(8)], writes=[('Ss', b) for b in range(8)], semkey='sc')
                P.dma('sp', [(nks[:, 0:127, :], ck[:, 1:128, :]), (nvs[:, 0:127, :], cv[:, 1:128, :])], semkey='sd', final=True)
                for tns, nm in ((ZQ, 'ZQ'), (QB, 'QB'), (ZQa, 'ZQa'), (Apad, 'Apad')):
                    P.op('pool', lambda e, tns=tns: e.memset(tns[:], 0.0), writes=[nm])

                P.op('pe', [lambda e, kc=kc: e.transpose(out=pT[:, kc * NB:(kc + 1) * NB], in_=xs_t[:, kc * 128:(kc + 1) * 128], identity=idf[0:NB, 0:NB]) for kc in range(8)],
                     reads=['xs_t', 'idf'], writes=['pT'])
                P.op('dve', lambda e: e.tensor_tensor(out=tmpS[:].rearrange("p (a b) -> p a b", a=8), in0=pT[:, 0:128].rearrange("p (a b) -> p a b", a=8), in1=sc1[:, :, 1:17], op=ALU.mult),
                     reads=['pT', 'sc1'], writes=['tmpS'])
                P.op('dve', lambda e: e.tensor_tensor(out=hsT[:], in0=tmpS[:].rearrange("p (a b) -> p a b", a=8), in1=mT1[:, 0:8, 1:17], op=ALU.add), reads=['tmpS', 'mT1'], writes=['hsT'])
                for ci in range(6):
                    n0 = ci * 512
                    nw = 512 if ci < 5 else 256
                    P.op('pe', [lambda e, kc=kc, ci=ci, n0=n0, nw=nw: e.matmul(pZ[ci % 2][0:NB, 0:nw], lhsT=hsT[:, kc, :], rhs=win[:, kc, n0:n0 + nw], start=(kc == 0), stop=(kc == 7))
                                for kc in range(8)], reads=['hsT', ('win', ci)], writes=[('pZ', ci % 2)])
                    P.op('act', lambda e, ci=ci, n0=n0, nw=nw: e.activation(out=zs[:, n0:n0 + nw], in_=pZ[ci % 2][0:NB, 0:nw], func=AF.Copy), reads=[('pZ', ci % 2)], writes=['zs'])
                P.dma('pool', [(wdn[:, j * 6:min(22, (j + 1) * 6), :], w_down[j * 768:min(DFF, (j + 1) * 768), :].rearrange("(c p) n -> p c n", p=128)) for j in range(4)],
                  writes=[('win', j) for j in range(6)] + ['wdn'], semkey='wdn', not_before=128e3)
                P.op('act', lambda e: e.activation(out=osq_s[:], in_=zs[:, 2048:2560], func=AF.Exp, scale=-1.0), reads=['zs'], writes=['osq'])
                P.op('act', lambda e: e.activation(out=osq_s[:], in_=osq_s[:], func=AF.Ln, bias=1.0, scale=1.0), reads=['osq'], writes=['osq'])
                P.op('act', lambda e: e.activation(out=osq_s[:], in_=osq_s[:], func=AF.Exp, scale=-1.0), reads=['osq'], writes=['osq'])
                P.op('dve', lambda e: e.tensor_tensor(out=sg_s[:], in0=zs[:, 2048:2560], in1=osq_s[:], op=ALU.mult), reads=['zs', 'osq'], writes=['sg_s'])

                def rope_s(c0, H):
                    v = zs[:, c0:c0 + H * 64].rearrange("p (h d) -> p h d", h=H)
                    vr = v[:, :, 0:16].rearrange("p h (two j) -> p h two j", two=2)
                    tav = tas[:, 0:H, :].rearrange("p h (two j) -> p h two j", two=2)
                    tbv = tbs[:, 0:H, :].rearrange("p h (two j) -> p h two j", two=2)
                    cb = tabS[:, 128:136].unsqueeze(1).unsqueeze(1).to_broadcast([NB, H, 2, 8])
                    sb_ = tabS[:, 136:144].unsqueeze(1).unsqueeze(1).to_broadcast([NB, H, 2, 8])
                    P.op('dve', lambda e: e.tensor_tensor(out=tav, in0=vr, in1=cb, op=ALU.mult), reads=['zs', 'tabS'], writes=['tas'])
                    P.op('dve', lambda e: e.tensor_tensor(out=tbv, in0=vr, in1=sb_, op=ALU.mult), reads=['zs', 'tabS'], writes=['tbs'])
                    P.op('dve', lambda e: e.tensor_tensor(out=v[:, :, 0:8], in0=tav[:, :, 0, :], in1=tbv[:, :, 1, :], op=ALU.subtract), reads=['tas', 'tbs'], writes=['zs'])
                    P.op('dve', lambda e: e.tensor_tensor(out=v[:, :, 8:16], in0=tav[:, :, 1, :], in1=tbv[:, :, 0, :], op=ALU.add), reads=['tas', 'tbs'], writes=['zs'])
                rope_s(0, 8)
                rope_s(2560, 2)
                for (c0, dst, dnm) in ((512, rqs, 'rqs'), (1024, rks, 'rks')):
                    zv = zs[:, c0:c0 + 512].rearrange("p (h two j) -> p h two j", h=4, two=2)
                    cb = tabS[:, 0:64].unsqueeze(1).unsqueeze(1).to_broadcast([NB, 4, 2, 64])
                    sb_ = tabS[:, 64:128].unsqueeze(1).unsqueeze(1).to_broadcast([NB, 4, 2, 64])
                    Av = tAs[:].rearrange("p (h two j) -> p h two j", h=4, two=2)
                    Bv = tBs[:].rearrange("p (h two j) -> p h two j", h=4, two=2)
                    dv = dst[:].rearrange("p (h two j) -> p h two j", h=4, two=2)
                    P.op('dve', lambda e, zv=zv, cb=cb, Av=Av: e.tensor_tensor(out=Av, in0=zv, in1=cb, op=ALU.mult), reads=['zs', 'tabS'], writes=['tAs'])
                    P.op('dve', lambda e, zv=zv, sb_=sb_, Bv=Bv: e.tensor_tensor(out=Bv, in0=zv, in1=sb_, op=ALU.mult), reads=['zs', 'tabS'], writes=['tBs'])
                    P.op('dve', lambda e, dv=dv, Av=Av, Bv=Bv: e.tensor_tensor(out=dv[:, :, 0, :], in0=Av[:, :, 0, :], in1=Bv[:, :, 1, :], op=ALU.subtract), reads=['tAs', 'tBs'], writes=[dnm])
                    P.op('dve', lambda e, dv=dv, Av=Av, Bv=Bv: e.tensor_tensor(out=dv[:, :, 1, :], in0=Av[:, :, 1, :], in1=Bv[:, :, 0, :], op=ALU.add), reads=['tAs', 'tBs'], writes=[dnm])
                P.op('dve', lambda e: e.tensor_scalar(out=rks[:], in0=rks[:], scalar1=128.0 ** -0.5, scalar2=None, op0=ALU.mult), reads=['rks'], writes=['rks'])
                P.dma('sp', [(nks[:, 127, :], zs[:, 2560:2688]), (nvs[:, 127, :], zs[:, 2688:2816])], reads=['zs'], semkey='se', final=True)

                rvs = zs[:, 1536:2048]
                P.op('pe', [lambda e, h=h: e.transpose(out=pT[:, h * NB:(h + 1) * NB], in_=rqs[:, h * 128:(h + 1) * 128], identity=idf[0:NB, 0:NB]) for h in range(4)],
                     reads=['rqs', 'idf'], writes=['pT'])
                P.op('act', lambda e: e.activation(out=rqT_s[:].rearrange("p a b -> p (a b)"), in_=pT[:, 0:4 * NB], func=AF.Copy), reads=['pT'], writes=['rqT_s'])
                P.op('dve', lambda e: e.tensor_copy(out=ZQ[:, :, 0:NB * NB:NB + 1], in_=rqT_s[:]), reads=['rqT_s', 'ZQ'], writes=['ZQ'])
                for half in range(2):
                    if half == 1:
                        P.dma('sp', [(Ss[:, (b % 8) * 4:(b % 8 + 1) * 4, :], stt[b].rearrange("h k v -> k h v")) for b in range(8, NB)],
                              reads=[('nrsd', b) for b in range(8)], writes=[('Ss', b) for b in range(8)], semkey='sc')
                    fns = []
                    for h in range(4):
                        for b in range(half * 8, half * 8 + 8):
                            fns.append(lambda e, h=h, b=b: e.matmul(pO[0:NB, h * 128:(h + 1) * 128], lhsT=ZQ[:, h, b * NB:(b + 1) * NB], rhs=Ss[:, (b % 8) * 4 + h, :],
                                                                   start=(h == 0 and b == 0), stop=(h == 3 and b == NB - 1), skip_group_check=True))
                    P.op('pe', fns, reads=['ZQ'] + [('Ss', b) for b in range(8)], writes=['pO'])
                    for b in range(half * 8, half * 8 + 8):
                        P.op('dve', lambda e, b=b: e.tensor_scalar(out=KM[:, b % 4, :], in0=rks[:], scalar1=idf[0:NB, b:b + 1], scalar2=None, op0=ALU.mult), reads=['rks', 'idf'], writes=[('KM', b % 4)])
                        pkv = pK if b % 2 == 0 else pS[1]
                        pkey = 'pK' if b % 2 == 0 else ('pS', 1)
                        P.op('pe', [lambda e, h=h, b=b, pkv=pkv: e.matmul(pkv[:, h * 128:(h + 1) * 128], lhsT=KM[:, b % 4, h * 128:(h + 1) * 128], rhs=rvs[:, h * 128:(h + 1) * 128], start=True, stop=True) for h in range(4)],
                             reads=[('KM', b % 4), 'zs'], writes=[pkey])
                        sv = Ss[:, (b % 8) * 4:(b % 8 + 1) * 4, :].rearrange("p a b -> p (a b)")
                        P.op('dve', lambda e, sv=sv: e.tensor_tensor(out=sv, in0=sv, in1=G1t[:], op=ALU.mult), reads=[('Ss', b % 8), 'G1t'], writes=[('Ss', b % 8)])
                        P.op('dve', lambda e, sv=sv, pkv=pkv: e.tensor_tensor(out=sv, in0=sv, in1=pkv[:], op=ALU.add), reads=[('Ss', b % 8), pkey], writes=[('Ss', b % 8)])
                        P.dma('sp', (nrs[b].rearrange("h k v -> k h v"), Ss[:, (b % 8) * 4:(b % 8 + 1) * 4, :]), reads=[('Ss', b % 8)], writes=[('nrsd', b)], semkey='nrs%d' % (b % 8), final=True, slack=0.0)
                P.op('dve', lambda e: e.tensor_tensor(out=tAs[:], in0=rqs[:], in1=rks[:], op=ALU.mult), reads=['rqs', 'rks'], writes=['tAs'])
                P.op('dve', lambda e: e.tensor_reduce(out=qk_s[:], in_=tAs[:].rearrange("p (a b) -> p a b", a=4), axis=AX.X, op=ALU.add), reads=['tAs'], writes=['qk_s'])
                P.op('dve', lambda e: e.tensor_tensor(out=tBs[:].rearrange("p (a b) -> p a b", a=4), in0=rvs.rearrange("p (a b) -> p a b", a=4),
                                                      in1=qk_s[:].unsqueeze(2).to_broadcast([NB, 4, 128]), op=ALU.mult), reads=['zs', 'qk_s'], writes=['tBs'])
                for h in range(4):
                    P.op('dve', lambda e, h=h: e.scalar_tensor_tensor(out=osb_s[:, h * 128:(h + 1) * 128], in0=pO[0:NB, h * 128:(h + 1) * 128], scalar=gam[h], in1=tBs[:, h * 128:(h + 1) * 128],
                                                                     op0=ALU.mult, op1=ALU.add), reads=['pO', 'tBs'], writes=['osb'])
                gn_tail(osb_s, sg_s, 'sg_s', rets, 'rets', NB, osq_=osq_s, onr_=onr_s, st4_=st4_s)
                qv4 = zs[:, 0:512].rearrange("p (hg g d) -> p hg g d", hg=4, g=2)
                QB4 = QB[:].rearrange("p (hg g) c -> p hg g c", g=2)
                P.op('dve', lambda e: e.tensor_copy(out=QB4[:, :, 0, 0:64], in_=qv4[:, :, 0, :]), reads=['zs', 'QB'], writes=['QB'])
                P.op('dve', lambda e: e.tensor_copy(out=QB4[:, :, 1, 64:128], in_=qv4[:, :, 1, :]), reads=['zs', 'QB'], writes=['QB'])
                P.op('pe', [lambda e, hh=hh: e.transpose(out=pT[:, hh * NB:(hh + 1) * NB], in_=QB[:, hh, :], identity=idf[0:NB, 0:NB]) for hh in range(8)], reads=['QB', 'idf'], writes=['pT'])
                P.op('act', lambda e: e.activation(out=QT[:], in_=pT[:, 0:128], func=AF.Copy), reads=['pT'], writes=['QT'])
                ZQa_v = ZQa[:].rearrange("p (b c) -> p b c", c=129)[:, :, 0:128:16]
                P.op('dve', lambda e: e.tensor_copy(out=ZQa_v, in_=QT[:].rearrange("p (hh b) -> p b hh", b=NB)), reads=['QT', 'ZQa'], writes=['ZQa'])
                for q4 in range(4):
                    pb = pS[q4 % 2]
                    P.op('pe', [lambda e, j=j, q4=q4, pb=pb: e.transpose(out=pb[:, j * 128:(j + 1) * 128], in_=ckt[:, q4 * 4 + j, :], identity=idf[:]) for j in range(4)],
                         reads=['ckt', 'idf'], writes=[('pS', q4 % 2)])
                    P.op('act', lambda e, q4=q4, pb=pb: e.activation(out=KT[:, q4 * 4:(q4 + 1) * 4, 0:128], in_=pb[:].rearrange("p (a b) -> p a b", a=4), func=AF.Copy),
                         reads=[('pS', q4 % 2)], writes=['KT'])
                P.dma('sp', (cvt[:], cv.rearrange("b j c -> j b c")), writes=['ckt'], semkey='sb')
                P.op('pe', lambda e: e.transpose(out=pT[:, 0:NB], in_=zs[:, 2560:2688], identity=idf[0:NB, 0:NB]), reads=['zs', 'idf'], writes=['pT'])
                P.op('act', lambda e: e.activation(out=KT[:, :, 128], in_=pT[:, 0:NB], func=AF.Copy), reads=['pT'], writes=['KT'])
                P.op('pe', [lambda e, b=b: e.matmul(pO[:, 0:129], lhsT=ZQa[:, b * 128:(b + 1) * 128], rhs=KT[:, b, :], start=(b == 0), stop=(b == NB - 1)) for b in range(NB)],
                     reads=['ZQa', 'KT'], writes=['pO'])
                P.op('dve', lambda e: e.tensor_reduce(out=sm[:, 0:1], in_=pO[:, 0:129], axis=AX.X, op=ALU.max), reads=['pO'], writes=['sm'])
                P.op('dve', lambda e: e.tensor_scalar(out=sm[:, 0:1], in0=sm[:, 0:1], scalar1=0.125, scalar2=None, op0=ALU.mult), reads=['sm'], writes=['sm'])
                P.op('dve', lambda e: e.tensor_tensor(out=sm[:, 0:1], in0=sm[:, 0:1], in1=sinkrow[:], op=ALU.max), reads=['sm', 'sinkrow'], writes=['sm'])
                P.op('dve', lambda e: e.tensor_scalar(out=sm[:, 1:2], in0=sm[:, 0:1], scalar1=-1.0, scalar2=None, op0=ALU.mult), reads=['sm'], writes=['sm'])
                P.op('act', lambda e: e.activation(out=Ps[:], in_=pO[:, 0:129], func=AF.Exp, scale=0.125, bias=sm[:, 1:2], accum_out=sm[:, 2:3]), reads=['pO', 'sm'], writes=['Ps', 'sm'])
                P.op('act', lambda e: e.activation(out=sm[:, 3:4], in_=sinkrow[:], func=AF.Exp, scale=1.0, bias=sm[:, 1:2]), reads=['sinkrow', 'sm'], writes=['sm'])
                P.op('dve', lambda e: e.tensor_tensor(out=sm[:, 4:5], in0=sm[:, 2:3], in1=sm[:, 3:4], op=ALU.add), reads=['sm'], writes=['sm'])
                P.op('dve', lambda e: e.reciprocal(out=sm[:, 5:6], in_=sm[:, 4:5]), reads=['sm'], writes=['sm'])
                P.op('dve', lambda e: e.tensor_scalar(out=sm[:, 6:8], in0=selg[:], scalar1=sm[:, 5:6], scalar2=None, op0=ALU.mult), reads=['sm', 'selg'], writes=['sm'])
                P.op('pe', lambda e: e.transpose(out=pK[:, 0:128], in_=Ps[:, 0:128], identity=idf[:]), reads=['Ps', 'idf'], writes=['pK'])
                P.op('act', lambda e: e.activation(out=PTs[:], in_=pK[:, 0:128], func=AF.Copy), reads=['pK'], writes=['PTs'])
                ZP_v = ZP[:].rearrange("p (b c) -> p b c", c=129)[:, :, 0:128:16]
                P.op('dve', lambda e: e.tensor_copy(out=ZP_v, in_=PTs[:].rearrange("p (hh b) -> p b hh", b=NB)), reads=['PTs', 'ZQa'], writes=['ZQa'])
                P.op('dve', lambda e: e.tensor_scalar(out=PNT[:], in0=selb[:], scalar1=Ps[:, 128:129], scalar2=None, op0=ALU.mult), reads=['Ps', 'selb'], writes=['PNT'])
                P.op('pe', lambda e: e.transpose(out=pK[0:NB, 0:128], in_=PNT[:], identity=idf[:]), reads=['PNT', 'idf'], writes=['pK'])
                P.op('act', lambda e: e.activation(out=PN[:], in_=pK[0:NB, 0:128], func=AF.Copy), reads=['pK'], writes=['PN'])
                P.op('pe', [lambda e, b=b: e.matmul(pS[0][:, 0:128], lhsT=ZP[:, b * 128:(b + 1) * 128], rhs=cvt[:, b, :], start=(b == 0), stop=False) for b in range(NB)] +
                           [lambda e: e.matmul(pS[0][:, 0:128], lhsT=PN[:], rhs=zs[:, 2688:2816], start=False, stop=True)],
                     reads=['ZQa', 'ckt', 'PN', 'zs'], writes=[('pS', 0)])
                P.op('dve', lambda e: e.tensor_scalar(out=Apad[:, 0:64], in0=pS[0][:, 0:64], scalar1=sm[:, 6:7], scalar2=None, op0=ALU.mult), reads=[('pS', 0), 'sm', 'Apad'], writes=['Apad'])
                P.op('dve', lambda e: e.tensor_scalar(out=Apad[:, 64:128], in0=pS[0][:, 64:128], scalar1=sm[:, 7:8], scalar2=None, op0=ALU.mult), reads=[('pS', 0), 'sm', 'Apad'], writes=['Apad'])
                P.op('pe', lambda e: e.transpose(out=pK[:, 0:128], in_=Apad[:], identity=idf[:]), reads=['Apad', 'idf'], writes=['pK'])
                Tv = pK[:, 0:128].rearrange("p (hg g b) -> p hg g b", hg=4, g=2)
                P.op('act', lambda e: e.activation(out=tmpS[:, 0:64].rearrange("p (a b) -> p a b", a=4), in_=Tv[:, :, 0, :], func=AF.Copy), reads=['pK'], writes=['tmpS'])
                P.op('dve', lambda e: e.tensor_tensor(out=attT_s[:], in0=tmpS[:, 0:64].rearrange("p (a b) -> p a b", a=4), in1=Tv[:, :, 1, :], op=ALU.add), reads=['pK', 'tmpS'], writes=['attT_s'])
                P.op('pe', [lambda e, h=h: e.transpose(out=pT[:, h * NB:(h + 1) * NB], in_=rets[:, h * 128:(h + 1) * 128], identity=idf[0:NB, 0:NB]) for h in range(4)],
                     reads=['rets', 'idf'], writes=['pT'])
                P.op('act', lambda e: e.activation(out=retT_s[:].rearrange("p a b -> p (a b)"), in_=pT[:, 0:4 * NB], func=AF.Copy), reads=['pT'], writes=['retT_s'])
                for nn in range(2):
                    P.op('pe', [lambda e, hg=hg, nn=nn: e.matmul(pZ[nn][0:NB, :], lhsT=attT_s[:, hg, :], rhs=WoA[:, hg, nn * 512:(nn + 1) * 512], start=(hg == 0), stop=False) for hg in range(4)] +
                               [lambda e, h=h, nn=nn: e.matmul(pZ[nn][0:NB, :], lhsT=retT_s[:, h, :], rhs=WoR[:, h, nn * 512:(nn + 1) * 512], start=False, stop=(h == 3)) for h in range(4)],
                         reads=['attT_s', 'retT_s', 'WoA', 'WoR'], writes=[('pZ', nn)])
                    P.op('dve', lambda e, nn=nn: e.tensor_tensor(out=tm_s[:, nn * 512:(nn + 1) * 512], in0=pZ[nn][0:NB, :], in1=g1s[:, nn * 512:(nn + 1) * 512], op=ALU.mult), reads=[('pZ', nn), 'g1s'], writes=[('KM', 0), ('KM', 1)])
                P.op('dve', lambda e: e.scalar_tensor_tensor(out=rr_s, in0=xs_t[:], scalar=ALPHA, in1=tm_s, op0=ALU.mult, op1=ALU.add), reads=['xs_t', ('KM', 0), ('KM', 1)], writes=[('KM', 2), ('KM', 3)])
                ln_tail(rr_s, D, st2_s, ln1wb[0:NB, :], ln1bb[0:NB, :], y1s[:], [('KM', 2), ('KM', 3)], 'y1s', npart=NB, jk_=jk_s)
              P.barrier()

        s1o.close()
        with contextlib.ExitStack() as s2c:
            def sb2(name, shape, dt=F32):
                return s2c.enter_context(nc.sbuf_tensor("s_" + name, shape, dt))
            NR = 4
            wring = [sb2("wring%d" % i, [128, 8, 256], BF16) for i in range(NR)]
            xT2 = [sb2("xT2%d" % i, [128, 8, 512], BF16) for i in range(2)]
            yB = [sb2("yB%d" % i, [128, D]) for i in range(2)]
            h2T = [sb2("h2T%d" % i, [128, 8, 512], BF16) for i in range(2)]
            hidT = [sb2("hidT%d" % i, [128, 22, 512], BF16) for i in range(2)]
            sgt = [sb2("sgt%d" % i, [128, 512]) for i in range(2)]
            tm2 = sb2("tm2", [128, D]); rr2 = sb2("rr2", [128, D]); jk2 = sb2("jk2", [128, D], BF16)
            yo = [sb2("yo%d" % i, [128, D]) for i in range(2)]
            st5 = sb2("st5", [128, 8])
            ln2wb = sb2("ln2wb", [128, D]); ln2bb = sb2("ln2bb", [128, D]); g2 = sb2("g2", [128, D]); g2s = sb2("g2s", [NB, D])
            dg2 = sb2("dg2", [128, 8, 128]); onesf2 = sb2("onesf2", [128, 128])
            P.dma('sp', [(ln2wb[:], ln2w.partition_broadcast(128)), (ln2bb[:], ln2b.partition_broadcast(128))], writes=['ln2wb', 'ln2bb'], semkey='c3')
            P.op('pool', lambda e: e.memset(onesf2[:], 1.0), writes=['onesf'])
            make_gate(gp2, 'gp2', g2, 'g2', g2s, 'g2s', dg2, onesf2)
            pG = [pS[0], pS[1]]
            pU = [pO, pK]

            def ln2_tail(rr_ap, stt, out_ap, keys_r, key_out, npart=128):
                n = D
                pp = slice(0, npart)
                P.op('act', lambda e: e.activation(out=jk2[pp, 0:n], in_=rr_ap, func=AF.Identity, accum_out=stt[pp, 0:1]), reads=keys_r, writes=['jk2', 'st5'])
                P.op('act', lambda e: e.activation(out=jk2[pp, 0:n], in_=rr_ap, func=AF.Square, accum_out=stt[pp, 1:2]), reads=keys_r, writes=['jk2', 'st5'])
                P.op('dve', lambda e: e.tensor_scalar(out=stt[pp, 2:3], in0=stt[pp, 0:1], scalar1=1.0 / n, scalar2=None, op0=ALU.mult), reads=['st5'], writes=['st5'])
                P.op('dve', lambda e: e.tensor_tensor(out=stt[pp, 3:4], in0=stt[pp, 2:3], in1=stt[pp, 2:3], op=ALU.mult), reads=['st5'], writes=['st5'])
                P.op('dve', lambda e: e.scalar_tensor_tensor(out=stt[pp, 4:5], in0=stt[pp, 1:2], scalar=1.0 / n, in1=stt[pp, 3:4], op0=ALU.mult, op1=ALU.subtract), reads=['st5'], writes=['st5'])
                P.op('act', lambda e: e.activation(out=stt[pp, 5:6], in_=stt[pp, 4:5], func=AF.Ln, bias=epsl[pp, 0:1], scale=1.0), reads=['st5', 'epsl'], writes=['st5'])
                P.op('act', lambda e: e.activation(out=stt[pp, 6:7], in_=stt[pp, 5:6], func=AF.Exp, scale=-0.5), reads=['st5'], writes=['st5'])
                P.op('dve', lambda e: e.scalar_tensor_tensor(out=stt[pp, 7:8], in0=stt[pp, 2:3], scalar=-1.0, in1=stt[pp, 6:7], op0=ALU.mult, op1=ALU.mult), reads=['st5'], writes=['st5'])
                P.op('act', lambda e: e.activation(out=rr_ap, in_=rr_ap, func=AF.Identity, scale=stt[pp, 6:7], bias=stt[pp, 7:8]), reads=keys_r + ['st5'], writes=keys_r)
                P.op('dve', lambda e: e.tensor_tensor(out=rr_ap, in0=rr_ap, in1=ln2wb[pp, :], op=ALU.mult), reads=keys_r + ['ln2wb'], writes=keys_r)
                P.op('dve', lambda e: e.tensor_tensor(out=out_ap, in0=rr_ap, in1=ln2bb[pp, :], op=ALU.add), reads=keys_r + ['ln2bb'], writes=[key_out])

            wcount = [0]

            yos = sb2("yos", [NB, D]); h2Ts = sb2("h2Ts", [128, 8, NB], BF16); hidTs = sb2("hidTs", [128, 22, NB], BF16); sgts = sb2("sgts", [128, 2, NB])

            def ffn_group(gi, ntok, tiles, sample=False, ride=False):
                hs = gi % 2
                if ride:
                    P.op('pe', [lambda e, kc=kc: e.transpose(out=pT[:, kc * NB:(kc + 1) * NB], in_=y1s[:, kc * 128:(kc + 1) * 128], identity=idf[0:NB, 0:NB]) for kc in range(8)],
                         reads=['y1s', 'idf'], writes=['pT'])
                    P.op('dve', lambda e: e.tensor_tensor(out=tm2[:, 0:128].rearrange("p (a b) -> p a b", a=8), in0=pT[:, 0:128].rearrange("p (a b) -> p a b", a=8), in1=sc2[:, :, 1:17], op=ALU.mult),
                         reads=['pT', 'sc2'], writes=['tm2'])
                    P.op('dve', lambda e: e.tensor_tensor(out=h2Ts[:], in0=tm2[:, 0:128].rearrange("p (a b) -> p a b", a=8), in1=mT2[:, 0:8, 1:17], op=ALU.add), reads=['tm2', 'mT2'], writes=['h2Ts'])
                if not sample:
                    g0 = tiles[0] * 128
                    P.dma('sp', [(xT2[hs][:, kc, :], y1b[g0:g0 + 512, kc * 128:(kc + 1) * 128]) for kc in range(8)],
                          reads=[('y1b', t) for t in tiles], writes=[('xT2', hs)], semkey='xT2%d' % hs, transpose=True)
                    for kc in range(8):
                        if kc % 2 == 0:
                            P.op('dve', lambda e, kc=kc: e.tensor_scalar(out=h2T[hs][:, kc, :], in0=xT2[hs][:, kc, :], scalar1=sc2[:, kc, 0:1], scalar2=mT2[:, kc, 0:1],
                                                                       op0=ALU.mult, op1=ALU.add), reads=[('xT2', hs), 'sc2', 'mT2'], writes=[('h2T', hs)])
                        else:
                            P.op('act', lambda e, kc=kc: e.activation(out=h2T[hs][:, kc, :], in_=xT2[hs][:, kc, :], func=AF.Identity, scale=sc2[:, kc, 0:1], bias=mT2[:, kc, 0:1]),
                                 reads=[('xT2', hs), 'sc2', 'mT2'], writes=[('h2T', hs)])
                else:
                    P.op('pe', [lambda e, kc=kc: e.transpose(out=pT[:, kc * NB:(kc + 1) * NB], in_=y1s[:, kc * 128:(kc + 1) * 128], identity=idf[0:NB, 0:NB]) for kc in range(8)],
                         reads=['y1s', 'idf'], writes=['pT'])
                    hv = h2T[hs][:, :, 0:NB]
                    P.op('dve', lambda e: e.tensor_tensor(out=tm2[:, 0:128].rearrange("p (a b) -> p a b", a=8), in0=pT[:, 0:128].rearrange("p (a b) -> p a b", a=8), in1=sc2[:, :, 1:17], op=ALU.mult),
                         reads=['pT', 'sc2'], writes=['tm2'])
                    P.op('dve', lambda e: e.tensor_tensor(out=hv, in0=tm2[:, 0:128].rearrange("p (a b) -> p a b", a=8), in1=mT2[:, 0:8, 1:17], op=ALU.add), reads=['tm2', 'mT2'], writes=[('h2T', hs)])
                for c in range(22):
                    w = wcount[0] % NR
                    wcount[0] += 1
                    P.dma('sp', (wring[w][:], wus[c]), reads=[('wus', c)], writes=[('wring', w)], semkey='wr%d' % w)
                    bi = c % 2
                    P.op('pe', [lambda e, kc=kc, w=w, bi=bi: e.matmul(pG[bi][:, 0:ntok], lhsT=wring[w][:, kc, 0:128], rhs=h2T[hs][:, kc, 0:ntok], start=(kc == 0), stop=(kc == 7)) for kc in range(8)],
                         reads=[('wring', w), ('h2T', hs)], writes=[('pG', bi)])
                    P.op('pe', [lambda e, kc=kc, w=w, bi=bi: e.matmul(pU[bi][:, 0:ntok], lhsT=wring[w][:, kc, 128:256], rhs=h2T[hs][:, kc, 0:ntok], start=(kc == 0), stop=(kc == 7)) for kc in range(8)],
                         reads=[('wring', w), ('h2T', hs)], writes=[('pU', bi)])
                    P.op('act', lambda e, bi=bi: e.activation(out=sgt[bi][:, 0:ntok], in_=pG[bi][:, 0:ntok], func=AF.Exp, scale=-1.0), reads=[('pG', bi)], writes=[('sgt', bi)])
                    P.op('act', lambda e, bi=bi: e.activation(out=sgt[bi][:, 0:ntok], in_=sgt[bi][:, 0:ntok], func=AF.Ln, bias=1.0, scale=1.0), reads=[('sgt', bi)], writes=[('sgt', bi)])
                    P.op('act', lambda e, bi=bi: e.activation(out=sgt[bi][:, 0:ntok], in_=sgt[bi][:, 0:ntok], func=AF.Exp, scale=-1.0), reads=[('sgt', bi)], writes=[('sgt', bi)])
                    P.op('dve', lambda e, bi=bi: e.tensor_tensor(out=sgt[bi][:, 0:ntok], in0=pG[bi][:, 0:ntok], in1=sgt[bi][:, 0:ntok], op=ALU.mult), reads=[('pG', bi), ('sgt', bi)], writes=[('sgt', bi)])
                    P.op('dve', lambda e, bi=bi, c=c: e.tensor_tensor(out=hidT[hs][:, c, 0:ntok], in0=pU[bi][:, 0:ntok], in1=sgt[bi][:, 0:ntok], op=ALU.mult),
                         reads=[('pU', bi), ('sgt', bi)], writes=[('hidT', hs, c)])
                    if ride:
                        o = (c % 2) * 2 * NB
                        P.op('pe', [lambda e, kc=kc, w=w, o=o: e.matmul(pT[:, o:o + NB], lhsT=wring[w][:, kc, 0:128], rhs=h2Ts[:, kc, :], start=(kc == 0), stop=(kc == 7)) for kc in range(8)] +
                                   [lambda e, kc=kc, w=w, o=o: e.matmul(pT[:, o + NB:o + 2 * NB], lhsT=wring[w][:, kc, 128:256], rhs=h2Ts[:, kc, :], start=(kc == 0), stop=(kc == 7)) for kc in range(8)],
                             reads=[('wring', w), 'h2Ts'], writes=['pT'])
                        sv_ = sgts[:, bi, :]
                        P.op('act', lambda e, o=o, sv_=sv_: e.activation(out=sv_, in_=pT[:, o:o + NB], func=AF.Exp, scale=-1.0), reads=['pT'], writes=[('sgts', bi)])
                        P.op('act', lambda e, sv_=sv_: e.activation(out=sv_, in_=sv_, func=AF.Ln, bias=1.0, scale=1.0), reads=[('sgts', bi)], writes=[('sgts', bi)])
                        P.op('act', lambda e, sv_=sv_: e.activation(out=sv_, in_=sv_, func=AF.Exp, scale=-1.0), reads=[('sgts', bi)], writes=[('sgts', bi)])
                        P.op('dve', lambda e, o=o, sv_=sv_: e.tensor_tensor(out=sv_, in0=pT[:, o:o + NB], in1=sv_, op=ALU.mult), reads=['pT', ('sgts', bi)], writes=[('sgts', bi)])
                        P.op('dve', lambda e, o=o, sv_=sv_, c=c: e.tensor_tensor(out=hidTs[:, c, :], in0=pT[:, o + NB:o + 2 * NB], in1=sv_, op=ALU.mult), reads=['pT', ('sgts', bi)], writes=[('hidTs', c)])
                ntl = 1 if sample else 4
                for j in range(ntl):
                    mp = NB if sample else 128
                    for nn in range(2):
                        P.op('pe', [lambda e, c=c, nn=nn, j=j, mp=mp: e.matmul(pZ[nn][0:mp, :], lhsT=hidT[hs][:, c, j * 128:j * 128 + mp], rhs=wdn[:, c, nn * 512:(nn + 1) * 512], start=(c == 0), stop=(c == 21))
                                    for c in range(22)], reads=[('hidT', hs, c) for c in range(22)] + ['wdn'], writes=[('pZ', nn)])
                    if not sample:
                        t = tiles[j]
                        sl = (gi * 4 + j) % 2
                        P.dma('sp', (yB[sl][:], y1d[t * 128:(t + 1) * 128, :]), reads=[('y1d', t)], writes=[('yB', sl)], semkey='yB%d' % sl)
                        for nn in range(2):
                            P.op('dve', lambda e, nn=nn: e.tensor_tensor(out=tm2[:, nn * 512:(nn + 1) * 512], in0=pZ[nn][:], in1=g2[:, nn * 512:(nn + 1) * 512], op=ALU.mult), reads=[('pZ', nn), 'g2'], writes=['tm2'])
                        P.op('dve', lambda e, sl=sl: e.scalar_tensor_tensor(out=rr2[:], in0=yB[sl][:], scalar=ALPHA, in1=tm2[:], op0=ALU.mult, op1=ALU.add), reads=[('yB', sl), 'tm2'], writes=['rr2'])
                        ln2_tail(rr2[:], st5, yo[sl][:], ['rr2'], ('yo', sl))
                        P.dma('sp', (yp[t * 128:(t + 1) * 128, :], yo[sl][:]), reads=[('yo', sl)], semkey='yo%d' % sl, final=True)
                    else:
                        for nn in range(2):
                            P.op('dve', lambda e, nn=nn: e.tensor_tensor(out=tm2[0:NB, nn * 512:(nn + 1) * 512], in0=pZ[nn][0:NB, :], in1=g2s[:, nn * 512:(nn + 1) * 512], op=ALU.mult),
                                 reads=[('pZ', nn), 'g2s'], writes=['tm2'])
                        P.op('dve', lambda e: e.scalar_tensor_tensor(out=rr2[0:NB, :], in0=y1s[:], scalar=ALPHA, in1=tm2[0:NB, :], op0=ALU.mult, op1=ALU.add), reads=['y1s', 'tm2'], writes=['rr2'])
                        ln2_tail(rr2[0:NB, :], st5, yo[0][0:NB, :], ['rr2'], ('yo', 0), npart=NB)
                        P.dma('sp', (ys, yo[0][0:NB, :]), reads=[('yo', 0)], semkey='yo0', final=True)
                if ride:
                    for nn in range(2):
                        P.op('pe', [lambda e, c=c, nn=nn: e.matmul(pZ[nn][0:NB, :], lhsT=hidTs[:, c, :], rhs=wdn[:, c, nn * 512:(nn + 1) * 512], start=(c == 0), stop=(c == 21)) for c in range(22)],
                             reads=[('hidTs', c) for c in range(22)] + ['wdn'], writes=[('pZ', nn)])
                        P.op('dve', lambda e, nn=nn: e.tensor_tensor(out=tm2[0:NB, nn * 512:(nn + 1) * 512], in0=pZ[nn][0:NB, :], in1=g2s[:, nn * 512:(nn + 1) * 512], op=ALU.mult),
                             reads=[('pZ', nn), 'g2s'], writes=['tm2'])
                    P.op('dve', lambda e: e.scalar_tensor_tensor(out=rr2[0:NB, :], in0=y1s[:], scalar=ALPHA, in1=tm2[0:NB, :], op0=ALU.mult, op1=ALU.add), reads=['y1s', 'tm2'], writes=['rr2'])
                    ln2_tail(rr2[0:NB, :], st5, yos[:], ['rr2'], 'yos', npart=NB)
                    P.dma('sp', (ys, yos[:]), reads=['yos'], semkey='yos', final=True)

            for gi in range(DBG['ng2']):
                ffn_group(gi, 512, [gi * 4 + j for j in range(4)], ride=(SAMPLE and gi == 0))

        P.emit()
    return nc


SAMPLE = True
DBG = {'nt1': NT, 'ng2': NT // 4, 'ph0only': False}
_NC = None


def kernel(x_prompt, x_sample, c_prompt, c_sample, cache_k, cache_v, state_ret,
           w_ada_mix, b_ada_mix, w_in, att_sinks, ret_gn_w, w_out, ln1_w, ln1_b,
           w_ada_ffn, b_ada_ffn, w_up, w_down, ln2_w, ln2_b):
    global _NC
    f = lambda a: np.ascontiguousarray(np.asarray(a, dtype=np.float32))
    cst = _consts()
    qperm = np.arange(512).reshape(2, 4, 64).transpose(1, 0, 2).reshape(-1)
    perm = np.concatenate([qperm, np.arange(768, 2816), np.arange(512, 768)])
    shared = {
        'wam': f(w_ada_mix[0]), 'bam': f(np.asarray(b_ada_mix[0]).reshape(24, 128).T),
        'waf': f(w_ada_ffn[0]), 'baf': f(np.asarray(b_ada_ffn[0]).reshape(24, 128).T),
        'w_in': f(np.asarray(w_in[0])[:, perm]),
        'sinks': f(np.repeat(np.asarray(att_sinks[0]).reshape(2, 1, 4), 64, axis=1).reshape(128, 4)),
        'sinkr': f(np.asarray(att_sinks[0])[cst['hrow']].reshape(128, 1)),
        'gnw': f(np.asarray(ret_gn_w[0]).reshape(4, 128).T), 'w_out': f(w_out[0]),
        'ln1w': f(ln1_w[0]), 'ln1b': f(ln1_b[0]), 'ln2w': f(ln2_w[0]), 'ln2b': f(ln2_b[0]),
        'w_up': f(w_up[0]), 'w_down': f(w_down[0]),
        'ident': cst['ident'], 'tabs': cst['tabs'], 'ropeA': cst['ropeA'], 'tabS': cst['tabS'],
        'mcur': cst['mcur'], 'mprev': cst['mprev'], 'Gt': cst['Gt'], 'G1t': cst['G1t'], 'selb': cst['selb'], 'selg': cst['selg'],
    }
    xp_ = np.asarray(x_prompt, dtype=np.float32)
    xs_ = np.asarray(x_sample, dtype=np.float32)
    cp_ = np.asarray(c_prompt, dtype=np.float32)
    cs_ = np.asarray(c_sample, dtype=np.float32)
    in_maps = []
    for c in range(8):
        cv17 = np.concatenate([cp_[c:c + 1], cs_[c * NB:(c + 1) * NB]], axis=0)
        cTm = np.ascontiguousarray(cv17.T.reshape(8, 128, 17).transpose(1, 0, 2)).reshape(128, 8 * 17)
        m = dict(shared)
        m.update({
            'xp': f(xp_[c]), 'xs': f(xs_[c * NB:(c + 1) * NB, 0, :]), 'cT': f(cTm),
            'ck': f(np.asarray(cache_k[0, c * NB:(c + 1) * NB]).reshape(NB, 128, 128)),
            'cv': f(np.asarray(cache_v[0, c * NB:(c + 1) * NB]).reshape(NB, 128, 128)),
            'st': f(state_ret[0, c * NB:(c + 1) * NB]),
        })
        in_maps.append(m)
    if _NC is None:
        _NC = build_nc()
    res = run_bass_kernel_spmd(_NC, in_maps, core_ids=list(range(8)))
    R = res.results
    y_p = np.stack([R[c]['yp'].reshape(NT * 128, D) for c in range(8)], 0)
    y_s = np.concatenate([R[c]['ys'].reshape(NB, 1, D) for c in range(8)], 0)
    nkp = np.stack([R[c]['nkp'].reshape(128, 2, 64) for c in range(8)], 0)[None]
    nvp = np.stack([R[c]['nvp'].reshape(128, 2, 64) for c in range(8)], 0)[None]
    nrp = np.stack([R[c]['nrp'].reshape(4, 128, 128) for c in range(8)], 0)[None]
    nks = np.concatenate([R[c]['nks'].reshape(NB, 128, 2, 64) for c in range(8)], 0)[None]
    nvs = np.concatenate([R[c]['nvs'].reshape(NB, 128, 2, 64) for c in range(8)], 0)[None]
    nrs = np.concatenate([R[c]['nrs'].reshape(NB, 4, 128, 128) for c in range(8)], 0)[None]
    return (y_p.astype(np.float32), y_s.astype(np.float32), nkp.astype(np.float32), nvp.astype(np.float32),
            nrp.astype(np.float32), nks.astype(np.float32), nvs.astype(np.float32), nrs.astype(np.float32))
```

```python
import contextlib
import math
import numpy as np
import concourse.bass as bass
import concourse.mybir as mybir
from concourse.bass_utils import run_bass_kernel_spmd

F32 = mybir.dt.float32
BF16 = mybir.dt.bfloat16
ALU = mybir.AluOpType
AF = mybir.ActivationFunctionType
AX = mybir.AxisListType

NT = 32
D = 1024
DFF = 2816
ALPHA = 2.0 ** 0.25
LN_EPS = 1e-5
GN_EPS = 1e-6
NB = 16


class _Probe:
    def __init__(self):
        self.calls = []

    def __getattr__(self, name):
        def f(*a, **k):
            self.calls.append((name, a, k))
            return self
        return f


def _free_size(ap):
    n = 1
    for d in ap.shape[1:]:
        n *= int(d)
    return n


def _is_psum(ap):
    try:
        return type(ap.tensor).__name__.startswith('PSum')
    except Exception:
        return False


def _est_ns(eng, fns):
    tot = 0.0
    for f in fns:
        pr = _Probe()
        try:
            f(pr)
        except Exception:
            tot += 300.0
            continue
        for (name, a, k) in pr.calls:
            aps = [v for v in list(a) + list(k.values()) if hasattr(v, 'shape') and hasattr(v, 'tensor')]
            out = k.get('out', aps[0] if aps else None)
            if eng == 'pe':
                mv = k.get('rhs', k.get('in_', out))
                n = _free_size(mv) if name != 'transpose' else _free_size(out)
                tot += 45.0 if n <= 32 else 228.0
            else:
                fd = _free_size(out) if out is not None else 64
                ps = any(_is_psum(v) for v in aps)
                if eng == 'act':
                    tot += 210.0 + 0.8 * fd
                elif eng == 'dve':
                    if ps:
                        tot += 175.0 + 1.0 * fd
                    elif name == 'scalar_tensor_tensor':
                        tot += 230.0 + 1.0 * fd
                    else:
                        tot += 110.0 + 1.0 * fd
                else:
                    tot += 100.0 + fd * 2.35
    return tot


class Prog:
    PSUM_ROOTS = ('pT', 'pTb', 'pO', 'pK', 'pZ', 'pS', 'pG', 'pU', 'pA')
    DMA_BW = 230.0
    DMA_LAT = 2000.0
    STORE_SLACK = 12000.0
    SYNC_LAT = 400.0
    WINDOW = 0.0

    def __init__(self, nc):
        self.nc = nc
        self.names = ['pe', 'act', 'dve', 'pool', 'sp']
        self.ops = []
        self.epoch = 0
        self.prio = 0.0
        self.sched = True

    def op(self, eng, fn, reads=(), writes=(), dur=None):
        fns = fn if isinstance(fn, (list, tuple)) else [fn]
        writes = list(writes) + [k for k in reads if (k if isinstance(k, str) else k[0]) in self.PSUM_ROOTS and k not in writes]
        self.ops.append(dict(id=len(self.ops), eng=eng, fns=list(fns), reads=list(reads), writes=writes, dma=None,
                             epoch=self.epoch, prio=self.prio, dur=dur, final=False))

    def dma(self, eng, pairs, reads=(), writes=(), semkey=None, final=False, slow=False, transpose=False, slack=None, not_before=0.0):
        if not isinstance(pairs, list):
            pairs = [pairs]
        nbytes = 0
        for (o, a) in pairs:
            n = 1
            for d in o.shape:
                n *= int(d)
            nbytes += n * 4
        issue_ns = (900.0 if eng == 'pool' else 60.0) * len(pairs)
        if transpose:
            fns = [lambda e, o=o, a=a: e.dma_start_transpose(out=o, in_=a) for (o, a) in pairs]
            issue_ns = 1280.0 * len(pairs)
        else:
            fns = [lambda e, o=o, a=a: e.dma_start(out=o, in_=a, allow_slow_non_contiguous=slow) for (o, a) in pairs]
        is_store = not type(pairs[0][0].tensor).__name__.startswith('SB')
        self.ops.append(dict(id=len(self.ops), eng=eng, fns=fns, reads=list(reads), writes=list(writes), dma='D:' + str(semkey),
                             epoch=self.epoch, prio=self.prio, dur=None, final=final, nbytes=nbytes, issue_ns=issue_ns, slack=(slack if slack is not None else (self.STORE_SLACK if is_store else 0.0)), not_before=not_before))

    def barrier(self):
        self.epoch += 1

    def _deps(self, ops):
        lastw, readers = {}, {}
        for o in ops:
            d = set()
            for k in o['reads']:
                if k in lastw:
                    d.add(lastw[k])
            for k in o['writes']:
                if k in lastw:
                    d.add(lastw[k])
                for r in readers.get(k, ()):
                    d.add(r)
            d.discard(o['id'])
            o['deps'] = d
            for k in o['writes']:
                lastw[k] = o['id']
                readers[k] = []
            for k in o['reads']:
                if k not in o['writes']:
                    readers.setdefault(k, []).append(o['id'])

    def _schedule(self, ops):
        byid = {o['id']: o for o in ops}
        if not self.sched:
            return list(ops)
        succ = {o['id']: [] for o in ops}
        nd = {}
        for o in ops:
            nd[o['id']] = len(o['deps'])
            for d in o['deps']:
                succ[d].append(o['id'])
        for o in ops:
            if o['dur'] is None:
                o['dur'] = 60.0 if o['dma'] else _est_ns(o['eng'], o['fns'])
        free = {e: 0.0 for e in self.names}
        cand = {e: [] for e in self.names}
        ready_t = {}
        fin = {}
        dma_free = [0.0]
        for o in ops:
            if nd[o['id']] == 0:
                ready_t[o['id']] = 0.0
                cand[o['eng']].append(o['id'])
        order = []
        nleft = len(ops)
        while nleft:
            best = None
            for e in self.names:
                if not cand[e]:
                    continue
                fe = free[e]
                est = {i: max(fe, ready_t[i] + byid[i].get('slack', 0.0), byid[i].get('not_before', 0.0)) for i in cand[e]}
                m0 = min(est.values())
                bi = min((i for i in cand[e] if est[i] <= m0 + self.WINDOW), key=lambda i: (byid[i]['prio'], i))
                st = est[bi]
                if best is None or (st, byid[bi]['prio'], bi) < (best[0], best[1], best[2]):
                    best = (st, byid[bi]['prio'], bi, e)
            st, _, i, e = best
            o = byid[i]
            cand[e].remove(i)
            if o['dma']:
                free[e] = st + o['issue_ns']
                t0 = max(st, dma_free[0])
                dma_free[0] = t0 + o['nbytes'] / self.DMA_BW
                fin[i] = dma_free[0] + self.DMA_LAT
            else:
                free[e] = st + o['dur']
                fin[i] = free[e]
            o['start'] = st
            order.append(o)
            nleft -= 1
            for j in succ[i]:
                nd[j] -= 1
                ready_t[j] = max(ready_t.get(j, 0.0), fin[i] + (self.SYNC_LAT if byid[j]['eng'] != e else 60.0))
                if nd[j] == 0:
                    cand[byid[j]['eng']].append(j)
        self.model_ns = getattr(self, 'model_ns', 0.0) + max(fin.values())
        return order

    def emit(self):
        nc = self.nc
        engs = {'pe': 'tensor', 'act': 'scalar', 'dve': 'vector', 'pool': 'gpsimd', 'sp': 'sync'}
        nep = self.epoch + 1
        q = {e: [] for e in self.names}
        cnt = {e: 0 for e in self.names}
        dcnt = {}
        known = {e: {} for e in self.names}
        tok = {}
        finals = {}
        sem_names = set('E:' + e for e in self.names)
        for ep in range(nep):
            ops = [o for o in self.ops if o['epoch'] == ep]
            if not ops:
                continue
            self._deps(ops)
            order = self._schedule(ops)
            if ep > 0:
                toks = [('E:' + e, cnt[e]) for e in self.names if cnt[e] > 0] + list(dcnt.items())
                for e in self.names:
                    waits = []
                    for (s, v) in toks:
                        if e == 'pe' and s == 'E:pe':
                            continue
                        if known[e].get(s, 0) < v:
                            known[e][s] = v
                            waits.append((s, v))
                    if waits:
                        q[e].append(([], waits, None))
            for o in order:
                e = o['eng']
                waits = {}
                for d in sorted(o['deps']):
                    s, v, de = tok[d]
                    if de == 'pe' and e == 'pe':
                        continue
                    if known[e].get(s, 0) >= v:
                        continue
                    if waits.get(s, 0) < v:
                        waits[s] = v
                for s, v in waits.items():
                    known[e][s] = v
                if o['dma']:
                    s = o['dma']
                    sem_names.add(s)
                    dcnt[s] = dcnt.get(s, 0) + 16 * len(o['fns'])
                    tok[o['id']] = (s, dcnt[s], 'dma')
                    wl = list(waits.items())
                    for i, f in enumerate(o['fns']):
                        q[e].append(([f], wl if i == 0 else [], (s, 16)))
                    if o['final']:
                        finals[s] = dcnt[s]
                else:
                    cnt[e] += 1
                    tok[o['id']] = ('E:' + e, cnt[e], e)
                    q[e].append((o['fns'], list(waits.items()), ('E:' + e, 1)))
        with contextlib.ExitStack() as st:
            sems = {}
            for i, s in enumerate(sorted(sem_names)):
                sems[s] = st.enter_context(nc.semaphore('s%d' % i))
            fin = dict(finals)
            for e in self.names:
                if cnt[e] > 0:
                    fin['E:' + e] = cnt[e]
            block = st.enter_context(nc.Block())

            def make(ename):
                def body(eng):
                    for fns, waits, inc in q[ename]:
                        for (s, v) in waits:
                            eng.wait_ge(sems[s], v)
                        ins = None
                        for f in fns:
                            ins = f(eng)
                        if inc is not None:
                            ins.then_inc(sems[inc[0]], inc[1])
                    if ename == 'sp':
                        for s, v in fin.items():
                            eng.wait_ge(sems[s], v)
                return body

            for ename in self.names:
                getattr(block, engs[ename])(make(ename))
        return nc


def _log_gamma():
    lg = np.log1p(-np.exp(np.linspace(math.log(1.0 / 32), math.log(1.0 / 512), 4).astype(np.float32))).astype(np.float32)
    return lg.astype(np.float64)


def _consts():
    c = {}
    c['ident'] = np.eye(128, dtype=np.float32)
    lg = _log_gamma()
    inv = (np.float32(10000.0) ** (-np.arange(64, dtype=np.float32) / np.float32(64))).astype(np.float32)
    pos = np.arange(NT * 128, dtype=np.float32)
    ang = (pos[:, None] * inv[None, :]).astype(np.float32).astype(np.float64)
    cos, sin = np.cos(ang).reshape(NT, 128, 1, 64), np.sin(ang).reshape(NT, 128, 1, 64)
    i = np.arange(128, dtype=np.float64)
    gq = np.exp(lg[None, :] * (i[:, None] - 127.0)).reshape(1, 128, 4, 1)
    gk = (np.exp(lg[None, :] * (127.0 - i[:, None])) * (128.0 ** -0.5)).reshape(1, 128, 4, 1)
    c['tabs'] = np.stack([cos * gq, sin * gq, cos * gk, sin * gk], axis=2).astype(np.float32).reshape(NT, 128, 1024)
    inva = (np.float32(500000.0) ** (-np.arange(8, dtype=np.float32) / np.float32(8))).astype(np.float32)
    anga = (pos[:, None] * inva[None, :]).astype(np.float32).astype(np.float64).reshape(NT, 128, 8)
    c['ropeA'] = np.ascontiguousarray(np.concatenate([np.cos(anga), np.sin(anga)], -1).transpose(1, 0, 2)).astype(np.float32).reshape(128, NT * 16)
    ps = np.float32(16384.0)
    a1 = (ps * inv).astype(np.float32).astype(np.float64)
    a2 = (ps * inva).astype(np.float32).astype(np.float64)
    row = np.concatenate([np.cos(a1), np.sin(a1), np.cos(a2), np.sin(a2)]).astype(np.float32)
    c['tabS'] = np.tile(row[None, :], (NB, 1))
    jj, ii = np.meshgrid(np.arange(128), np.arange(128), indexing='ij')
    c['mcur'] = (jj <= ii).astype(np.float32)
    c['mprev'] = (jj >= ii).astype(np.float32)
    G = np.exp(lg * 128.0)
    c['Gt'] = np.tile(np.repeat(G, 128)[None, :], (128, 1)).astype(np.float32)
    c['G1t'] = np.tile(np.repeat(np.exp(lg), 128)[None, :], (128, 1)).astype(np.float32)
    c['gam'] = np.exp(lg).astype(np.float64)
    c['G'] = G
    r = np.arange(128)
    c['selb'] = (r[:, None] % 16 == np.arange(16)[None, :]).astype(np.float32)
    gg = (r // 16) % 2
    c['selg'] = np.stack([(gg == 0), (gg == 1)], 1).astype(np.float32)
    c['hrow'] = gg * 4 + (r // 16) // 2
    return c


_C = None


def build_nc():
    nc = bass.Bass("TRN2", target_bir_lowering=False)

    def din(name, shape):
        return nc.dram_tensor(name, list(shape), F32, kind="ExternalInput").ap()

    def dout(name, shape):
        return nc.dram_tensor(name, list(shape), F32, kind="ExternalOutput").ap()

    xp = din("xp", [NT * 128, D]); xs = din("xs", [NB, D]); cT = din("cT", [128, 8 * 17])
    ck = din("ck", [NB, 128, 128]); cv = din("cv", [NB, 128, 128]); stt = din("st", [NB, 4, 128, 128])
    wam = din("wam", [D, 3 * D]); bam = din("bam", [128, 24]); waf = din("waf", [D, 3 * D]); baf = din("baf", [128, 24])
    w_in = din("w_in", [D, DFF]); sinks = din("sinks", [128, 4]); sinkr = din("sinkr", [128, 1]); gnw = din("gnw", [128, 4]); w_out = din("w_out", [D, D])
    ln1w = din("ln1w", [D]); ln1b = din("ln1b", [D]); ln2w = din("ln2w", [D]); ln2b = din("ln2b", [D])
    w_up = din("w_up", [D, 2 * DFF]); w_down = din("w_down", [DFF, D])
    ident_d = din("ident", [128, 128]); tabs = din("tabs", [NT, 128, 1024]); ropeA_d = din("ropeA", [128, NT * 16])
    tabS_d = din("tabS", [NB, 144]); mcur_d = din("mcur", [128, 128]); mprev_d = din("mprev", [128, 128])
    Gt_d = din("Gt", [128, 512]); G1t_d = din("G1t", [128, 512]); selb_d = din("selb", [128, 16]); selg_d = din("selg", [128, 2])

    yp = dout("yp", [NT * 128, D]); ys = dout("ys", [NB, D])
    nkp = dout("nkp", [128, 128]); nvp = dout("nvp", [128, 128]); nrp = dout("nrp", [4, 128, 128])
    nks = dout("nks", [NB, 128, 128]); nvs = dout("nvs", [NB, 128, 128]); nrs = dout("nrs", [NB, 4, 128, 128])

    y1d = nc.dram_tensor("y1d", [NT * 128, D], F32, kind="Internal").ap()
    wus = nc.dram_tensor("wus", [22, 128, 8, 256], BF16, kind="Internal").ap()
    xbd = nc.dram_tensor("xbd", [NT * 128, D], BF16, kind="Internal").ap()
    y1b = nc.dram_tensor("y1b", [NT * 128, D], BF16, kind="Internal").ap()

    cst = _consts()
    gam = [float(x) for x in cst['gam']]
    Gh = [float(x) for x in cst['G']]

    P = Prog(nc)

    with contextlib.ExitStack() as gst:
        def sbg(name, shape, dt=F32):
            return gst.enter_context(nc.sbuf_tensor("s_" + name, shape, dt))

        pT = gst.enter_context(nc.psum_tensor("pT", [128, 512], F32))
        pZ = [gst.enter_context(nc.psum_tensor("pZ%d" % i, [128, 512], F32)) for i in range(2)]
        pS = [gst.enter_context(nc.psum_tensor("pS%d" % i, [128, 512], F32)) for i in range(2)]
        pO = gst.enter_context(nc.psum_tensor("pO", [128, 512], F32))
        pK = gst.enter_context(nc.psum_tensor("pK", [128, 512], F32))

        s1o = contextlib.ExitStack()

        def sbo(name, shape, dt=F32):
            return s1o.enter_context(nc.sbuf_tensor("s_" + name, shape, dt))
        idf = sbg("idf", [128, 128]); idb = sbg("idb", [128, 128], BF16)
        esink = sbg("esink", [128, 4]); sinkrow = sbg("sinkrow", [128, 1])
        mT1 = sbg("mT1", [128, 24, 17]); mT2 = sbg("mT2", [128, 24, 17])
        sc1 = sbg("sc1", [128, 8, 17]); sc2 = sbg("sc2", [128, 8, 17]); gp1 = sbg("gp1", [128, 8, 17]); gp2 = sbg("gp2", [128, 8, 17])
        epsl = sbg("epsl", [128, 1]); epsg = sbg("epsg", [128, 1])
        y1s = sbg("y1s", [NB, D])
        selb = sbg("selb", [128, 16])
        win = sbg("win", [128, 8, DFF], BF16)
        wdn = win[:].rearrange("p a b -> p (a b)").rearrange("p (c n) -> p c n", c=22)
        mcur = sbo("mcur", [128, 128], BF16); mprev = sbo("mprev", [128, 128], BF16)
        mtmp = sbo("mtmp", [128, 256])
        Gt = sbo("Gt", [128, 512]); ropeA = sbo("ropeA", [128, NT * 16])
        gnwT = sbo("gnwT", [128, 4])
        ln1wb = sbo("ln1wb", [128, D]); ln1bb = sbo("ln1bb", [128, D])
        g1 = sbo("g1", [128, D]); g1s = sbo("g1s", [NB, D])
        onesp = sbo("onesp", [128, 2, 128], BF16)

        P.dma('sp', (idf[:], ident_d), writes=['idf'], semkey='c0')
        P.dma('sp', [(mtmp[:, 0:128], mcur_d), (mtmp[:, 128:256], mprev_d)], writes=['mtmp'], semkey='c1')
        P.dma('sp', [(Gt[:], Gt_d), (ropeA[:], ropeA_d), (esink[:], sinks), (sinkrow[:], sinkr), (selb[:], selb_d)], writes=['Gt', 'ropeA', 'esink', 'sinkrow', 'selb'], semkey='c2')
        P.dma('sp', [(gnwT[:], gnw), (ln1wb[:], ln1w.partition_broadcast(128)), (ln1bb[:], ln1b.partition_broadcast(128))],
              writes=['gnwT', 'ln1wb', 'ln1bb'], semkey='c3')
        P.op('dve', lambda e: e.tensor_copy(out=idb[:], in_=idf[:]), reads=['idf'], writes=['idb'])
        P.op('dve', lambda e: e.tensor_copy(out=mcur[:], in_=mtmp[:, 0:128]), reads=['mtmp'], writes=['mcur'])
        P.op('dve', lambda e: e.tensor_copy(out=mprev[:], in_=mtmp[:, 128:256]), reads=['mtmp'], writes=['mprev'])
        P.op('act', lambda e: e.activation(out=esink[:], in_=esink[:], func=AF.Exp), reads=['esink'], writes=['esink'])
        P.op('dve', lambda e: e.memset(epsl[:], LN_EPS), writes=['epsl'])
        P.op('dve', lambda e: e.memset(epsg[:], GN_EPS), writes=['epsg'])
        P.op('dve', lambda e: e.memset(onesp[:], 0.0), writes=['onesp'])
        P.op('dve', lambda e: e.memset(onesp[:, 0, 0:64], 1.0), writes=['onesp'])
        P.op('dve', lambda e: e.memset(onesp[:, 1, 64:128], 1.0), writes=['onesp'])

        WoA = sbo("WoA", [128, 4, D], BF16); WoR = sbo("WoR", [128, 4, D], BF16)

        def make_gate(gp, gnm, gt, gtn, gs, gsn, dg, onesf, xk=()):
            for kc in range(8):
                P.op('dve', lambda e, kc=kc: e.tensor_scalar(out=dg[:, kc, :], in0=idf[:], scalar1=gp[:, kc, 0:1], scalar2=None, op0=ALU.mult),
                     reads=['idf', gnm], writes=[('dg', kc)] + list(xk))
            for half in range(2):
                P.op('pe', [lambda e, kk=kk, half=half: e.matmul(pT[:, kk * 128:(kk + 1) * 128], lhsT=onesf[:], rhs=dg[:, half * 4 + kk, :], start=True, stop=True)
                            for kk in range(4)], reads=[('dg', half * 4 + kk) for kk in range(4)] + ['onesf'] + list(xk), writes=['pT'])
                P.op('act', lambda e, half=half: e.activation(out=gt[:, half * 512:(half + 1) * 512], in_=pT[:], func=AF.Copy), reads=['pT'], writes=[gtn])
                P.op('pe', [lambda e, kk=kk, half=half: e.transpose(out=pT[0:NB, kk * 128:(kk + 1) * 128], in_=gp[:, half * 4 + kk, 1:17], identity=idf[:])
                            for kk in range(4)], reads=[gnm, 'idf'], writes=['pT'])
                P.op('act', lambda e, half=half: e.activation(out=gs[:, half * 512:(half + 1) * 512], in_=pT[0:NB, :], func=AF.Copy), reads=['pT'], writes=[gsn])

        waring = sbo("waring", [128, 2, 8, 256], BF16)
        cTt = sbo("cTt", [128, 8 * 17]); scT = sbo("scT", [128, 8, 17], BF16); sct1 = sbo("sct1", [128, 8 * 17])
        b1 = sbo("b1", [128, 24]); b2 = sbo("b2", [128, 24])
        onesf = sbo("onesf", [128, 128])

        def ada(wsrc, bb, bnm, mT, mnm, slack0, dslack, pa=None, pkey='pT', slots=None):
            pa = pT if pa is None else pa
            if slots is None:
                slots = [(waring[:, i, :, :], ('waring', i), 'wa%d' % i) for i in range(2)]
            ns = len(slots)
            for j2 in range(12):
                wt, wkey, wsem = slots[j2 % ns]
                P.dma('pool', (wt, wsrc[:, j2 * 256:(j2 + 1) * 256].rearrange("(kc p) n -> p kc n", p=128)), writes=[wkey], semkey=wsem,
                      slack=0.0, not_before=slack0 + j2 * dslack)
                for jj in range(2):
                    j = j2 * 2 + jj
                    P.op('pe', [lambda e, j=j, jj=jj, kc=kc, wt=wt: e.matmul(pa[:, j * 17:(j + 1) * 17], lhsT=wt[:, kc, jj * 128:(jj + 1) * 128], rhs=scT[:, kc, :], start=(kc == 0), stop=(kc == 7))
                                for kc in range(8)], reads=[wkey, 'scT'], writes=[pkey])
            P.op('dve', lambda e: e.tensor_tensor(out=mT[:], in0=pa[:, 0:408].rearrange("p (a b) -> p a b", a=24), in1=bb[:].unsqueeze(2).to_broadcast([128, 24, 17]), op=ALU.add),
                 reads=[pkey, bnm], writes=[mnm])

        def emit_phase0():
            P.prio = -10.0
            P.dma('sp', [(cTt[:], cT), (b1[:], bam), (b2[:], baf)], writes=['cTt', 'b1', 'b2'], semkey='c4')
            P.op('act', lambda e: e.activation(out=sct1[:], in_=cTt[:], func=AF.Exp, scale=-1.0), reads=['cTt'], writes=['sct1'])
            P.op('act', lambda e: e.activation(out=sct1[:], in_=sct1[:], func=AF.Ln, bias=1.0, scale=1.0), reads=['sct1'], writes=['sct1'])
            P.op('act', lambda e: e.activation(out=sct1[:], in_=sct1[:], func=AF.Exp, scale=-1.0), reads=['sct1'], writes=['sct1'])
            P.op('dve', lambda e: e.tensor_tensor(out=scT[:].rearrange("p a b -> p (a b)"), in0=cTt[:], in1=sct1[:], op=ALU.mult), reads=['cTt', 'sct1'], writes=['scT'])
            def as_ring(t32):
                return t32[:].bitcast(BF16).rearrange("p (kc n) -> p kc n", kc=8)
            ring = [(waring[:, i, :, :], ('waring', i), 'wa%d' % i) for i in range(2)] + \
                   [(as_ring(xt[0]), ('xt', 0), 'wa4'), (as_ring(xt[1]), ('xt', 1), 'wa5'), (as_ring(y1[0]), ('y1', 0), 'wa6'), (as_ring(y1[1]), ('y1', 1), 'wa7'),
                    (as_ring(tm), 'tm', 'wa2'), (as_ring(rr), 'rr', 'wa3')]
            ada(wam, b1, 'b1', mT1, 'mT1', 0.0, 0.0, slots=ring)
            P.op('dve', lambda e: e.tensor_scalar_add(out=sc1[:], in0=mT1[:, 8:16, :], scalar1=1.0), reads=['mT1'], writes=['sc1'])
            P.op('dve', lambda e: e.tensor_scalar_add(out=gp1[:], in0=mT1[:, 16:24, :], scalar1=1.0), reads=['mT1'], writes=['gp1'])
            P.dma('pool', (xbd[0:512, :], xp[0:512, :]), writes=[('xbd', 0)], semkey='xbd0', slack=0.0, not_before=20e3)
            for j in range(6):
                P.dma('pool', (win[:, :, j * 512:min(DFF, (j + 1) * 512)], w_in[:, j * 512:min(DFF, (j + 1) * 512)].rearrange("(kc p) n -> p kc n", p=128)),
                      writes=[('win', j)], semkey='win%d' % j, slack=0.0, not_before=30e3 + j * 6e3)
            wo_att = w_out[0:512, :].rearrange("(g hg d) n -> g d hg n", g=2, hg=4, d=64)
            P.dma('pool', [(WoA[0:64, :, :], wo_att[0]), (WoA[64:128, :, :], wo_att[1]),
                           (WoR[:], w_out[512:1024, :].rearrange("(h p) n -> p h n", p=128))], writes=['WoA', 'WoR'], semkey='wo', slack=0.0, not_before=70e3)
            for h in range(4):
                P.op('dve', lambda e, h=h: e.tensor_scalar(out=WoR[:, h, :], in0=WoR[:, h, :], scalar1=gnwT[:, h:h + 1], scalar2=None, op0=ALU.mult), reads=['WoR', 'gnwT'], writes=['WoR'])
            P.op('dve', lambda e: e.memset(onesf[:], 1.0), writes=['onesf'])
            make_gate(gp1, 'gp1', g1, 'g1', g1s, 'g1s', tm[:].rearrange("p (a b) -> p a b", a=8), onesf, xk=['tm'])
            for c4 in range(1, NT // 4):
                P.dma('pool', (xbd[c4 * 512:(c4 + 1) * 512, :], xp[c4 * 512:(c4 + 1) * 512, :]), reads=([('xbd', c4 - 2)] if c4 >= 2 else []), writes=[('xbd', c4)],
                      semkey='xbd%d' % (c4 % 2), slack=0.0, not_before=max(110e3, c4 * 100e3 - 60e3))
            P.prio = 1000.0
            for c in range(22):
                P.dma('pool', [(wus[c, :, :, 0:128], w_up[:, c * 128:(c + 1) * 128].rearrange("(kc p) n -> p kc n", p=128)),
                               (wus[c, :, :, 128:256], w_up[:, DFF + c * 128:DFF + (c + 1) * 128].rearrange("(kc p) n -> p kc n", p=128))],
                      reads=([('wus', c - 8)] if c >= 8 else []), writes=[('wus', c)], semkey='wus%d' % (c % 8), slack=0.0, not_before=350e3 + c * 20e3)
            P.prio = 0.0

        with contextlib.ExitStack() as s1:
            def sb1(name, shape, dt=F32):
                return s1.enter_context(nc.sbuf_tensor("s_" + name, shape, dt))
            with contextlib.ExitStack() as s1p:
                def sbp(name, shape, dt=F32):
                    return s1p.enter_context(nc.sbuf_tensor("s_" + name, shape, dt))
                pTb = s1p.enter_context(nc.psum_tensor("pTb", [128, 1024], BF16))
                xt = [sbp("xt%d" % i, [128, D]) for i in range(2)]
                xTb = [sbp("xTb%d" % i, [128, 8, 128], BF16) for i in range(2)]
                tab = [sbp("tab%d" % i, [128, 1024]) for i in range(2)]
                hT = [sbp("hT%d" % i, [128, 8, 128], BF16) for i in range(2)]
                qr = [sbp("qr%d" % i, [128, 512], BF16) for i in range(2)]
                kr = [sbp("kr%d" % i, [128, 128]) for i in range(2)]
                Vp = [sbp("Vp%d" % i, [128, 2, 128], BF16) for i in range(3)]
                kT = [sbp("kT%d" % i, [128, 128], BF16) for i in range(3)]
                rqh = [sbp("rqh%d" % i, [128, 512], BF16) for i in range(2)]
                rkh = [sbp("rkh%d" % i, [128, 512], BF16) for i in range(3)]
                rv = [sbp("rv%d" % i, [128, 512], BF16) for i in range(3)]
                sg = [sbp("sg%d" % i, [128, 512]) for i in range(3)]
                tA = [sbp("tA%d" % i, [128, 512]) for i in range(2)]
                tB = [sbp("tB%d" % i, [128, 512]) for i in range(2)]
                ta = sbp("ta", [128, 8, 16]); tb = sbp("tb", [128, 8, 16]); rst = sbp("rst", [128, 8, 16])
                qT = [sbp("qT%d" % i, [128, 4, 128], BF16) for i in range(3)]
                rqkT = [sbp("rqkT%d" % i, [128, 8, 128], BF16) for i in range(3)]
                pex = [sbp("pex%d" % i, [128, 512], BF16) for i in range(2)]
                PT = [sbp("PT%d" % i, [128, 512], BF16) for i in range(4)]
                dsum = sbp("dsum", [128, 512])
                attT = [sbp("attT%d" % i, [128, 4, 128], BF16) for i in range(2)]
                PTr = sbp("PTr", [128, 512], BF16)
                osb = sbp("osb", [128, 512]); osq = sbp("osq", [128, 512]); onr = sbp("onr", [128, 512])
                S = sbp("S", [128, 512]); Sb = sbp("Sb", [128, 4, 128], BF16)
                ret = [sbp("ret%d" % i, [128, 512], BF16) for i in range(2)]
                retT = sbp("retT", [128, 4, 128], BF16)
                st4 = sbp("st4", [128, 16])
                tm = sbp("tm", [128, D]); rr = sbp("rr", [128, D]); jk = None
                y1 = [sbp("y1_%d" % i, [128, D]) for i in range(2)]
                st2 = sbp("st2", [128, 8])
                etmp = sbp("etmp", [128, 512])

                P.op('dve', lambda e: e.memset(S[:], 0.0), writes=['S'])
                for i in range(3):
                    P.op('dve', lambda e, i=i: e.memset(Vp[i][:], 0.0), writes=[('Vp', i)])

                def ln_tail(rr_ap, n, stt, wb, bb, out_ap, keys_r, key_out, npart=128, jk_=None):
                    pp = slice(0, npart)
                    jk_ = tm[:].bitcast(BF16) if jk_ is None else jk_
                    P.op('act', lambda e: e.activation(out=jk_[pp, 0:n], in_=rr_ap, func=AF.Square, accum_out=stt[pp, 1:2]), reads=keys_r, writes=['jk', 'tm', 'stt'])
                    P.op('dve', lambda e: e.tensor_scalar(out=stt[pp, 2:3], in0=stt[pp, 0:1], scalar1=1.0 / n, scalar2=None, op0=ALU.mult), reads=['stt'], writes=['stt'])
                    P.op('dve', lambda e: e.tensor_tensor(out=stt[pp, 3:4], in0=stt[pp, 2:3], in1=stt[pp, 2:3], op=ALU.mult), reads=['stt'], writes=['stt'])
                    P.op('dve', lambda e: e.scalar_tensor_tensor(out=stt[pp, 4:5], in0=stt[pp, 1:2], scalar=1.0 / n, in1=stt[pp, 3:4], op0=ALU.mult, op1=ALU.subtract), reads=['stt'], writes=['stt'])
                    P.op('act', lambda e: e.activation(out=stt[pp, 5:6], in_=stt[pp, 4:5], func=AF.Ln, bias=epsl[pp, 0:1], scale=1.0), reads=['stt', 'epsl'], writes=['stt'])
                    P.op('act', lambda e: e.activation(out=stt[pp, 6:7], in_=stt[pp, 5:6], func=AF.Exp, scale=-0.5), reads=['stt'], writes=['stt'])
                    P.op('dve', lambda e: e.scalar_tensor_tensor(out=stt[pp, 7:8], in0=stt[pp, 2:3], scalar=-1.0, in1=stt[pp, 6:7], op0=ALU.mult, op1=ALU.mult), reads=['stt'], writes=['stt'])
                    P.op('act', lambda e: e.activation(out=rr_ap, in_=rr_ap, func=AF.Identity, scale=stt[pp, 6:7], bias=stt[pp, 7:8]), reads=keys_r + ['stt'], writes=keys_r)
                    P.op('dve', lambda e: e.tensor_tensor(out=rr_ap, in0=rr_ap, in1=wb, op=ALU.mult), reads=keys_r, writes=keys_r)
                    P.op('dve', lambda e: e.tensor_tensor(out=out_ap, in0=rr_ap, in1=bb, op=ALU.add), reads=keys_r, writes=[key_out])

                def rope_att(src, H, cosap, sinap, dst, dkey, zkey, npart=128):
                    pp = slice(0, npart)
                    v = src.rearrange("p (h d) -> p h d", h=H)[:, :, 0:16].rearrange("p h (two j) -> p h two j", two=2)
                    dv = dst.rearrange("p (h d) -> p h d", h=H)
                    tav = ta[pp, 0:H, :].rearrange("p h (two j) -> p h two j", two=2)
                    tbv = tb[pp, 0:H, :].rearrange("p h (two j) -> p h two j", two=2)
                    cb = cosap.unsqueeze(1).unsqueeze(1).to_broadcast([npart, H, 2, 8])
                    sb_ = sinap.unsqueeze(1).unsqueeze(1).to_broadcast([npart, H, 2, 8])
                    rsv = rst[pp, 0:H, :].rearrange("p h (two j) -> p h two j", two=2)
                    P.op('act', lambda e: e.activation(out=dst, in_=src, func=AF.Copy), reads=[zkey], writes=[dkey])
                    P.op('act', lambda e: e.activation(out=rsv, in_=v, func=AF.Copy), reads=[zkey], writes=['rst'])
                    P.op('dve', lambda e: e.tensor_tensor(out=tav, in0=rsv, in1=cb, op=ALU.mult), reads=['rst', 'ropeA'], writes=['ta'])
                    P.op('dve', lambda e: e.tensor_tensor(out=tbv, in0=rsv, in1=sb_, op=ALU.mult), reads=['rst', 'ropeA'], writes=['tb'])
                    P.op('dve', lambda e: e.tensor_tensor(out=dv[:, :, 0:8], in0=tav[:, :, 0, :], in1=tbv[:, :, 1, :], op=ALU.subtract), reads=['ta', 'tb'], writes=[dkey])
                    P.op('dve', lambda e: e.tensor_tensor(out=dv[:, :, 8:16], in0=tav[:, :, 1, :], in1=tbv[:, :, 0, :], op=ALU.add), reads=['ta', 'tb'], writes=[dkey])

                def front(t):
                    s2, s3, s4 = t % 2, t % 3, t % 4
                    def xT_load(tt):
                        P.dma('sp', [(xTb[tt % 2][:, kc, :], xbd[tt * 128:(tt + 1) * 128, kc * 128:(kc + 1) * 128]) for kc in range(8)],
                              reads=[('xbd', tt // 4)], writes=[('xTb', tt % 2)], semkey='xTb%d' % (tt % 2), transpose=True)
                    if t == 0:
                        xT_load(0)
                    if t + 1 < NT:
                        xT_load(t + 1)
                    P.dma('sp', (tab[s2][:], tabs[t]), writes=[('tab', s2)], semkey='tab%d' % s2)
                    for kc in range(8):
                        if kc % 2 == 0:
                            P.op('dve', lambda e, kc=kc: e.tensor_scalar(out=hT[s2][:, kc, :], in0=xTb[s2][:, kc, :], scalar1=sc1[:, kc, 0:1], scalar2=mT1[:, kc, 0:1],
                                                                       op0=ALU.mult, op1=ALU.add), reads=[('xTb', s2), 'sc1', 'mT1'], writes=[('hT', s2)])
                        else:
                            P.op('act', lambda e, kc=kc: e.activation(out=hT[s2][:, kc, :], in_=xTb[s2][:, kc, :], func=AF.Identity, scale=sc1[:, kc, 0:1], bias=mT1[:, kc, 0:1]),
                                 reads=[('xTb', s2), 'sc1', 'mT1'], writes=[('hT', s2)])
                    for ci in range(6):
                        n0 = ci * 512
                        nw = 512 if ci < 5 else 256
                        pz = (pZ[0], pZ[1], pT)[ci % 3]
                        zkey = (('pZ', 0), ('pZ', 1), 'pT')[ci % 3]
                        P.op('pe', [lambda e, kc=kc, pz=pz, n0=n0, nw=nw: e.matmul(pz[:, 0:nw], lhsT=hT[s2][:, kc, :], rhs=win[:, kc, n0:n0 + nw], start=(kc == 0), stop=(kc == 7))
                                    for kc in range(8)], reads=[('hT', s2), ('win', ci)], writes=[zkey])
                        if ci == 0:
                            rope_att(pz[:, 0:512], 8, ropeA[:, t * 16:t * 16 + 8], ropeA[:, t * 16 + 8:t * 16 + 16], qr[s2][:], ('qr', s2), zkey)
                        elif ci in (1, 2):
                            dst = rqh[s2] if ci == 1 else rkh[s3]
                            dkey = ('rqh', s2) if ci == 1 else ('rkh', s3)
                            cofs = 0 if ci == 1 else 512
                            A, B = tA[ci - 1], tB[ci - 1]
                            zv = pz[:, 0:512].rearrange("p (h two j) -> p h two j", h=4, two=2)
                            cb = tab[s2][:, cofs:cofs + 256].rearrange("p (h j) -> p h j", h=4).unsqueeze(2).to_broadcast([128, 4, 2, 64])
                            sb_ = tab[s2][:, cofs + 256:cofs + 512].rearrange("p (h j) -> p h j", h=4).unsqueeze(2).to_broadcast([128, 4, 2, 64])
                            Av = A[:].rearrange("p (h two j) -> p h two j", h=4, two=2)
                            Bv = B[:].rearrange("p (h two j) -> p h two j", h=4, two=2)
                            dv = dst[:].rearrange("p (h two j) -> p h two j", h=4, two=2)
                            P.op('dve', lambda e, Av=Av, zv=zv, cb=cb: e.tensor_tensor(out=Av, in0=zv, in1=cb, op=ALU.mult), reads=[zkey, ('tab', s2)], writes=[('tA', ci)])
                            P.op('dve', lambda e, Bv=Bv, zv=zv, sb_=sb_: e.tensor_tensor(out=Bv, in0=zv, in1=sb_, op=ALU.mult), reads=[zkey, ('tab', s2)], writes=[('tB', ci)])
                            P.op('dve', lambda e, dv=dv, Av=Av, Bv=Bv: e.tensor_tensor(out=dv[:, :, 0, :], in0=Av[:, :, 0, :], in1=Bv[:, :, 1, :], op=ALU.subtract),
                                 reads=[('tA', ci), ('tB', ci)], writes=[dkey])
                            P.op('dve', lambda e, dv=dv, Av=Av, Bv=Bv: e.tensor_tensor(out=dv[:, :, 1, :], in0=Av[:, :, 1, :], in1=Bv[:, :, 0, :], op=ALU.add),
                                 reads=[('tA', ci), ('tB', ci)], writes=[dkey])
                        elif ci == 3:
                            P.op('act', lambda e, pz=pz: e.activation(out=rv[s3][:], in_=pz[:, 0:512], func=AF.Copy), reads=[zkey], writes=[('rv', s3)])
                        elif ci == 4:
                            P.op('act', lambda e, pz=pz: e.activation(out=sg[s3][:], in_=pz[:, 0:512], func=AF.Copy), reads=[zkey], writes=[('sg', s3)])
                            P.op('act', lambda e, pz=pz: e.activation(out=etmp[:], in_=pz[:, 0:512], func=AF.Exp, scale=-1.0), reads=[zkey], writes=['etmp'])
                            P.op('act', lambda e: e.activation(out=etmp[:], in_=etmp[:], func=AF.Ln, bias=1.0, scale=1.0), reads=['etmp'], writes=['etmp'])
                            P.op('act', lambda e: e.activation(out=etmp[:], in_=etmp[:], func=AF.Exp, scale=-1.0), reads=['etmp'], writes=['etmp'])
                            P.op('dve', lambda e: e.tensor_tensor(out=sg[s3][:], in0=sg[s3][:], in1=etmp[:], op=ALU.mult), reads=[('sg', s3), 'etmp'], writes=[('sg', s3)])
                        else:
                            rope_att(pz[:, 0:128], 2, ropeA[:, t * 16:t * 16 + 8], ropeA[:, t * 16 + 8:t * 16 + 16], kr[s2][:], ('kr', s2), zkey)
                            vdst = Vp[s3][:].rearrange("p g c -> p (g c)")
                            P.op('act', lambda e, pz=pz, vdst=vdst: e.activation(out=vdst[:, 0:64], in_=pz[:, 128:192], func=AF.Copy), reads=[zkey], writes=[('Vp', s3)])
                            P.op('act', lambda e, pz=pz, vdst=vdst: e.activation(out=vdst[:, 192:256], in_=pz[:, 192:256], func=AF.Copy), reads=[zkey], writes=[('Vp', s3)])
                            if t == NT - 1:
                                vout = tA[0]
                                P.op('act', lambda e, pz=pz, vout=vout: e.activation(out=vout[:, 0:128], in_=pz[:, 128:256], func=AF.Copy), reads=[zkey], writes=[('tA', 1)])
                                P.dma('sp', (nvp, vout[:, 0:128]), reads=[('tA', 1)], semkey='nvp', final=True)
                    if t == NT - 1:
                        P.dma('sp', (nkp, kr[s2][:]), reads=[('kr', s2)], semkey='nkp', final=True)
                    P.op('pe', [lambda e, hg=hg: e.transpose(out=pTb[:, hg * 128:(hg + 1) * 128], in_=qr[s2][:, hg * 128:(hg + 1) * 128], identity=idb[:]) for hg in range(4)],
                         reads=[('qr', s2), 'idb'], writes=['pTb'])
                    P.op('act', lambda e: e.activation(out=qT[s3][:].rearrange("p a b -> p (a b)"), in_=pTb[:, 0:512], func=AF.Copy), reads=['pTb'], writes=[('qT', s3)])
                    P.op('pe', lambda e: e.transpose(out=pK[:, 0:128], in_=kr[s2][:], identity=idf[:]), reads=[('kr', s2), 'idf'], writes=['pK'])
                    P.op('dve', lambda e: e.tensor_copy(out=kT[s3][:], in_=pK[:, 0:128]), reads=['pK'], writes=[('kT', s3)])
                    P.op('pe', [lambda e, h=h: e.transpose(out=pTb[:, h * 128:(h + 1) * 128], in_=rqh[s2][:, h * 128:(h + 1) * 128], identity=idb[:]) for h in range(4)] +
                               [lambda e, h=h: e.transpose(out=pTb[:, (4 + h) * 128:(5 + h) * 128], in_=rkh[s3][:, h * 128:(h + 1) * 128], identity=idb[:]) for h in range(4)],
                         reads=[('rqh', s2), ('rkh', s3), 'idb'], writes=['pTb'])
                    P.op('dve', lambda e: e.tensor_copy(out=rqkT[s3][:].rearrange("p a b -> p (a b)"), in_=pTb[:, 0:1024]), reads=['pTb'], writes=[('rqkT', s3)])

                def back1(t):
                    s2, s3 = t % 2, t % 3
                    sprev = (t - 1) % 3
                    blks = ([('prev', sprev, mprev)] if t > 0 else []) + [('cur', s3, mcur)]
                    idx = 0
                    ptl = []
                    for g in range(2):
                        for (bn, ks, msk) in blks:
                            ps_ = pS[idx % 2]
                            pe_ = pex[idx % 2]
                            pt_ = PT[idx]
                            P.op('pe', lambda e, ps_=ps_, ks=ks, g=g: e.matmul(ps_[:], lhsT=kT[ks][g * 64:(g + 1) * 64, :], rhs=qT[s3][g * 64:(g + 1) * 64, :, :].rearrange("p a b -> p (a b)"),
                                                                              start=True, stop=True), reads=[('kT', ks), ('qT', s3)], writes=[('pS', idx % 2)])
                            P.op('act', lambda e, ps_=ps_, pe_=pe_: e.activation(out=pe_[:], in_=ps_[:], func=AF.Exp, scale=0.125), reads=[('pS', idx % 2)], writes=[('pex', idx % 2)])
                            P.op('dve', lambda e, pe_=pe_, pt_=pt_, msk=msk: e.tensor_tensor(out=pt_[:].rearrange("p (a b) -> p a b", a=4), in0=pe_[:].rearrange("p (a b) -> p a b", a=4),
                                                                                         in1=msk[:].unsqueeze(1).to_broadcast([128, 4, 128]), op=ALU.mult),
                                 reads=[('pex', idx % 2), 'mcur', 'mprev'], writes=[('PT', idx)])
                            ptl.append((g, ks, idx))
                            idx += 1
                    n = len(ptl)
                    P.op('pe', [lambda e, g=g, ks=ks, ix=ix, j=j: e.matmul(pO[:], lhsT=Vp[ks][:, g, :], rhs=PT[ix][:], start=(j == 0), stop=(j == n - 1)) for j, (g, ks, ix) in enumerate(ptl)],
                         reads=[('Vp', ks) for (_, ks, _) in ptl] + [('PT', ix) for (_, _, ix) in ptl], writes=['pO'])
                    P.op('pe', [lambda e, g=g, ix=ix, j=j: e.matmul(pK[:], lhsT=onesp[:, g, :], rhs=PT[ix][:], start=(j == 0), stop=(j == n - 1)) for j, (g, ks, ix) in enumerate(ptl)],
                         reads=['onesp'] + [('PT', ix) for (_, _, ix) in ptl], writes=['pK'])
                    P.op('dve', lambda e: e.tensor_tensor(out=dsum[:].rearrange("p (a b) -> p a b", a=4), in0=pK[:].rearrange("p (a b) -> p a b", a=4),
                                                          in1=esink[:].unsqueeze(2).to_broadcast([128, 4, 128]), op=ALU.add), reads=['pK', 'esink'], writes=['dsum'])
                    P.op('act', lambda e: e.activation(out=dsum[:], in_=dsum[:], func=AF.Ln), reads=['dsum'], writes=['dsum'])
                    P.op('act', lambda e: e.activation(out=dsum[:], in_=dsum[:], func=AF.Exp, scale=-1.0), reads=['dsum'], writes=['dsum'])
                    P.op('dve', lambda e: e.tensor_tensor(out=attT[s2][:].rearrange("p a b -> p (a b)"), in0=pO[:], in1=dsum[:], op=ALU.mult), reads=['pO', 'dsum'], writes=[('attT', s2)])
                    P.op('pe', [lambda e, h=h: e.matmul(pS[0][:, h * 128:(h + 1) * 128], lhsT=rqkT[s3][:, 4 + h, :], rhs=rqkT[s3][:, h, :], start=True, stop=True) for h in range(4)],
                         reads=[('rqkT', s3)], writes=[('pS', 0)])
                    P.op('dve', lambda e: e.tensor_tensor(out=PTr[:].rearrange("p (a b) -> p a b", a=4), in0=pS[0][:].rearrange("p (a b) -> p a b", a=4),
                                                          in1=mcur[:].unsqueeze(1).to_broadcast([128, 4, 128]), op=ALU.mult), reads=[('pS', 0), 'mcur'], writes=['PTr'])
                    fns = []
                    for h in range(4):
                        fns.append(lambda e, h=h: e.matmul(pO[:, h * 128:(h + 1) * 128], lhsT=PTr[:, h * 128:(h + 1) * 128], rhs=rv[s3][:, h * 128:(h + 1) * 128], start=True, stop=(t == 0)))
                        if t > 0:
                            fns.append(lambda e, h=h: e.matmul(pO[:, h * 128:(h + 1) * 128], lhsT=rqkT[s3][:, h, :], rhs=Sb[:, h, :], start=False, stop=True))
                    P.op('pe', fns, reads=['PTr', ('rv', s3), ('rqkT', s3), 'Sb'], writes=['pO'])
                    P.op('act', lambda e: e.activation(out=osb[:], in_=pO[:], func=AF.Copy), reads=['pO'], writes=['osb'])
                    P.op('pe', [lambda e, h=h: e.matmul(pK[:, h * 128:(h + 1) * 128], lhsT=rkh[s3][:, h * 128:(h + 1) * 128], rhs=rv[s3][:, h * 128:(h + 1) * 128], start=True, stop=True) for h in range(4)],
                         reads=[('rkh', s3), ('rv', s3)], writes=['pK'])
                    for h in range(4):
                        P.op('dve', lambda e, h=h: e.scalar_tensor_tensor(out=S[:, h * 128:(h + 1) * 128], in0=S[:, h * 128:(h + 1) * 128], scalar=Gh[h], in1=pK[:, h * 128:(h + 1) * 128],
                                                                         op0=ALU.mult, op1=ALU.add), reads=['S', 'pK'], writes=['S'])
                    if t < NT - 1:
                        P.op('dve', lambda e: e.tensor_tensor(out=Sb[:].rearrange("p a b -> p (a b)"), in0=S[:], in1=Gt[:], op=ALU.mult), reads=['S', 'Gt'], writes=['Sb'])
                    else:
                        P.dma('sp', (nrp.rearrange("h k v -> k h v"), S[:].rearrange("p (h v) -> p h v", h=4)), reads=['S'], semkey='nrp', final=True)
                    gn_tail(osb, sg[s3], ('sg', s3), ret[s2], ('ret', s2), 128)

                def gn_tail(osb_, sg_, sgkey, ret_, retkey, npart, osq_=None, onr_=None, st4_=None):
                    pp = slice(0, npart)
                    osq_ = osq if osq_ is None else osq_
                    onr_ = onr if onr_ is None else onr_
                    st4_ = st4 if st4_ is None else st4_
                    o3 = osb_[pp, :].rearrange("p (a b) -> p a b", a=4)
                    P.op('dve', lambda e: e.tensor_reduce(out=st4_[pp, 0:4], in_=o3, axis=AX.X, op=ALU.add), reads=['osb'], writes=['st4'])
                    P.op('act', lambda e: e.activation(out=osq_[pp, :], in_=osb_[pp, :], func=AF.Square), reads=['osb'], writes=['osq'])
                    P.op('dve', lambda e: e.tensor_reduce(out=st4_[pp, 4:8], in_=osq_[pp, :].rearrange("p (a b) -> p a b", a=4), axis=AX.X, op=ALU.add), reads=['osq'], writes=['st4'])
                    P.op('dve', lambda e: e.tensor_scalar(out=st4_[pp, 0:4], in0=st4_[pp, 0:4], scalar1=1.0 / 128, scalar2=None, op0=ALU.mult), reads=['st4'], writes=['st4'])
                    P.op('dve', lambda e: e.tensor_tensor(out=st4_[pp, 8:12], in0=st4_[pp, 0:4], in1=st4_[pp, 0:4], op=ALU.mult), reads=['st4'], writes=['st4'])
                    P.op('dve', lambda e: e.scalar_tensor_tensor(out=st4_[pp, 4:8], in0=st4_[pp, 4:8], scalar=1.0 / 128, in1=st4_[pp, 8:12], op0=ALU.mult, op1=ALU.subtract), reads=['st4'], writes=['st4'])
                    P.op('act', lambda e: e.activation(out=st4_[pp, 8:12], in_=st4_[pp, 4:8], func=AF.Ln, bias=epsg[pp, 0:1], scale=1.0), reads=['st4', 'epsg'], writes=['st4'])
                    P.op('act', lambda e: e.activation(out=st4_[pp, 12:16], in_=st4_[pp, 8:12], func=AF.Exp, scale=-0.5), reads=['st4'], writes=['st4'])
                    n3 = onr_[pp, :].rearrange("p (a b) -> p a b", a=4)
                    P.op('dve', lambda e: e.scalar_tensor_tensor(out=st4_[pp, 8:12], in0=st4_[pp, 0:4], scalar=-1.0, in1=st4_[pp, 12:16], op0=ALU.mult, op1=ALU.mult), reads=['st4'], writes=['st4'])
                    for h in range(4):
                        P.op('act', lambda e, h=h: e.activation(out=onr_[pp, h * 128:(h + 1) * 128], in_=osb_[pp, h * 128:(h + 1) * 128], func=AF.Identity,
                                                               scale=st4_[pp, 12 + h:13 + h], bias=st4_[pp, 8 + h:9 + h]), reads=['osb', 'st4'], writes=['onr'])
                    P.op('dve', lambda e: e.tensor_tensor(out=ret_[pp, :], in0=onr_[pp, :], in1=sg_[pp, :], op=ALU.mult), reads=['onr', sgkey], writes=[retkey])

                def back2(t):
                    s2, s4 = t % 2, t % 2
                    P.dma('sp', (xt[s4][:], xp[t * 128:(t + 1) * 128, :]), writes=[('xt', s4)], semkey='xt%d' % s4)
                    P.op('pe', [lambda e, h=h: e.transpose(out=pTb[:, h * 128:(h + 1) * 128], in_=ret[s2][:, h * 128:(h + 1) * 128], identity=idb[:]) for h in range(4)],
                         reads=[('ret', s2), 'idb'], writes=['pTb'])
                    P.op('act', lambda e: e.activation(out=retT[:].rearrange("p a b -> p (a b)"), in_=pTb[:, 0:512], func=AF.Copy), reads=['pTb'], writes=['retT'])
                    for nn in range(2):
                        P.op('pe', [lambda e, hg=hg, nn=nn: e.matmul(pZ[nn][:], lhsT=attT[s2][:, hg, :], rhs=WoA[:, hg, nn * 512:(nn + 1) * 512], start=(hg == 0), stop=False) for hg in range(4)] +
                                   [lambda e, h=h, nn=nn: e.matmul(pZ[nn][:], lhsT=retT[:, h, :], rhs=WoR[:, h, nn * 512:(nn + 1) * 512], start=False, stop=(h == 3)) for h in range(4)],
                             reads=[('attT', s2), 'retT', 'WoA', 'WoR'], writes=[('pZ', nn)])
                        P.op('dve', lambda e, nn=nn: e.tensor_tensor(out=tm[:, nn * 512:(nn + 1) * 512], in0=pZ[nn][:], in1=g1[:, nn * 512:(nn + 1) * 512], op=ALU.mult), reads=[('pZ', nn), 'g1'], writes=['tm'])
                    P.op('dve', lambda e: e.scalar_tensor_tensor(out=rr[:], in0=xt[s4][:], scalar=ALPHA, in1=tm[:], op0=ALU.mult, op1=ALU.add, accum_out=st2[:, 0:1]), reads=[('xt', s4), 'tm'], writes=['rr', 'stt'])
                    ln_tail(rr[:], D, st2, ln1wb[:], ln1bb[:], y1[s2][:], ['rr'], ('y1', s2))
                    P.dma('sp', (y1d[t * 128:(t + 1) * 128, :], y1[s2][:]), reads=[('y1', s2)], writes=[('y1d', t)], semkey='y1o%d' % s2)
                    if t % 4 == 3:
                        g0 = (t - 3) * 128
                        P.dma('pool', (y1b[g0:g0 + 512, :], y1d[g0:g0 + 512, :]), reads=[('y1d', tt) for tt in range(t - 3, t + 1)] + ([('y1b', t - 8)] if t >= 11 else []),
                              writes=[('y1b', tt) for tt in range(t - 3, t + 1)], semkey='y1p%d' % ((t // 4) % 2), slack=0.0)

                emit_phase0()
                NT1 = DBG['nt1']
                for step in range(NT1 + 2):
                    if step < NT1:
                        P.prio = float(step)
                        front(step)
                    if 0 <= step - 1 < NT1:
                        P.prio = float(step - 1) + 0.3
                        back1(step - 1)
                    if 0 <= step - 2 < NT1:
                        P.prio = float(step - 2) + 0.6
                        back2(step - 2)
                P.prio = 0.0
            P.barrier()
            if SAMPLE:
              with contextlib.ExitStack() as s1s:
                def sbs(name, shape, dt=F32):
                    return s1s.enter_context(nc.sbuf_tensor("s_" + name, shape, dt))
                pA = s1s.enter_context(nc.psum_tensor("pA", [128, 512], F32))
                ada(waf, b2, 'b2', mT2, 'mT2', 0.0, 0.0, pa=pA, pkey='pA')
                P.op('dve', lambda e: e.tensor_scalar_add(out=sc2[:], in0=mT2[:, 8:16, :], scalar1=1.0), reads=['mT2'], writes=['sc2'])
                P.op('dve', lambda e: e.tensor_scalar_add(out=gp2[:], in0=mT2[:, 16:24, :], scalar1=1.0), reads=['mT2'], writes=['gp2'])
                xs_t = sbs("xs_t", [NB, D]); tabS = sbs("tabS", [NB, 144]); G1t = sbs("G1t", [128, 512]); selg = sbs("selg", [128, 2])
                ckt = sbs("ckt", [128, NB, 128]); cvt = ckt; Ss = sbs("Ss", [128, 32, 128])
                tmpS = sbs("tmpS", [128, 128]); hsT = sbs("hsT", [128, 8, NB], BF16)
                zs = sbs("zs", [NB, DFF]); rqs = sbs("rqs", [NB, 512]); rks = sbs("rks", [NB, 512])
                tAs = sbs("tAs", [NB, 512]); tBs = sbs("tBs", [NB, 512]); tas = sbs("tas", [NB, 8, 16]); tbs = sbs("tbs", [NB, 8, 16])
                rqT_s = sbs("rqT_s", [128, 4, NB]); ZQ = sbs("ZQ", [128, 4, NB * NB])
                qk_s = sbs("qk_s", [NB, 4]); osb_s = sbs("osb_s", [NB, 512]); osq_s = sbs("osq_s", [NB, 512]); onr_s = sbs("onr_s", [NB, 512])
                st4_s = sbs("st4_s", [NB, 16]); sg_s = sbs("sg_s", [NB, 512]); rets = sbs("rets", [NB, 512])
                KM = sbs("KM", [NB, 4, 512])
                QB = sbs("QB", [NB, 8, 128]); QT = sbs("QT", [128, 128]); ZQa = sbs("ZQa", [128, NB * 129]); ZP = ZQa
                KT = sbs("KT", [128, NB, 129]); Ps = sbs("Ps", [128, 129]); PTs = sbs("PTs", [128, 128])
                sm = sbs("sm", [128, 8]); PNT = sbs("PNT", [128, NB]); PN = sbs("PN", [NB, 128]); Apad = sbs("Apad", [128, 128])
                attT_s = sbs("attT_s", [128, 4, NB], BF16); retT_s = sbs("retT_s", [128, 4, NB], BF16)
                KMf = KM[:].rearrange("p a b -> p (a b)"); tm_s = KMf[:, 0:D]; rr_s = KMf[:, D:2 * D]; jk_s = sbs("jk_s", [NB, D], BF16); st2_s = sbs("st2_s", [NB, 8])

                P.dma('sp', [(xs_t[:], xs), (tabS[:], tabS_d), (G1t[:], G1t_d), (selg[:], selg_d)], writes=['xs_t', 'tabS', 'G1t', 'selg'], semkey='sa')
                P.dma('sp', (ckt[:], ck.rearrange("b j c -> j b c")), writes=['ckt'], semkey='sb')
                P.dma('sp', [(Ss[:, b * 4:(b + 1) * 4, :], stt[b].rearrange("h k v -> k h v")) for b in range(8)], writes=[('Ss', b) for b in range(8)], semkey='sc')
                P.dma('sp', [(nks[:, 0:127, :], ck[:, 1:128, :]), (nvs[:, 0:127, :], cv[:, 1:128, :])], semkey='sd', final=True)
                for tns, nm in ((ZQ, 'ZQ'), (QB, 'QB'), (ZQa, 'ZQa'), (Apad, 'Apad')):
                    P.op('pool', lambda e, tns=tns: e.memset(tns[:], 0.0), writes=[nm])

                P.op('pe', [lambda e, kc=kc: e.transpose(out=pT[:, kc * NB:(kc + 1) * NB], in_=xs_t[:, kc * 128:(kc + 1) * 128], identity=idf[0:NB, 0:NB]) for kc in range(8)],
                     reads=['xs_t', 'idf'], writes=['pT'])
                P.op('dve', lambda e: e.tensor_tensor(out=tmpS[:].rearrange("p (a b) -> p a b", a=8), in0=pT[:, 0:128].rearrange("p (a b) -> p a b", a=8), in1=sc1[:, :, 1:17], op=ALU.mult),
                     reads=['pT', 'sc1'], writes=['tmpS'])
                P.op('dve', lambda e: e.tensor_tensor(out=hsT[:], in0=tmpS[:].rearrange("p (a b) -> p a b", a=8), in1=mT1[:, 0:8, 1:17], op=ALU.add), reads=['tmpS', 'mT1'], writes=['hsT'])
                for ci in range(6):
                    n0 = ci * 512
                    nw = 512 if ci < 5 else 256
                    P.op('pe', [lambda e, kc=kc, ci=ci, n0=n0, nw=nw: e.matmul(pZ[ci % 2][0:NB, 0:nw], lhsT=hsT[:, kc, :], rhs=win[:, kc, n0:n0 + nw], start=(kc == 0), stop=(kc == 7))
                                for kc in range(8)], reads=['hsT', ('win', ci)], writes=[('pZ', ci % 2)])
                    P.op('act', lambda e, ci=ci, n0=n0, nw=nw: e.activation(out=zs[:, n0:n0 + nw], in_=pZ[ci % 2][0:NB, 0:nw], func=AF.Copy), reads=[('pZ', ci % 2)], writes=['zs'])
                P.dma('pool', [(wdn[:, j * 6:min(22, (j + 1) * 6), :], w_down[j * 768:min(DFF, (j + 1) * 768), :].rearrange("(c p) n -> p c n", p=128)) for j in range(4)],
                  writes=[('win', j) for j in range(6)] + ['wdn'], semkey='wdn', not_before=100e3)
                P.op('act', lambda e: e.activation(out=osq_s[:], in_=zs[:, 2048:2560], func=AF.Exp, scale=-1.0), reads=['zs'], writes=['osq'])
                P.op('act', lambda e: e.activation(out=osq_s[:], in_=osq_s[:], func=AF.Ln, bias=1.0, scale=1.0), reads=['osq'], writes=['osq'])
                P.op('act', lambda e: e.activation(out=osq_s[:], in_=osq_s[:], func=AF.Exp, scale=-1.0), reads=['osq'], writes=['osq'])
                P.op('dve', lambda e: e.tensor_tensor(out=sg_s[:], in0=zs[:, 2048:2560], in1=osq_s[:], op=ALU.mult), reads=['zs', 'osq'], writes=['sg_s'])

                def rope_s(c0, H):
                    v = zs[:, c0:c0 + H * 64].rearrange("p (h d) -> p h d", h=H)
                    vr = v[:, :, 0:16].rearrange("p h (two j) -> p h two j", two=2)
                    tav = tas[:, 0:H, :].rearrange("p h (two j) -> p h two j", two=2)
                    tbv = tbs[:, 0:H, :].rearrange("p h (two j) -> p h two j", two=2)
                    cb = tabS[:, 128:136].unsqueeze(1).unsqueeze(1).to_broadcast([NB, H, 2, 8])
                    sb_ = tabS[:, 136:144].unsqueeze(1).unsqueeze(1).to_broadcast([NB, H, 2, 8])
                    P.op('dve', lambda e: e.tensor_tensor(out=tav, in0=vr, in1=cb, op=ALU.mult), reads=['zs', 'tabS'], writes=['tas'])
                    P.op('dve', lambda e: e.tensor_tensor(out=tbv, in0=vr, in1=sb_, op=ALU.mult), reads=['zs', 'tabS'], writes=['tbs'])
                    P.op('dve', lambda e: e.tensor_tensor(out=v[:, :, 0:8], in0=tav[:, :, 0, :], in1=tbv[:, :, 1, :], op=ALU.subtract), reads=['tas', 'tbs'], writes=['zs'])
                    P.op('dve', lambda e: e.tensor_tensor(out=v[:, :, 8:16], in0=tav[:, :, 1, :], in1=tbv[:, :, 0, :], op=ALU.add), reads=['tas', 'tbs'], writes=['zs'])
                rope_s(0, 8)
                rope_s(2560, 2)
                for (c0, dst, dnm) in ((512, rqs, 'rqs'), (1024, rks, 'rks')):
                    zv = zs[:, c0:c0 + 512].rearrange("p (h two j) -> p h two j", h=4, two=2)
                    cb = tabS[:, 0:64].unsqueeze(1).unsqueeze(1).to_broadcast([NB, 4, 2, 64])
                    sb_ = tabS[:, 64:128].unsqueeze(1).unsqueeze(1).to_broadcast([NB, 4, 2, 64])
                    Av = tAs[:].rearrange("p (h two j) -> p h two j", h=4, two=2)
                    Bv = tBs[:].rearrange("p (h two j) -> p h two j", h=4, two=2)
                    dv = dst[:].rearrange("p (h two j) -> p h two j", h=4, two=2)
                    P.op('dve', lambda e, zv=zv, cb=cb, Av=Av: e.tensor_tensor(out=Av, in0=zv, in1=cb, op=ALU.mult), reads=['zs', 'tabS'], writes=['tAs'])
                    P.op('dve', lambda e, zv=zv, sb_=sb_, Bv=Bv: e.tensor_tensor(out=Bv, in0=zv, in1=sb_, op=ALU.mult), reads=['zs', 'tabS'], writes=['tBs'])
                    P.op('dve', lambda e, dv=dv, Av=Av, Bv=Bv: e.tensor_tensor(out=dv[:, :, 0, :], in0=Av[:, :, 0, :], in1=Bv[:, :, 1, :], op=ALU.subtract), reads=['tAs', 'tBs'], writes=[dnm])
                    P.op('dve', lambda e, dv=dv, Av=Av, Bv=Bv: e.tensor_tensor(out=dv[:, :, 1, :], in0=Av[:, :, 1, :], in1=Bv[:, :, 0, :], op=ALU.add), reads=['tAs', 'tBs'], writes=[dnm])
                P.op('dve', lambda e: e.tensor_scalar(out=rks[:], in0=rks[:], scalar1=128.0 ** -0.5, scalar2=None, op0=ALU.mult), reads=['rks'], writes=['rks'])
                P.dma('sp', [(nks[:, 127, :], zs[:, 2560:2688]), (nvs[:, 127, :], zs[:, 2688:2816])], reads=['zs'], semkey='se', final=True)

                rvs = zs[:, 1536:2048]
                P.op('pe', [lambda e, h=h: e.transpose(out=pT[:, h * NB:(h + 1) * NB], in_=rqs[:, h * 128:(h + 1) * 128], identity=idf[0:NB, 0:NB]) for h in range(4)],
                     reads=['rqs', 'idf'], writes=['pT'])
                P.op('act', lambda e: e.activation(out=rqT_s[:].rearrange("p a b -> p (a b)"), in_=pT[:, 0:4 * NB], func=AF.Copy), reads=['pT'], writes=['rqT_s'])
                P.op('dve', lambda e: e.tensor_copy(out=ZQ[:, :, 0:NB * NB:NB + 1], in_=rqT_s[:]), reads=['rqT_s', 'ZQ'], writes=['ZQ'])
                for half in range(2):
                    if half == 1:
                        P.dma('sp', [(Ss[:, (b % 8) * 4:(b % 8 + 1) * 4, :], stt[b].rearrange("h k v -> k h v")) for b in range(8, NB)],
                              reads=[('nrsd', b) for b in range(8)], writes=[('Ss', b) for b in range(8)], semkey='sc')
                    fns = []
                    for h in range(4):
                        for b in range(half * 8, half * 8 + 8):
                            fns.append(lambda e, h=h, b=b: e.matmul(pO[0:NB, h * 128:(h + 1) * 128], lhsT=ZQ[:, h, b * NB:(b + 1) * NB], rhs=Ss[:, (b % 8) * 4 + h, :],
                                                                   start=(h == 0 and b == 0), stop=(h == 3 and b == NB - 1), skip_group_check=True))
                    P.op('pe', fns, reads=['ZQ'] + [('Ss', b) for b in range(8)], writes=['pO'])
                    for b in range(half * 8, half * 8 + 8):
                        P.op('dve', lambda e, b=b: e.tensor_scalar(out=KM[:, b % 4, :], in0=rks[:], scalar1=idf[0:NB, b:b + 1], scalar2=None, op0=ALU.mult), reads=['rks', 'idf'], writes=[('KM', b % 4)])
                        pkv = pK if b % 2 == 0 else pS[1]
                        pkey = 'pK' if b % 2 == 0 else ('pS', 1)
                        P.op('pe', [lambda e, h=h, b=b, pkv=pkv: e.matmul(pkv[:, h * 128:(h + 1) * 128], lhsT=KM[:, b % 4, h * 128:(h + 1) * 128], rhs=rvs[:, h * 128:(h + 1) * 128], start=True, stop=True) for h in range(4)],
                             reads=[('KM', b % 4), 'zs'], writes=[pkey])
                        sv = Ss[:, (b % 8) * 4:(b % 8 + 1) * 4, :].rearrange("p a b -> p (a b)")
                        P.op('dve', lambda e, sv=sv: e.tensor_tensor(out=sv, in0=sv, in1=G1t[:], op=ALU.mult), reads=[('Ss', b % 8), 'G1t'], writes=[('Ss', b % 8)])
                        P.op('dve', lambda e, sv=sv, pkv=pkv: e.tensor_tensor(out=sv, in0=sv, in1=pkv[:], op=ALU.add), reads=[('Ss', b % 8), pkey], writes=[('Ss', b % 8)])
                        P.dma('sp', (nrs[b].rearrange("h k v -> k h v"), Ss[:, (b % 8) * 4:(b % 8 + 1) * 4, :]), reads=[('Ss', b % 8)], writes=[('nrsd', b)], semkey='nrs%d' % (b % 8), final=True, slack=0.0)
                P.op('dve', lambda e: e.tensor_tensor(out=tAs[:], in0=rqs[:], in1=rks[:], op=ALU.mult), reads=['rqs', 'rks'], writes=['tAs'])
                P.op('dve', lambda e: e.tensor_reduce(out=qk_s[:], in_=tAs[:].rearrange("p (a b) -> p a b", a=4), axis=AX.X, op=ALU.add), reads=['tAs'], writes=['qk_s'])
                P.op('dve', lambda e: e.tensor_tensor(out=tBs[:].rearrange("p (a b) -> p a b", a=4), in0=rvs.rearrange("p (a b) -> p a b", a=4),
                                                      in1=qk_s[:].unsqueeze(2).to_broadcast([NB, 4, 128]), op=ALU.mult), reads=['zs', 'qk_s'], writes=['tBs'])
                for h in range(4):
                    P.op('dve', lambda e, h=h: e.scalar_tensor_tensor(out=osb_s[:, h * 128:(h + 1) * 128], in0=pO[0:NB, h * 128:(h + 1) * 128], scalar=gam[h], in1=tBs[:, h * 128:(h + 1) * 128],
                                                                     op0=ALU.mult, op1=ALU.add), reads=['pO', 'tBs'], writes=['osb'])
                gn_tail(osb_s, sg_s, 'sg_s', rets, 'rets', NB, osq_=osq_s, onr_=onr_s, st4_=st4_s)
                qv4 = zs[:, 0:512].rearrange("p (hg g d) -> p hg g d", hg=4, g=2)
                QB4 = QB[:].rearrange("p (hg g) c -> p hg g c", g=2)
                P.op('dve', lambda e: e.tensor_copy(out=QB4[:, :, 0, 0:64], in_=qv4[:, :, 0, :]), reads=['zs', 'QB'], writes=['QB'])
                P.op('dve', lambda e: e.tensor_copy(out=QB4[:, :, 1, 64:128], in_=qv4[:, :, 1, :]), reads=['zs', 'QB'], writes=['QB'])
                P.op('pe', [lambda e, hh=hh: e.transpose(out=pT[:, hh * NB:(hh + 1) * NB], in_=QB[:, hh, :], identity=idf[0:NB, 0:NB]) for hh in range(8)], reads=['QB', 'idf'], writes=['pT'])
                P.op('act', lambda e: e.activation(out=QT[:], in_=pT[:, 0:128], func=AF.Copy), reads=['pT'], writes=['QT'])
                ZQa_v = ZQa[:].rearrange("p (b c) -> p b c", c=129)[:, :, 0:128:16]
                P.op('dve', lambda e: e.tensor_copy(out=ZQa_v, in_=QT[:].rearrange("p (hh b) -> p b hh", b=NB)), reads=['QT', 'ZQa'], writes=['ZQa'])
                for q4 in range(4):
                    pb = pS[q4 % 2]
                    P.op('pe', [lambda e, j=j, q4=q4, pb=pb: e.transpose(out=pb[:, j * 128:(j + 1) * 128], in_=ckt[:, q4 * 4 + j, :], identity=idf[:]) for j in range(4)],
                         reads=['ckt', 'idf'], writes=[('pS', q4 % 2)])
                    P.op('act', lambda e, q4=q4, pb=pb: e.activation(out=KT[:, q4 * 4:(q4 + 1) * 4, 0:128], in_=pb[:].rearrange("p (a b) -> p a b", a=4), func=AF.Copy),
                         reads=[('pS', q4 % 2)], writes=['KT'])
                P.dma('sp', (cvt[:], cv.rearrange("b j c -> j b c")), writes=['ckt'], semkey='sb')
                P.op('pe', lambda e: e.transpose(out=pT[:, 0:NB], in_=zs[:, 2560:2688], identity=idf[0:NB, 0:NB]), reads=['zs', 'idf'], writes=['pT'])
                P.op('act', lambda e: e.activation(out=KT[:, :, 128], in_=pT[:, 0:NB], func=AF.Copy), reads=['pT'], writes=['KT'])
                P.op('pe', [lambda e, b=b: e.matmul(pO[:, 0:129], lhsT=ZQa[:, b * 128:(b + 1) * 128], rhs=KT[:, b, :], start=(b == 0), stop=(b == NB - 1)) for b in range(NB)],
                     reads=['ZQa', 'KT'], writes=['pO'])
                P.op('dve', lambda e: e.tensor_reduce(out=sm[:, 0:1], in_=pO[:, 0:129], axis=AX.X, op=ALU.max), reads=['pO'], writes=['sm'])
                P.op('dve', lambda e: e.tensor_scalar(out=sm[:, 0:1], in0=sm[:, 0:1], scalar1=0.125, scalar2=None, op0=ALU.mult), reads=['sm'], writes=['sm'])
                P.op('dve', lambda e: e.tensor_tensor(out=sm[:, 0:1], in0=sm[:, 0:1], in1=sinkrow[:], op=ALU.max), reads=['sm', 'sinkrow'], writes=['sm'])
                P.op('dve', lambda e: e.tensor_scalar(out=sm[:, 1:2], in0=sm[:, 0:1], scalar1=-1.0, scalar2=None, op0=ALU.mult), reads=['sm'], writes=['sm'])
                P.op('act', lambda e: e.activation(out=Ps[:], in_=pO[:, 0:129], func=AF.Exp, scale=0.125, bias=sm[:, 1:2], accum_out=sm[:, 2:3]), reads=['pO', 'sm'], writes=['Ps', 'sm'])
                P.op('act', lambda e: e.activation(out=sm[:, 3:4], in_=sinkrow[:], func=AF.Exp, scale=1.0, bias=sm[:, 1:2]), reads=['sinkrow', 'sm'], writes=['sm'])
                P.op('dve', lambda e: e.tensor_tensor(out=sm[:, 4:5], in0=sm[:, 2:3], in1=sm[:, 3:4], op=ALU.add), reads=['sm'], writes=['sm'])
                P.op('dve', lambda e: e.reciprocal(out=sm[:, 5:6], in_=sm[:, 4:5]), reads=['sm'], writes=['sm'])
                P.op('dve', lambda e: e.tensor_scalar(out=sm[:, 6:8], in0=selg[:], scalar1=sm[:, 5:6], scalar2=None, op0=ALU.mult), reads=['sm', 'selg'], writes=['sm'])
                P.op('pe', lambda e: e.transpose(out=pK[:, 0:128], in_=Ps[:, 0:128], identity=idf[:]), reads=['Ps', 'idf'], writes=['pK'])
                P.op('act', lambda e: e.activation(out=PTs[:], in_=pK[:, 0:128], func=AF.Copy), reads=['pK'], writes=['PTs'])
                ZP_v = ZP[:].rearrange("p (b c) -> p b c", c=129)[:, :, 0:128:16]
                P.op('dve', lambda e: e.tensor_copy(out=ZP_v, in_=PTs[:].rearrange("p (hh b) -> p b hh", b=NB)), reads=['PTs', 'ZQa'], writes=['ZQa'])
                P.op('dve', lambda e: e.tensor_scalar(out=PNT[:], in0=selb[:], scalar1=Ps[:, 128:129], scalar2=None, op0=ALU.mult), reads=['Ps', 'selb'], writes=['PNT'])
                P.op('pe', lambda e: e.transpose(out=pK[0:NB, 0:128], in_=PNT[:], identity=idf[:]), reads=['PNT', 'idf'], writes=['pK'])
                P.op('act', lambda e: e.activation(out=PN[:], in_=pK[0:NB, 0:128], func=AF.Copy), reads=['pK'], writes=['PN'])
                P.op('pe', [lambda e, b=b: e.matmul(pS[0][:, 0:128], lhsT=ZP[:, b * 128:(b + 1) * 128], rhs=cvt[:, b, :], start=(b == 0), stop=False) for b in range(NB)] +
                           [lambda e: e.matmul(pS[0][:, 0:128], lhsT=PN[:], rhs=zs[:, 2688:2816], start=False, stop=True)],
                     reads=['ZQa', 'ckt', 'PN', 'zs'], writes=[('pS', 0)])
                P.op('dve', lambda e: e.tensor_scalar(out=Apad[:, 0:64], in0=pS[0][:, 0:64], scalar1=sm[:, 6:7], scalar2=None, op0=ALU.mult), reads=[('pS', 0), 'sm', 'Apad'], writes=['Apad'])
                P.op('dve', lambda e: e.tensor_scalar(out=Apad[:, 64:128], in0=pS[0][:, 64:128], scalar1=sm[:, 7:8], scalar2=None, op0=ALU.mult), reads=[('pS', 0), 'sm', 'Apad'], writes=['Apad'])
                P.op('pe', lambda e: e.transpose(out=pK[:, 0:128], in_=Apad[:], identity=idf[:]), reads=['Apad', 'idf'], writes=['pK'])
                Tv = pK[:, 0:128].rearrange("p (hg g b) -> p hg g b", hg=4, g=2)
                P.op('act', lambda e: e.activation(out=tmpS[:, 0:64].rearrange("p (a b) -> p a b", a=4), in_=Tv[:, :, 0, :], func=AF.Copy), reads=['pK'], writes=['tmpS'])
                P.op('dve', lambda e: e.tensor_tensor(out=attT_s[:], in0=tmpS[:, 0:64].rearrange("p (a b) -> p a b", a=4), in1=Tv[:, :, 1, :], op=ALU.add), reads=['pK', 'tmpS'], writes=['attT_s'])
                P.op('pe', [lambda e, h=h: e.transpose(out=pT[:, h * NB:(h + 1) * NB], in_=rets[:, h * 128:(h + 1) * 128], identity=idf[0:NB, 0:NB]) for h in range(4)],
                     reads=['rets', 'idf'], writes=['pT'])
                P.op('act', lambda e: e.activation(out=retT_s[:].rearrange("p a b -> p (a b)"), in_=pT[:, 0:4 * NB], func=AF.Copy), reads=['pT'], writes=['retT_s'])
                for nn in range(2):
                    P.op('pe', [lambda e, hg=hg, nn=nn: e.matmul(pZ[nn][0:NB, :], lhsT=attT_s[:, hg, :], rhs=WoA[:, hg, nn * 512:(nn + 1) * 512], start=(hg == 0), stop=False) for hg in range(4)] +
                               [lambda e, h=h, nn=nn: e.matmul(pZ[nn][0:NB, :], lhsT=retT_s[:, h, :], rhs=WoR[:, h, nn * 512:(nn + 1) * 512], start=False, stop=(h == 3)) for h in range(4)],
                         reads=['attT_s', 'retT_s', 'WoA', 'WoR'], writes=[('pZ', nn)])
                    P.op('dve', lambda e, nn=nn: e.tensor_tensor(out=tm_s[:, nn * 512:(nn + 1) * 512], in0=pZ[nn][0:NB, :], in1=g1s[:, nn * 512:(nn + 1) * 512], op=ALU.mult), reads=[('pZ', nn), 'g1s'], writes=[('KM', 0), ('KM', 1)])
                P.op('dve', lambda e: e.scalar_tensor_tensor(out=rr_s, in0=xs_t[:], scalar=ALPHA, in1=tm_s, op0=ALU.mult, op1=ALU.add, accum_out=st2_s[:, 0:1]), reads=['xs_t', ('KM', 0), ('KM', 1)], writes=[('KM', 2), ('KM', 3), 'stt'])
                ln_tail(rr_s, D, st2_s, ln1wb[0:NB, :], ln1bb[0:NB, :], y1s[:], [('KM', 2), ('KM', 3)], 'y1s', npart=NB, jk_=jk_s)
              P.barrier()

        s1o.close()
        with contextlib.ExitStack() as s2c:
            def sb2(name, shape, dt=F32):
                return s2c.enter_context(nc.sbuf_tensor("s_" + name, shape, dt))
            NR = 4
            wring = [sb2("wring%d" % i, [128, 8, 256], BF16) for i in range(NR)]
            xT2 = [sb2("xT2%d" % i, [128, 8, 512], BF16) for i in range(2)]
            yB = [sb2("yB%d" % i, [128, D]) for i in range(2)]
            h2T = [sb2("h2T%d" % i, [128, 8, 512], BF16) for i in range(2)]
            hidT = [sb2("hidT%d" % i, [128, 22, 512], BF16) for i in range(2)]
            sgt = [sb2("sgt%d" % i, [128, 512]) for i in range(2)]
            tm2 = sb2("tm2", [128, D]); rr2 = sb2("rr2", [128, D]); jk2 = sb2("jk2", [128, D], BF16)
            yo = [sb2("yo%d" % i, [128, D]) for i in range(2)]
            st5 = sb2("st5", [128, 8])
            ln2wb = sb2("ln2wb", [128, D]); ln2bb = sb2("ln2bb", [128, D]); g2 = sb2("g2", [128, D]); g2s = sb2("g2s", [NB, D])
            dg2 = sb2("dg2", [128, 8, 128]); onesf2 = sb2("onesf2", [128, 128])
            P.dma('sp', [(ln2wb[:], ln2w.partition_broadcast(128)), (ln2bb[:], ln2b.partition_broadcast(128))], writes=['ln2wb', 'ln2bb'], semkey='c3')
            P.op('pool', lambda e: e.memset(onesf2[:], 1.0), writes=['onesf'])
            make_gate(gp2, 'gp2', g2, 'g2', g2s, 'g2s', dg2, onesf2)
            pG = [pS[0], pS[1]]
            pU = [pO, pK]

            def ln2_tail(rr_ap, stt, out_ap, keys_r, key_out, npart=128):
                n = D
                pp = slice(0, npart)
                P.op('act', lambda e: e.activation(out=jk2[pp, 0:n], in_=rr_ap, func=AF.Square, accum_out=stt[pp, 1:2]), reads=keys_r, writes=['jk2', 'st5'])
                P.op('dve', lambda e: e.tensor_scalar(out=stt[pp, 2:3], in0=stt[pp, 0:1], scalar1=1.0 / n, scalar2=None, op0=ALU.mult), reads=['st5'], writes=['st5'])
                P.op('dve', lambda e: e.tensor_tensor(out=stt[pp, 3:4], in0=stt[pp, 2:3], in1=stt[pp, 2:3], op=ALU.mult), reads=['st5'], writes=['st5'])
                P.op('dve', lambda e: e.scalar_tensor_tensor(out=stt[pp, 4:5], in0=stt[pp, 1:2], scalar=1.0 / n, in1=stt[pp, 3:4], op0=ALU.mult, op1=ALU.subtract), reads=['st5'], writes=['st5'])
                P.op('act', lambda e: e.activation(out=stt[pp, 5:6], in_=stt[pp, 4:5], func=AF.Ln, bias=epsl[pp, 0:1], scale=1.0), reads=['st5', 'epsl'], writes=['st5'])
                P.op('act', lambda e: e.activation(out=stt[pp, 6:7], in_=stt[pp, 5:6], func=AF.Exp, scale=-0.5), reads=['st5'], writes=['st5'])
                P.op('dve', lambda e: e.scalar_tensor_tensor(out=stt[pp, 7:8], in0=stt[pp, 2:3], scalar=-1.0, in1=stt[pp, 6:7], op0=ALU.mult, op1=ALU.mult), reads=['st5'], writes=['st5'])
                P.op('act', lambda e: e.activation(out=rr_ap, in_=rr_ap, func=AF.Identity, scale=stt[pp, 6:7], bias=stt[pp, 7:8]), reads=keys_r + ['st5'], writes=keys_r)
                P.op('dve', lambda e: e.tensor_tensor(out=rr_ap, in0=rr_ap, in1=ln2wb[pp, :], op=ALU.mult), reads=keys_r + ['ln2wb'], writes=keys_r)
                P.op('dve', lambda e: e.tensor_tensor(out=out_ap, in0=rr_ap, in1=ln2bb[pp, :], op=ALU.add), reads=keys_r + ['ln2bb'], writes=[key_out])

            wcount = [0]

            yos = sb2("yos", [NB, D]); h2Ts = sb2("h2Ts", [128, 8, NB], BF16); hidTs = sb2("hidTs", [128, 22, NB], BF16); sgts = sb2("sgts", [128, 2, NB])

            def ffn_group(gi, ntok, tiles, sample=False, ride=False):
                hs = gi % 2
                if ride:
                    P.op('pe', [lambda e, kc=kc: e.transpose(out=pT[:, kc * NB:(kc + 1) * NB], in_=y1s[:, kc * 128:(kc + 1) * 128], identity=idf[0:NB, 0:NB]) for kc in range(8)],
                         reads=['y1s', 'idf'], writes=['pT'])
                    P.op('dve', lambda e: e.tensor_tensor(out=tm2[:, 0:128].rearrange("p (a b) -> p a b", a=8), in0=pT[:, 0:128].rearrange("p (a b) -> p a b", a=8), in1=sc2[:, :, 1:17], op=ALU.mult),
                         reads=['pT', 'sc2'], writes=['tm2'])
                    P.op('dve', lambda e: e.tensor_tensor(out=h2Ts[:], in0=tm2[:, 0:128].rearrange("p (a b) -> p a b", a=8), in1=mT2[:, 0:8, 1:17], op=ALU.add), reads=['tm2', 'mT2'], writes=['h2Ts'])
                if not sample:
                    g0 = tiles[0] * 128
                    P.dma('sp', [(xT2[hs][:, kc, :], y1b[g0:g0 + 512, kc * 128:(kc + 1) * 128]) for kc in range(8)],
                          reads=[('y1b', t) for t in tiles], writes=[('xT2', hs)], semkey='xT2%d' % hs, transpose=True)
                    for kc in range(8):
                        if kc % 2 == 0:
                            P.op('dve', lambda e, kc=kc: e.tensor_scalar(out=h2T[hs][:, kc, :], in0=xT2[hs][:, kc, :], scalar1=sc2[:, kc, 0:1], scalar2=mT2[:, kc, 0:1],
                                                                       op0=ALU.mult, op1=ALU.add), reads=[('xT2', hs), 'sc2', 'mT2'], writes=[('h2T', hs)])
                        else:
                            P.op('act', lambda e, kc=kc: e.activation(out=h2T[hs][:, kc, :], in_=xT2[hs][:, kc, :], func=AF.Identity, scale=sc2[:, kc, 0:1], bias=mT2[:, kc, 0:1]),
                                 reads=[('xT2', hs), 'sc2', 'mT2'], writes=[('h2T', hs)])
                else:
                    P.op('pe', [lambda e, kc=kc: e.transpose(out=pT[:, kc * NB:(kc + 1) * NB], in_=y1s[:, kc * 128:(kc + 1) * 128], identity=idf[0:NB, 0:NB]) for kc in range(8)],
                         reads=['y1s', 'idf'], writes=['pT'])
                    hv = h2T[hs][:, :, 0:NB]
                    P.op('dve', lambda e: e.tensor_tensor(out=tm2[:, 0:128].rearrange("p (a b) -> p a b", a=8), in0=pT[:, 0:128].rearrange("p (a b) -> p a b", a=8), in1=sc2[:, :, 1:17], op=ALU.mult),
                         reads=['pT', 'sc2'], writes=['tm2'])
                    P.op('dve', lambda e: e.tensor_tensor(out=hv, in0=tm2[:, 0:128].rearrange("p (a b) -> p a b", a=8), in1=mT2[:, 0:8, 1:17], op=ALU.add), reads=['tm2', 'mT2'], writes=[('h2T', hs)])
                for c in range(22):
                    w = wcount[0] % NR
                    wcount[0] += 1
                    P.dma('sp', (wring[w][:], wus[c]), reads=[('wus', c)], writes=[('wring', w)], semkey='wr%d' % w)
                    bi = c % 2
                    P.op('pe', [lambda e, kc=kc, w=w, bi=bi: e.matmul(pG[bi][:, 0:ntok], lhsT=wring[w][:, kc, 0:128], rhs=h2T[hs][:, kc, 0:ntok], start=(kc == 0), stop=(kc == 7)) for kc in range(8)],
                         reads=[('wring', w), ('h2T', hs)], writes=[('pG', bi)])
                    P.op('pe', [lambda e, kc=kc, w=w, bi=bi: e.matmul(pU[bi][:, 0:ntok], lhsT=wring[w][:, kc, 128:256], rhs=h2T[hs][:, kc, 0:ntok], start=(kc == 0), stop=(kc == 7)) for kc in range(8)],
                         reads=[('wring', w), ('h2T', hs)], writes=[('pU', bi)])
                    P.op('act', lambda e, bi=bi: e.activation(out=sgt[bi][:, 0:ntok], in_=pG[bi][:, 0:ntok], func=AF.Exp, scale=-1.0), reads=[('pG', bi)], writes=[('sgt', bi)])
                    P.op('act', lambda e, bi=bi: e.activation(out=sgt[bi][:, 0:ntok], in_=sgt[bi][:, 0:ntok], func=AF.Ln, bias=1.0, scale=1.0), reads=[('sgt', bi)], writes=[('sgt', bi)])
                    P.op('act', lambda e, bi=bi: e.activation(out=sgt[bi][:, 0:ntok], in_=sgt[bi][:, 0:ntok], func=AF.Exp, scale=-1.0), reads=[('sgt', bi)], writes=[('sgt', bi)])
                    P.op('dve', lambda e, bi=bi: e.tensor_tensor(out=sgt[bi][:, 0:ntok], in0=pG[bi][:, 0:ntok], in1=sgt[bi][:, 0:ntok], op=ALU.mult), reads=[('pG', bi), ('sgt', bi)], writes=[('sgt', bi)])
                    P.op('dve', lambda e, bi=bi, c=c: e.tensor_tensor(out=hidT[hs][:, c, 0:ntok], in0=pU[bi][:, 0:ntok], in1=sgt[bi][:, 0:ntok], op=ALU.mult),
                         reads=[('pU', bi), ('sgt', bi)], writes=[('hidT', hs, c)])
                    if ride:
                        o = (c % 2) * 2 * NB
                        P.op('pe', [lambda e, kc=kc, w=w, o=o: e.matmul(pT[:, o:o + NB], lhsT=wring[w][:, kc, 0:128], rhs=h2Ts[:, kc, :], start=(kc == 0), stop=(kc == 7)) for kc in range(8)] +
                                   [lambda e, kc=kc, w=w, o=o: e.matmul(pT[:, o + NB:o + 2 * NB], lhsT=wring[w][:, kc, 128:256], rhs=h2Ts[:, kc, :], start=(kc == 0), stop=(kc == 7)) for kc in range(8)],
                             reads=[('wring', w), 'h2Ts'], writes=['pT'])
                        sv_ = sgts[:, bi, :]
                        P.op('act', lambda e, o=o, sv_=sv_: e.activation(out=sv_, in_=pT[:, o:o + NB], func=AF.Exp, scale=-1.0), reads=['pT'], writes=[('sgts', bi)])
                        P.op('act', lambda e, sv_=sv_: e.activation(out=sv_, in_=sv_, func=AF.Ln, bias=1.0, scale=1.0), reads=[('sgts', bi)], writes=[('sgts', bi)])
                        P.op('act', lambda e, sv_=sv_: e.activation(out=sv_, in_=sv_, func=AF.Exp, scale=-1.0), reads=[('sgts', bi)], writes=[('sgts', bi)])
                        P.op('dve', lambda e, o=o, sv_=sv_: e.tensor_tensor(out=sv_, in0=pT[:, o:o + NB], in1=sv_, op=ALU.mult), reads=['pT', ('sgts', bi)], writes=[('sgts', bi)])
                        P.op('dve', lambda e, o=o, sv_=sv_, c=c: e.tensor_tensor(out=hidTs[:, c, :], in0=pT[:, o + NB:o + 2 * NB], in1=sv_, op=ALU.mult), reads=['pT', ('sgts', bi)], writes=[('hidTs', c)])
                ntl = 1 if sample else 4
                for j in range(ntl):
                    mp = NB if sample else 128
                    for nn in range(2):
                        P.op('pe', [lambda e, c=c, nn=nn, j=j, mp=mp: e.matmul(pZ[nn][0:mp, :], lhsT=hidT[hs][:, c, j * 128:j * 128 + mp], rhs=wdn[:, c, nn * 512:(nn + 1) * 512], start=(c == 0), stop=(c == 21))
                                    for c in range(22)], reads=[('hidT', hs, c) for c in range(22)] + ['wdn'], writes=[('pZ', nn)])
                    if not sample:
                        t = tiles[j]
                        sl = (gi * 4 + j) % 2
                        P.dma('sp', (yB[sl][:], y1d[t * 128:(t + 1) * 128, :]), reads=[('y1d', t)], writes=[('yB', sl)], semkey='yB%d' % sl)
                        for nn in range(2):
                            P.op('dve', lambda e, nn=nn: e.tensor_tensor(out=tm2[:, nn * 512:(nn + 1) * 512], in0=pZ[nn][:], in1=g2[:, nn * 512:(nn + 1) * 512], op=ALU.mult), reads=[('pZ', nn), 'g2'], writes=['tm2'])
                        P.op('dve', lambda e, sl=sl: e.scalar_tensor_tensor(out=rr2[:], in0=yB[sl][:], scalar=ALPHA, in1=tm2[:], op0=ALU.mult, op1=ALU.add, accum_out=st5[:, 0:1]), reads=[('yB', sl), 'tm2'], writes=['rr2', 'st5'])
                        ln2_tail(rr2[:], st5, yo[sl][:], ['rr2'], ('yo', sl))
                        P.dma('sp', (yp[t * 128:(t + 1) * 128, :], yo[sl][:]), reads=[('yo', sl)], semkey='yo%d' % sl, final=True)
                    else:
                        for nn in range(2):
                            P.op('dve', lambda e, nn=nn: e.tensor_tensor(out=tm2[0:NB, nn * 512:(nn + 1) * 512], in0=pZ[nn][0:NB, :], in1=g2s[:, nn * 512:(nn + 1) * 512], op=ALU.mult),
                                 reads=[('pZ', nn), 'g2s'], writes=['tm2'])
                        P.op('dve', lambda e: e.scalar_tensor_tensor(out=rr2[0:NB, :], in0=y1s[:], scalar=ALPHA, in1=tm2[0:NB, :], op0=ALU.mult, op1=ALU.add, accum_out=st5[0:NB, 0:1]), reads=['y1s', 'tm2'], writes=['rr2', 'st5'])
                        ln2_tail(rr2[0:NB, :], st5, yo[0][0:NB, :], ['rr2'], ('yo', 0), npart=NB)
                        P.dma('sp', (ys, yo[0][0:NB, :]), reads=[('yo', 0)], semkey='yo0', final=True)
                if ride:
                    for nn in range(2):
                        P.op('pe', [lambda e, c=c, nn=nn: e.matmul(pZ[nn][0:NB, :], lhsT=hidTs[:, c, :], rhs=wdn[:, c, nn * 512:(nn + 1) * 512], start=(c == 0), stop=(c == 21)) for c in range(22)],
                             reads=[('hidTs', c) for c in range(22)] + ['wdn'], writes=[('pZ', nn)])
                        P.op('dve', lambda e, nn=nn: e.tensor_tensor(out=tm2[0:NB, nn * 512:(nn + 1) * 512], in0=pZ[nn][0:NB, :], in1=g2s[:, nn * 512:(nn + 1) * 512], op=ALU.mult),
                             reads=[('pZ', nn), 'g2s'], writes=['tm2'])
                    P.op('dve', lambda e: e.scalar_tensor_tensor(out=rr2[0:NB, :], in0=y1s[:], scalar=ALPHA, in1=tm2[0:NB, :], op0=ALU.mult, op1=ALU.add, accum_out=st5[0:NB, 0:1]), reads=['y1s', 'tm2'], writes=['rr2', 'st5'])
                    ln2_tail(rr2[0:NB, :], st5, yos[:], ['rr2'], 'yos', npart=NB)
                    P.dma('sp', (ys, yos[:]), reads=['yos'], semkey='yos', final=True)

            for gi in range(DBG['ng2']):
                ffn_group(gi, 512, [gi * 4 + j for j in range(4)], ride=(SAMPLE and gi == 0))

        P.emit()
    return nc


SAMPLE = True
DBG = {'nt1': NT, 'ng2': NT // 4, 'ph0only': False}
_NC = None


def kernel(x_prompt, x_sample, c_prompt, c_sample, cache_k, cache_v, state_ret,
           w_ada_mix, b_ada_mix, w_in, att_sinks, ret_gn_w, w_out, ln1_w, ln1_b,
           w_ada_ffn, b_ada_ffn, w_up, w_down, ln2_w, ln2_b):
    global _NC
    f = lambda a: np.ascontiguousarray(np.asarray(a, dtype=np.float32))
    cst = _consts()
    qperm = np.arange(512).reshape(2, 4, 64).transpose(1, 0, 2).reshape(-1)
    perm = np.concatenate([qperm, np.arange(768, 2816), np.arange(512, 768)])
    shared = {
        'wam': f(w_ada_mix[0]), 'bam': f(np.asarray(b_ada_mix[0]).reshape(24, 128).T),
        'waf': f(w_ada_ffn[0]), 'baf': f(np.asarray(b_ada_ffn[0]).reshape(24, 128).T),
        'w_in': f(np.asarray(w_in[0])[:, perm]),
        'sinks': f(np.repeat(np.asarray(att_sinks[0]).reshape(2, 1, 4), 64, axis=1).reshape(128, 4)),
        'sinkr': f(np.asarray(att_sinks[0])[cst['hrow']].reshape(128, 1)),
        'gnw': f(np.asarray(ret_gn_w[0]).reshape(4, 128).T), 'w_out': f(w_out[0]),
        'ln1w': f(ln1_w[0]), 'ln1b': f(ln1_b[0]), 'ln2w': f(ln2_w[0]), 'ln2b': f(ln2_b[0]),
        'w_up': f(w_up[0]), 'w_down': f(w_down[0]),
        'ident': cst['ident'], 'tabs': cst['tabs'], 'ropeA': cst['ropeA'], 'tabS': cst['tabS'],
        'mcur': cst['mcur'], 'mprev': cst['mprev'], 'Gt': cst['Gt'], 'G1t': cst['G1t'], 'selb': cst['selb'], 'selg': cst['selg'],
    }
    xp_ = np.asarray(x_prompt, dtype=np.float32)
    xs_ = np.asarray(x_sample, dtype=np.float32)
    cp_ = np.asarray(c_prompt, dtype=np.float32)
    cs_ = np.asarray(c_sample, dtype=np.float32)
    in_maps = []
    for c in range(8):
        cv17 = np.concatenate([cp_[c:c + 1], cs_[c * NB:(c + 1) * NB]], axis=0)
        cTm = np.ascontiguousarray(cv17.T.reshape(8, 128, 17).transpose(1, 0, 2)).reshape(128, 8 * 17)
        m = dict(shared)
        m.update({
            'xp': f(xp_[c]), 'xs': f(xs_[c * NB:(c + 1) * NB, 0, :]), 'cT': f(cTm),
            'ck': f(np.asarray(cache_k[0, c * NB:(c + 1) * NB]).reshape(NB, 128, 128)),
            'cv': f(np.asarray(cache_v[0, c * NB:(c + 1) * NB]).reshape(NB, 128, 128)),
            'st': f(state_ret[0, c * NB:(c + 1) * NB]),
        })
        in_maps.append(m)
    if _NC is None:
        _NC = build_nc()
    res = run_bass_kernel_spmd(_NC, in_maps, core_ids=list(range(8)))
    R = res.results
    y_p = np.stack([R[c]['yp'].reshape(NT * 128, D) for c in range(8)], 0)
    y_s = np.concatenate([R[c]['ys'].reshape(NB, 1, D) for c in range(8)], 0)
    nkp = np.stack([R[c]['nkp'].reshape(128, 2, 64) for c in range(8)], 0)[None]
    nvp = np.stack([R[c]['nvp'].reshape(128, 2, 64) for c in range(8)], 0)[None]
    nrp = np.stack([R[c]['nrp'].reshape(4, 128, 128) for c in range(8)], 0)[None]
    nks = np.concatenate([R[c]['nks'].reshape(NB, 128, 2, 64) for c in range(8)], 0)[None]
    nvs = np.concatenate([R[c]['nvs'].reshape(NB, 128, 2, 64) for c in range(8)], 0)[None]
    nrs = np.concatenate([R[c]['nrs'].reshape(NB, 4, 128, 128) for c in range(8)], 0)[None]
    return (y_p.astype(np.float32), y_s.astype(np.float32), nkp.astype(np.float32), nvp.astype(np.float32),
            nrp.astype(np.float32), nks.astype(np.float32), nvs.astype(np.float32), nrs.astype(np.float32))
```
